# Optimizing a Trainium2 kernel written in Bass

```python
import math
import jax, jax.numpy as jnp
from jax import lax
import numpy as np

D_MODEL = 1024
BATCH = 8
SEQ = 4096
DEPTH = 4
DEC_BATCH = 32
DEC_SEQ = 16
PAST_LEN = 1024

CHUNK = 64
Q_BLOCK = 128
D_POOL = D_MODEL // 2
POOL_WINDOWS = (2, 4, 8, 16)
N_POOL_GROUPS = 4
POOL_GROUP = D_POOL // N_POOL_GROUPS
POOL_HIST = max(POOL_WINDOWS) - 1
N_HEADS = 8
HEAD_DIM = 64
QK_DIM = 2 * HEAD_DIM
V_DIM = 2 * HEAD_DIM
Q_WIDTH = N_HEADS * QK_DIM
D_ATT = N_HEADS * V_DIM
ROPE_THETA = 10000.0
LN_EPS = 1e-5
RMS_EPS = 1e-5
ALPHA = (2 * DEPTH) ** 0.25
BETA = (8 * DEPTH) ** -0.25
IN_WIDTHS = (D_POOL, D_POOL, Q_WIDTH, Q_WIDTH, D_ATT, D_ATT, D_MODEL, D_MODEL)
IN_SPLITS = tuple(int(s) for s in np.cumsum(IN_WIDTHS)[:-1])
D_IN = sum(IN_WIDTHS)

kernel_name = "pool_diffattn_deepnorm_stream_step"


def layer_norm(x, g, b):
    xf = x.astype(jnp.float32)
    mu = jnp.mean(xf, -1, keepdims=True)
    var = jnp.mean(jnp.square(xf - mu), -1, keepdims=True)
    y = (xf - mu) * lax.rsqrt(var + LN_EPS) * g.astype(jnp.float32) + b.astype(jnp.float32)
    return y.astype(x.dtype)


def rope(x, pos):
    half = HEAD_DIM // 2
    inv = ROPE_THETA ** (-jnp.arange(half, dtype=jnp.float32) / half)
    ang = pos.astype(jnp.float32)[:, None] * inv[None, :]
    shp = (ang.shape[0],) + (1,) * (x.ndim - 3) + (half,)
    cos = jnp.cos(ang).reshape(shp)
    sin = jnp.sin(ang).reshape(shp)
    xf = x.astype(jnp.float32)
    x1, x2 = xf[..., :half], xf[..., half:]
    return jnp.concatenate([x1 * cos - x2 * sin, x2 * cos + x1 * sin], -1).astype(x.dtype)


def pool_mix(u, hist, pos, w_pool, pool_scale):
    T = u.shape[1]
    ext = jnp.concatenate([hist.astype(u.dtype), u], 1)
    ef = ext.astype(jnp.float32)
    cs = jnp.concatenate([jnp.zeros_like(ef[:, :1]), lax.cumsum(ef, axis=1)], 1)
    uf = u.astype(jnp.float32)
    outs = []
    for g, w in enumerate(POOL_WINDOWS):
        sl = slice(g * POOL_GROUP, (g + 1) * POOL_GROUP)
        s = cs[:, POOL_HIST + 1:POOL_HIST + 1 + T, sl] - cs[:, POOL_HIST + 1 - w:POOL_HIST + 1 - w + T, sl]
        cnt = jnp.minimum(pos + 1, w).astype(jnp.float32)[None, :, None]
        pooled = (s / cnt - uf[..., sl]).astype(u.dtype)
        outs.append(jnp.einsum('btc,cd->btd', pooled, w_pool[g]))
    y = jnp.concatenate(outs, -1) * pool_scale
    return y, ext[:, -POOL_HIST:]


def diff_attend(q, k, v, lam, mask):
    s = jnp.einsum('bqhmd,bkhmd->bhmqk', q, k).astype(jnp.float32) * (HEAD_DIM ** -0.5)
    if mask is not None:
        s = jnp.where(mask, s, -jnp.inf)
    p = jax.nn.softmax(s, axis=-1)
    a = p[:, :, 0] - lam * p[:, :, 1]
    return jnp.einsum('bhqk,bkhv->bqhv', a.astype(v.dtype), v)


def prompt_diff_attention(q, k, v, lam):
    B, T = q.shape[0], q.shape[1]
    nb = T // Q_BLOCK
    qb = q.reshape(B, nb, Q_BLOCK, N_HEADS, 2, HEAD_DIM).transpose(1, 0, 2, 3, 4, 5)
    k_chunk = jnp.arange(T) // CHUNK

    def block(args):
        qi, i = args
        q_chunk = (i * Q_BLOCK + jnp.arange(Q_BLOCK)) // CHUNK
        mask = k_chunk[None, :] <= q_chunk[:, None]
        return diff_attend(qi, k, v, lam, mask)

    o = lax.map(block, (qb, jnp.arange(nb)))
    return o.transpose(1, 0, 2, 3, 4).reshape(B, T, N_HEADS, V_DIM)


def trunk_layer(x, pos, pool_hist, k_past, v_past, w_in, w_pool, pool_scale, lambda_qk,
                subln_w, w_a, w_b, w_o, ln_g, ln_b, layer_idx):
    B, T, _ = x.shape
    lam_init = 0.8 - 0.6 * math.exp(-0.3 * layer_idx)
    lq = lambda_qk.astype(jnp.float32)
    lam = jnp.exp(jnp.sum(lq[0] * lq[1])) - jnp.exp(jnp.sum(lq[2] * lq[3])) + lam_init

    h = jnp.einsum('btd,de->bte', x, w_in)
    px, pg, q, k, v, ag, ga, gb = jnp.split(h, IN_SPLITS, axis=-1)

    ya, new_hist = pool_mix(px, pool_hist, pos, w_pool, pool_scale)
    ya = ya * jax.nn.silu(pg)

    q = rope(q.reshape(B, T, N_HEADS, 2, HEAD_DIM), pos)
    k = rope(k.reshape(B, T, N_HEADS, 2, HEAD_DIM), pos)
    v = v.reshape(B, T, N_HEADS, V_DIM)
    if k_past is None:
        o = prompt_diff_attention(q, k, v, lam)
    else:
        P = k_past.shape[1]
        kc = jnp.concatenate([k_past.reshape(B, P, N_HEADS, 2, HEAD_DIM).astype(k.dtype), k], 1)
        vc = jnp.concatenate([v_past.astype(v.dtype), v], 1)
        o = diff_attend(q, kc, vc, lam, None)
    of = o.astype(jnp.float32)
    of = of * lax.rsqrt(jnp.mean(jnp.square(of), -1, keepdims=True) + RMS_EPS)
    of = of * subln_w.astype(jnp.float32) * (1.0 - lam_init)
    yb = of.astype(x.dtype).reshape(B, T, D_ATT) * jax.nn.silu(ag)

    merged = (jax.nn.sigmoid(ga) * jnp.einsum('btc,cd->btd', ya, w_a)
              + jax.nn.sigmoid(gb) * jnp.einsum('btc,cd->btd', yb, w_b))
    out = jnp.einsum('btd,de->bte', merged, w_o)
    x_new = layer_norm(ALPHA * x + out, ln_g, ln_b)
    return x_new, k.reshape(B, T, N_HEADS, QK_DIM), v, new_hist


def setup_inputs(seed: int = 0) -> dict:
    key = jax.random.key(seed)
    ks = jax.random.split(key, 20)
    f32 = jnp.float32
    nrm = lambda k, shp: jax.random.normal(k, shp, f32)
    return {
        'x_prompt': nrm(ks[0], (BATCH, SEQ, D_MODEL)),
        'x_sample': nrm(ks[1], (DEC_BATCH, DEC_SEQ, D_MODEL)),
        'cache_k': nrm(ks[2], (DEPTH, DEC_BATCH, PAST_LEN, N_HEADS, QK_DIM)),
        'cache_v': nrm(ks[3], (DEPTH, DEC_BATCH, PAST_LEN, N_HEADS, V_DIM)),
        'state_pool': nrm(ks[4], (DEPTH, DEC_BATCH, POOL_HIST, D_POOL)),
        'ln_in_g': 1.0 + 0.02 * nrm(ks[5], (D_MODEL,)),
        'ln_in_b': 0.02 * nrm(ks[6], (D_MODEL,)),
        'w_in': nrm(ks[7], (DEPTH, D_MODEL, D_IN)) * D_MODEL ** -0.5,
        'w_pool': nrm(ks[8], (DEPTH, N_POOL_GROUPS, POOL_GROUP, POOL_GROUP)) * POOL_GROUP ** -0.5,
        'pool_scale': 1.0 + 0.02 * nrm(ks[9], (DEPTH, D_POOL)),
        'lambda_qk': 0.1 * nrm(ks[10], (DEPTH, 4, HEAD_DIM)),
        'subln_w': 1.0 + 0.02 * nrm(ks[11], (DEPTH, V_DIM)),
        'w_a': nrm(ks[12], (DEPTH, D_POOL, D_MODEL)) * (D_POOL ** -0.5) * BETA,
        'w_b': nrm(ks[13], (DEPTH, D_ATT, D_MODEL)) * (D_ATT ** -0.5) * BETA,
        'w_o': nrm(ks[14], (DEPTH, D_MODEL, D_MODEL)) * (D_MODEL ** -0.5) * BETA,
        'ln_g': 1.0 + 0.02 * nrm(ks[15], (DEPTH, D_MODEL)),
        'ln_b': 0.02 * nrm(ks[16], (DEPTH, D_MODEL)),
    }


def reference(x_prompt, x_sample, cache_k, cache_v, state_pool, ln_in_g, ln_in_b, w_in, w_pool,
              pool_scale, lambda_qk, subln_w, w_a, w_b, w_o, ln_g, ln_b):
    xp = layer_norm(x_prompt, ln_in_g, ln_in_b)
    xs = layer_norm(x_sample, ln_in_g, ln_in_b)
    pos_p = jnp.arange(x_prompt.shape[1])
    pos_s = cache_k.shape[2] + jnp.arange(x_sample.shape[1])
    hist_p = jnp.zeros((x_prompt.shape[0], POOL_HIST, D_POOL), x_prompt.dtype)
    kp_l, vp_l, hp_l, ks_l, vs_l, hs_l = [], [], [], [], [], []
    for l in range(DEPTH):
        xp, kp, vp, hp = trunk_layer(xp, pos_p, hist_p, None, None, w_in[l], w_pool[l],
                                     pool_scale[l], lambda_qk[l], subln_w[l], w_a[l], w_b[l],
                                     w_o[l], ln_g[l], ln_b[l], l)
        xs, ks, vs, hs = trunk_layer(xs, pos_s, state_pool[l], cache_k[l], cache_v[l], w_in[l],
                                     w_pool[l], pool_scale[l], lambda_qk[l], subln_w[l], w_a[l],
                                     w_b[l], w_o[l], ln_g[l], ln_b[l], l)
        kp_l.append(kp); vp_l.append(vp); hp_l.append(hp)
        ks_l.append(ks); vs_l.append(vs); hs_l.append(hs)
    return (xp, xs, jnp.stack(kp_l), jnp.stack(vp_l), jnp.stack(hp_l),
            jnp.stack(ks_l), jnp.stack(vs_l), jnp.stack(hs_l))
```

```python
import math
import numpy as np
import ml_dtypes
import concourse.bass as bass
import concourse.mybir as mybir
from concourse.bass_utils import run_bass_kernel_spmd

F32 = mybir.dt.float32
BF16 = mybir.dt.bfloat16
AF = mybir.ActivationFunctionType
ALU = mybir.AluOpType

D = 1024
TT = 512
HEADS = 8
DEPTH_FULL = 4
ALPHA = (2 * DEPTH_FULL) ** 0.25
LN_EPS = 1e-5
RMS_EPS = 1e-5
PAST = 1024
NSEQ_S = 4
TS = 16
NCH = 19


class Buf:
    __slots__ = ("name", "w", "r", "dsem", "dcnt")

    def __init__(self, name):
        self.name = name
        self.w = None
        self.r = {}
        self.dsem = None
        self.dcnt = 0


class Eng:
    def __init__(self, fw, name, is_pe=False):
        self.fw = fw
        self.name = name
        self.is_pe = is_pe
        self.prog = []
        self.seen = {}
        self.sem = None
        self.cnt = 0
        self.epoch = 0
        self.new_epoch()

    def new_epoch(self):
        self.sem = self.fw.nc.alloc_semaphore(f"s_{self.name}_{self.epoch}")
        self.fw.nsem += 1
        self.epoch += 1
        self.cnt = 0


class FW:
    def __init__(self, nc):
        self.nc = nc
        self.nsem = 0
        self.pe = Eng(self, "pe", is_pe=True)
        self.act = Eng(self, "act")
        self.dve = Eng(self, "dve")
        self.pool = Eng(self, "pool")
        self.sp = Eng(self, "sp")
        self.out_events = []
        self.ninst = 0

    def new_epoch(self):
        for e in (self.pe, self.act, self.dve, self.pool):
            if e.cnt > 0:
                e.new_epoch()

    def _collect(self, E, reads, writes):
        waits = []

        def need(ev, same_ok):
            if ev is None:
                return
            sem, val = ev
            if sem is E.sem and same_ok:
                return
            k = id(sem)
            if E.seen.get(k, 0) >= val:
                return
            E.seen[k] = val
            waits.append((sem, val))

        for b in reads:
            need(b.w, E.is_pe)
        for b in writes:
            need(b.w, E.is_pe)
            for ev in b.r.values():
                need(ev, True)
        return waits

    def op(self, E, fn, reads=(), writes=(), sig=True):
        waits = self._collect(E, reads, writes)
        self.ninst += 1
        if sig:
            E.cnt += 1
            ev = (E.sem, E.cnt)
            E.prog.append((waits, fn, (E.sem, 1)))
        else:
            ev = (E.sem, E.cnt + 1)
            E.prog.append((waits, fn, None))
        for b in reads:
            b.r[id(E.sem)] = ev
        for b in writes:
            b.w = ev
            b.r = {}
        return ev

    def dma(self, Q, fns, sembuf, reads=(), writes=(), is_out=False):
        waits = self._collect(Q, reads, writes)
        if sembuf.dsem is None:
            sembuf.dsem = self.nc.alloc_semaphore(f"d_{sembuf.name}")
            self.nsem += 1
        for i, fn in enumerate(fns):
            sembuf.dcnt += 16
            self.ninst += 1
            Q.prog.append((waits if i == 0 else [], fn, (sembuf.dsem, 16)))
        ev = (sembuf.dsem, sembuf.dcnt)
        for b in reads:
            b.r[id(sembuf.dsem)] = ev
        for b in writes:
            b.w = ev
            b.r = {}
        if is_out:
            self.out_events.append(ev)
        return ev

    def emit(self):
        nc = self.nc
        last = {}
        for sem, val in self.out_events:
            k = id(sem)
            if k not in last or last[k][1] < val:
                last[k] = (sem, val)
        self.sp.prog.append((list(last.values()), None, None))

        def replay(E, eng):
            for waits, fn, inc in E.prog:
                for sem, val in waits:
                    eng.wait_ge(sem, val)
                if fn is None:
                    continue
                ins = fn(eng)
                if inc is not None:
                    ins.then_inc(inc[0], inc[1])

        with nc.Block() as block:
            @block.tensor
            def _(eng):
                replay(self.pe, eng)

            @block.scalar
            def _(eng):
                replay(self.act, eng)

            @block.vector
            def _(eng):
                replay(self.dve, eng)

            @block.gpsimd
            def _(eng):
                replay(self.pool, eng)

            @block.sync
            def _(eng):
                replay(self.sp, eng)


def build_program(L, SEQ, with_sample=True):
    import os as _os
    _STOP = int(_os.environ.get('KSTOP', '99'))
    NT = SEQ // TT
    KTW = max(SEQ, NSEQ_S * PAST)
    NKT = KTW // 128
    nc = bass.Bass("TRN2", target_bir_lowering=False)
    fw = FW(nc)
    PE, ACT, DVE, POOL, SP = fw.pe, fw.act, fw.dve, fw.pool, fw.sp

    def din(name, shape, dt=F32):
        return nc.dram_tensor(name, shape, dt, kind="ExternalInput").ap()

    def dout(name, shape, dt=F32):
        return nc.dram_tensor(name, shape, dt, kind="ExternalOutput").ap()

    xp = din("xp", [SEQ, D])
    xs_in = din("xs", [64, D])
    ck = din("ck", [L, NSEQ_S, PAST, D])
    cv = din("cv", [L, NSEQ_S, PAST, D])
    spool = din("spool", [L, 60, 512])
    ln_in_g = din("ln_in_g", [1, D])
    ln_in_b = din("ln_in_b", [1, D])
    w_in = din("w_in", [L, D, 7168])
    w_pool = din("w_pool", [L, 4, 128, 128])
    pool_scale = din("pool_scale", [L, 512])
    lambda_qk = din("lambda_qk", [L, 256])
    subln_w = din("subln_w", [L, 128])
    w_a = din("w_a", [L, 512, D])
    w_b = din("w_b", [L, D, D])
    w_o = din("w_o", [L, D, D])
    ln_g = din("ln_g", [L, D])
    ln_b = din("ln_b", [L, D])
    cs_p = din("cs_p", [SEQ, 128])
    cs_s = din("cs_s", [64, 128])
    idn_bf = din("idn_bf", [128, 128], BF16)
    idn_f = din("idn_f", [128, 128])
    invc = din("invc", [1, 64])

    y_p = dout("y_p", [SEQ, D])
    y_s = dout("y_s", [64, D])
    k_p = dout("k_p", [L, SEQ, D])
    v_p = dout("v_p", [L, SEQ, D])
    pool_p = dout("pool_p", [L, 15, 512])
    k_s = dout("k_s", [L, 64, D])
    v_s = dout("v_s", [L, 64, D])
    pool_s = dout("pool_s", [L, 60, 512])

    xcur_p = nc.dram_tensor("xcur_p", [SEQ, D], F32, kind="Internal").ap()
    xcur_s = nc.dram_tensor("xcur_s", [64, D], F32, kind="Internal").ap()
    wsc = nc.dram_tensor("wsc", [L, NCH, 128, 4096], BF16, kind="Internal").ap()
    B_wsc = [Buf(f"wsc{l}") for l in range(L)]
    B_xcur_p = [Buf(f"xcp{i}") for i in range(SEQ // 128)]
    B_xcur_s = Buf("xcs")

    def sb(name, shape, dt=F32):
        return nc.alloc_sbuf_tensor(name, shape, dt).ap()

    KT = sb("KT", [128, HEADS, KTW], BF16)
    VX = sb("VX", [128, NKT, HEADS, 128], BF16)
    xT = sb("xT", [128, 8, TT], BF16)
    QT = sb("QT", [128, 8, TT], BF16)
    ybT = sb("ybT", [128, 8, TT], BF16)
    WS = [sb(f"WS{i}", [128, 8, 512], BF16) for i in range(2)]
    arena = sb("arena", [128, 12 * 512], F32)
    arena_bf = arena.bitcast(BF16)
    tmpA = sb("tmpA", [128, 528], F32)
    tmpB = sb("tmpB", [128, 528], F32)
    hist = sb("hist", [128, 4, 16], F32)
    cs_sb = sb("cs_sb", [128, 4, 128], F32)
    ident = sb("ident", [128, 128], BF16)
    identf = sb("identf", [128, 128], F32)
    ones_bf = sb("ones_bf", [128, 128], BF16)
    ones_f = sb("ones_f", [128, 128], F32)
    zeros_bf = sb("zeros_bf", [128, 256], BF16)
    neghalf = sb("neghalf", [128, 2], F32)
    invc_sb = sb("invc_sb", [128, 4, 16], F32)
    wpool_sb = sb("wpool_sb", [128, 4, 128], BF16)
    pscale_all = sb("pscale_all", [128, L, 4], F32)
    subln_all = sb("subln_all", [128, L], F32)
    lq_all = arena[:, 0:L * 256].rearrange("p (l c) -> p l c", l=L)
    lam_tmp = sb("lam_tmp", [128, 4 * L + 8], F32)
    neglam_all = sb("neglam_all", [128, L], F32)
    stats = sb("stats", [128, 2, 6], F32)
    mv = sb("mv", [128, 8], F32)
    KTn = sb("KTn", [128, HEADS, 64], BF16)
    VN = sb("VN", [128, HEADS, 128], BF16)
    PTn = sb("PTn", [128, 2, HEADS, TS], BF16)
    vbf_s = tmpA.bitcast(BF16)[0:64, 0:D]

    B_KT, B_VX, B_xT, B_QT, B_ybT = Buf("KT"), Buf("VX"), Buf("xT"), Buf("QT"), Buf("ybT")
    B_WS = [Buf(f"WS{i}") for i in range(2)]
    AB = [Buf(f"ar{i}") for i in range(12)]
    B_tmpA, B_tmpB, B_hist, B_cs = Buf("tmpA"), Buf("tmpB"), Buf("hist"), Buf("cs")
    B_const = Buf("const")
    B_lay = Buf("laycst")
    B_stats, B_mv = Buf("stats"), Buf("mv")
    B_KTn, B_VN, B_PTn = Buf("KTn"), Buf("VN"), Buf("PTn")
    B_wpool = Buf("wpool")

    pA = nc.alloc_psum_tensor("pA", [128, 512], F32).ap()
    pB = nc.alloc_psum_tensor("pB", [128, 512], F32).ap()
    pT = nc.alloc_psum_tensor("pT", [128, 1024], BF16).ap()
    pO = nc.alloc_psum_tensor("pO", [128, 512], F32).ap()
    pSX = nc.alloc_psum_tensor("pSX", [128, 2, 512], F32).ap()
    pSY = nc.alloc_psum_tensor("pSY", [128, 2, 512], F32).ap()
    B_pA, B_pB, B_pT, B_pO, B_pSX, B_pSY = (Buf(n) for n in ["pA", "pB", "pT", "pO", "pSX", "pSY"])

    def arf(slot, ncols, off=0):
        return arena[:, slot * 512 + off: slot * 512 + off + ncols]

    def arb(slot, ncols, off=0):
        return arena_bf[:, slot * 1024 + off: slot * 1024 + off + ncols]

    def cp(E, out, in_, reads, writes):
        if E is ACT:
            return fw.op(E, lambda e: e.activation(out=out, in_=in_, func=AF.Copy), reads=reads, writes=writes)
        return fw.op(E, lambda e: e.tensor_copy(out=out, in_=in_), reads=reads, writes=writes)

    def dma1(Q, out, in_, sembuf, reads=(), writes=(), is_out=False, slow=False):
        if slow:
            f = lambda e: e.dma_start(out=out, in_=in_, allow_slow_non_contiguous=True)
        else:
            f = lambda e: e.dma_start(out=out, in_=in_)
        return fw.dma(Q, [f], sembuf, reads=reads, writes=writes, is_out=is_out)

    dma1(SP, ident, idn_bf, B_const, writes=[B_const])
    dma1(SP, identf, idn_f, B_const, writes=[B_const])
    dma1(SP, invc_sb.rearrange("p g t -> p (g t)"), invc.partition_broadcast(128), B_const, writes=[B_const])
    dma1(SP, arena[:, 0:L * 256], lambda_qk.rearrange("(o l) c -> o (l c)", o=1).partition_broadcast(128),
         AB[0], writes=[AB[0], AB[1]])
    dma1(SP, pscale_all, pool_scale.rearrange("l (g p) -> p l g", p=128), B_lay, writes=[B_lay], slow=True)
    dma1(SP, subln_all, subln_w.rearrange("l p -> p l"), B_lay, writes=[B_lay], slow=True)
    fw.op(DVE, lambda e: e.memset(ones_bf, 1.0), writes=[B_const])
    fw.op(DVE, lambda e: e.memset(ones_f, 1.0), writes=[B_const])
    fw.op(DVE, lambda e: e.memset(zeros_bf, 0.0), writes=[B_const])
    fw.op(DVE, lambda e: e.memset(neghalf, -0.5), writes=[B_const])
    fw.op(DVE, lambda e: e.memset(VN, 0.0), writes=[B_VN])
    fw.op(DVE, lambda e: e.memset(PTn, 0.0), writes=[B_PTn])
    for l in range(L):
        lam_init = 0.8 - 0.6 * math.exp(-0.3 * l)
        for i in range(2):
            fw.op(DVE, (lambda l, i: lambda e: e.tensor_tensor(out=tmpA[:, 0:64], in0=lq_all[:, l, 128 * i:128 * i + 64],
                                                                 in1=lq_all[:, l, 128 * i + 64:128 * i + 128], op=ALU.mult))(l, i),
                  reads=[AB[0], AB[1]], writes=[B_tmpA])
            fw.op(DVE, (lambda l, i: lambda e: e.reduce_sum(out=lam_tmp[:, 4 * l + i:4 * l + i + 1], in_=tmpA[:, 0:64],
                                                              axis=mybir.AxisListType.X))(l, i),
                  reads=[B_tmpA], writes=[B_lay])
        fw.op(ACT, (lambda l: lambda e: e.activation(out=lam_tmp[:, 4 * l + 2:4 * l + 4], in_=lam_tmp[:, 4 * l:4 * l + 2], func=AF.Exp))(l),
              reads=[B_lay], writes=[B_lay])
        fw.op(DVE, (lambda l, li: lambda e: e.scalar_tensor_tensor(out=neglam_all[:, l:l + 1], in0=lam_tmp[:, 4 * l + 3:4 * l + 4],
                                                                     scalar=-li, in1=lam_tmp[:, 4 * l + 2:4 * l + 3],
                                                                     op0=ALU.add, op1=ALU.subtract))(l, lam_init),
              reads=[B_lay], writes=[B_lay])
        fw.op(DVE, (lambda l, li: lambda e: e.tensor_scalar(out=subln_all[:, l:l + 1], in0=subln_all[:, l:l + 1], scalar1=1.0 - li,
                                                              scalar2=None, op0=ALU.mult))(l, lam_init),
              reads=[B_lay], writes=[B_lay])

    def wsc_view(l, ci, kc, ncol):
        return wsc[l, ci].rearrange("p (k c) -> p k c", k=kc)

    def cast_weights(l):
        fns = []
        for ci in range(14):
            src = w_in[l].rearrange("(k p) c -> p k c", p=128)[:, :, ci * 512:(ci + 1) * 512]
            dst = wsc_view(l, ci, 8, 512)
            fns.append((lambda s, d: lambda e: e.dma_start(out=d, in_=s))(src, dst))
        src = w_a[l].rearrange("(g p) c -> p g c", p=128)
        fns.append((lambda s, d: lambda e: e.dma_start(out=d, in_=s))(src, wsc_view(l, 14, 4, 1024)))
        for hb in range(2):
            src = w_b[l].rearrange("(k p) c -> p k c", p=128)[:, :, hb * 512:(hb + 1) * 512]
            fns.append((lambda s, d: lambda e: e.dma_start(out=d, in_=s))(src, wsc_view(l, 15 + hb, 8, 512)))
        for hb in range(2):
            src = w_o[l].rearrange("(k p) c -> p k c", p=128)[:, :, hb * 512:(hb + 1) * 512]
            fns.append((lambda s, d: lambda e: e.dma_start(out=d, in_=s))(src, wsc_view(l, 17 + hb, 8, 512)))
        fw.dma(POOL, fns, B_wsc[l], writes=[B_wsc[l]])

    cast_weights(0)

    CH_ORDER = [0, 1, 2, 3, 4, 5, 6, 7, 8, 9, 14, 10, 11, 15, 12, 16, 13, 17, 18]
    wfree = [0, 1]

    def wload(l, k, slot):
        ci = CH_ORDER[k]
        dst = WS[slot].rearrange("p k c -> p (k c)")
        dma1(SP, dst, wsc[l, ci], B_WS[slot], reads=[B_wsc[l]], writes=[B_WS[slot]])

    def ln_rows(r_ap, r_bufs, n, g_ap, b_ap, pbufs):
        for hlf in range(2):
            fw.op(DVE, (lambda hlf: lambda e: e.bn_stats(out=stats[0:n, hlf, :], in_=r_ap[0:n, hlf * 512:(hlf + 1) * 512]))(hlf),
                  reads=r_bufs, writes=[B_stats])
        fw.op(DVE, lambda e: e.bn_aggr(out=mv[0:n, 0:2], in_=stats[0:n].rearrange("p a b -> p (a b)")), reads=[B_stats], writes=[B_mv])
        fw.op(DVE, lambda e: e.tensor_scalar(out=mv[0:n, 2:3], in0=mv[0:n, 1:2], scalar1=LN_EPS, scalar2=None, op0=ALU.add),
              reads=[B_mv], writes=[B_mv])
        fw.op(POOL, lambda e: e.tensor_tensor(out=mv[0:n, 3:4], in0=mv[0:n, 2:3], in1=neghalf[0:n, 0:1], op=ALU.pow),
              reads=[B_mv, B_const], writes=[B_mv])
        fw.op(DVE, lambda e: e.scalar_tensor_tensor(out=mv[0:n, 4:5], in0=mv[0:n, 0:1], scalar=-1.0, in1=mv[0:n, 3:4],
                                                    op0=ALU.mult, op1=ALU.mult), reads=[B_mv], writes=[B_mv])
        fw.op(ACT, lambda e: e.activation(out=r_ap[0:n, :], in_=r_ap[0:n, :], func=AF.Identity, bias=mv[0:n, 4:5], scale=mv[0:n, 3:4]),
              reads=r_bufs + [B_mv], writes=r_bufs)
        fw.op(DVE, lambda e: e.tensor_tensor(out=r_ap[0:n, :], in0=r_ap[0:n, :], in1=g_ap[0:n, :], op=ALU.mult),
              reads=r_bufs + pbufs, writes=r_bufs)
        fw.op(POOL, lambda e: e.tensor_tensor(out=r_ap[0:n, :], in0=r_ap[0:n, :], in1=b_ap[0:n, :], op=ALU.add),
              reads=r_bufs + pbufs, writes=r_bufs)

    def load_lnp(g_src, b_src):
        dma1(SP, arf(8, 1024), g_src.partition_broadcast(128), AB[8], writes=[AB[8], AB[9]])
        dma1(SP, arf(10, 1024), b_src.partition_broadcast(128), AB[10], writes=[AB[10], AB[11]])

    load_lnp(ln_in_g, ln_in_b)
    rows = [(xp[i * 128:(i + 1) * 128, :], xcur_p[i * 128:(i + 1) * 128, :], 128, B_xcur_p[i]) for i in range(SEQ // 128)]
    if with_sample:
        rows.append((xs_in, xcur_s, 64, B_xcur_s))
    for i, (src, dst, n, bdst) in enumerate(rows):
        sl = 2 * (i % 4)
        r_ap = arf(sl, 1024)
        rb = [AB[sl], AB[sl + 1]]
        dma1(SP, r_ap[0:n, :], src, AB[sl], writes=rb)
        ln_rows(r_ap, rb, n, arf(8, 1024), arf(10, 1024), [AB[8], AB[9], AB[10], AB[11]])
        dma1(SP, dst, r_ap[0:n, :], AB[sl], reads=rb, writes=[bdst])

    pp = {"i": 0}

    def next_ps():
        pp["i"] += 1
        return (pA, B_pA) if pp["i"] % 2 else (pB, B_pB)

    def mm_acc(out_ap, out_buf, pairs, reads):
        n = len(pairs)
        for i, (lt, rh) in enumerate(pairs):
            fw.op(PE, (lambda lt, rh, i: lambda e: e.matmul(out_ap, lhsT=lt, rhs=rh, start=(i == 0), stop=(i == n - 1)))(lt, rh, i),
                  reads=reads, writes=[out_buf], sig=(i == n - 1))

    def sig_gate(ps_ap, ps_buf, nt, gt, gt_buf, silu):
        fw.op(ACT, lambda e: e.activation(out=gt[:, 0:nt], in_=ps_ap, func=AF.Tanh, scale=0.5), reads=[ps_buf], writes=[gt_buf])
        fw.op(DVE, lambda e: e.tensor_scalar(out=gt[:, 0:nt], in0=gt[:, 0:nt], scalar1=0.5, scalar2=0.5, op0=ALU.mult, op1=ALU.add),
              reads=[gt_buf], writes=[gt_buf])
        if silu:
            fw.op(DVE, lambda e: e.tensor_tensor(out=gt[:, 0:nt], in0=ps_ap, in1=gt[:, 0:nt], op=ALU.mult),
                  reads=[ps_buf, gt_buf], writes=[gt_buf])

    def attn_epilogue1(W_):
        rinv, t1 = arf(2, 2 * W_), arf(3, 2 * W_)
        fw.op(DVE, lambda e: e.reciprocal(out=rinv, in_=pB[:, 0:2 * W_]), reads=[B_pB], writes=[AB[2]])
        fw.op(DVE, lambda e: e.tensor_tensor(out=t1, in0=pO[:, 0:2 * W_], in1=rinv, op=ALU.mult), reads=[B_pO, AB[2]], writes=[AB[3]])

    def attn_epilogue2(l, W_):
        t1 = arf(3, 2 * W_)
        o, sq = arf(4, W_), arf(4, W_, 256)
        rs, rstd = arf(5, W_), arf(5, W_, 256)
        fw.op(DVE, lambda e: e.scalar_tensor_tensor(out=o, in0=t1[:, W_:2 * W_], scalar=neglam_all[:, l:l + 1], in1=t1[:, 0:W_],
                                                    op0=ALU.mult, op1=ALU.add), reads=[AB[3], B_lay], writes=[AB[4]])
        fw.op(ACT, lambda e: e.activation(out=sq, in_=o, func=AF.Square), reads=[AB[4]], writes=[AB[4]])
        fw.op(PE, lambda e: e.matmul(pA[:, 0:W_], lhsT=ones_f, rhs=sq, start=True, stop=True), reads=[AB[4], B_const], writes=[B_pA])
        fw.op(DVE, lambda e: e.tensor_scalar(out=rs, in0=pA[:, 0:W_], scalar1=1.0 / 128, scalar2=RMS_EPS, op0=ALU.mult, op1=ALU.add),
              reads=[B_pA], writes=[AB[5]])
        fw.op(POOL, lambda e: e.tensor_tensor(out=rstd, in0=rs, in1=neghalf[:, 0:1].to_broadcast([128, W_]), op=ALU.pow), reads=[AB[5], B_const], writes=[AB[5]])
        fw.op(DVE, lambda e: e.tensor_tensor(out=o, in0=o, in1=rstd, op=ALU.mult), reads=[AB[4], AB[5]], writes=[AB[4]])
        return o

    def tile_pass(l, ti, sample):
        last_layer = (l == L - 1)
        if sample:
            ntok, subs = 64, [(0, 64)]
        else:
            ntok, subs = TT, [(s, 128) for s in range(4)]
        nsub = len(subs)
        tok0 = 0 if sample else ti * TT
        xsrc = xcur_s if sample else xcur_p
        ydst = (y_s if sample else y_p) if last_layer else xsrc

        def xbuf(s):
            return B_xcur_s if sample else B_xcur_p[ti * 4 + s]

        wq = []
        nxt = {"k": 0}

        def wtop():
            while wfree and nxt["k"] < NCH:
                sl_ = wfree.pop(0)
                wload(l, nxt["k"], sl_)
                wq.append(sl_)
                nxt["k"] += 1

        def wnext():
            wtop()
            return wq.pop(0)

        def wrel(*slots):
            for sl_ in slots:
                wfree.append(sl_)
            wtop()

        wtop()

        if sample:
            for b in range(NSEQ_S):
                for jt in range(8):
                    i = b * 8 + jt
                    sl = i % 4
                    kst = arb(sl, 1024)
                    fw.dma(POOL, [(lambda b, jt, kst: lambda e: e.dma_start(out=kst, in_=ck[l, b, jt * 128:(jt + 1) * 128, :]))(b, jt, kst)],
                           AB[sl], writes=[AB[sl]])
                    fw.dma(POOL, [(lambda b, jt: lambda e: e.dma_start(out=VX[:, b * 8 + jt, :, 0:128],
                                                                         in_=cv[l, b, jt * 128:(jt + 1) * 128, :].rearrange("p (h v) -> p h v", h=8)))(b, jt)],
                           B_VX, writes=[B_VX])
                    for h in range(8):
                        fw.op(PE, (lambda h, kst: lambda e: e.transpose(out=pT[:, h * 128:(h + 1) * 128], in_=kst[:, h * 128:(h + 1) * 128],
                                                                        identity=ident))(h, kst),
                              reads=[AB[sl], B_const], writes=[B_pT], sig=(h == 7))
                    E = ACT if i % 2 == 0 else DVE
                    cp(E, KT[:, :, b * PAST + jt * 128: b * PAST + (jt + 1) * 128], pT.rearrange("p (h k) -> p h k", h=8),
                       reads=[B_pT], writes=[B_KT])

        for s, n in subs:
            sl = 2 * (s % 2)
            xin = arf(sl, 1024)
            xbf = arb(4 + (s % 2), 1024)
            r0 = tok0 + s * 128
            dma1(SP, xin[0:n, :], xsrc[r0:r0 + n, :], AB[sl], reads=[xbuf(s)], writes=[AB[sl], AB[sl + 1]])
            cp(ACT, xbf[0:n, :], xin[0:n, :], reads=[AB[sl], AB[sl + 1]], writes=[AB[4 + s % 2]])
            for c in range(8):
                fw.op(PE, (lambda c, xbf, n: lambda e: e.transpose(out=pT[:, c * 128:c * 128 + n], in_=xbf[0:n, c * 128:(c + 1) * 128],
                                                                  identity=ident[0:n, 0:n]))(c, xbf, n),
                      reads=[AB[4 + s % 2], B_const], writes=[B_pT], sig=(c == 7))
            cp(DVE, xT[:, :, s * 128:s * 128 + n], pT.rearrange("p (c k) -> p c k", c=8)[:, :, 0:n], reads=[B_pT], writes=[B_xT])
        if sample:
            dma1(SP, cs_sb[0:64, 0, :], cs_s, B_cs, writes=[B_cs])
        else:
            dma1(SP, cs_sb, cs_p[tok0:tok0 + TT, :].rearrange("(s p) c -> p s c", p=128), B_cs, writes=[B_cs])

        if _STOP == 1:
            wfree[:] = [0, 1]
            return
        Wd = 128 if sample else 528
        pxT = arena[:, 0:4 * Wd].rearrange("p (g w) -> p g w", g=4)
        pxb = [AB[0], AB[1], AB[2], AB[3], AB[4]] if not sample else [AB[0]]
        pooled = arb(6, 4 * TT).rearrange("p (g t) -> p g t", g=4)
        pooledb = [AB[6], AB[7]]
        yaT = arb(10, 4 * TT).rearrange("p (g t) -> p g t", g=4)
        yab = [AB[10], AB[11]]
        gt0, gt1 = arf(8, 512), arf(9, 512)

        def outview(ap2d):
            if sample:
                return ap2d.rearrange("p (b w) -> p b w", b=4)[:, :, 16:32]
            return ap2d[:, 16:528]

        def as_out(ap2d):
            if sample:
                return ap2d.rearrange("p (b t) -> p b t", b=4)
            return ap2d

        if sample:
            fw.op(DVE, lambda e: e.memset(arena[:, 0:4 * Wd], 0.0), writes=pxb)
            sp_sb = arf(2, 512)
            dma1(SP, sp_sb[0:60, :], spool[l], AB[2], writes=[AB[2]])
            for g in range(4):
                fw.op(PE, (lambda g: lambda e: e.transpose(out=pA[:, g * 64:g * 64 + 60], in_=sp_sb[0:60, g * 128:(g + 1) * 128],
                                                           identity=identf[0:60, 0:60]))(g),
                      reads=[AB[2], B_const], writes=[B_pA], sig=(g == 3))
            for g in range(4):
                cp(ACT, pxT[:, g, :].rearrange("p (b w) -> p b w", b=4)[:, :, 1:16],
                   pA[:, g * 64:g * 64 + 60].rearrange("p (b j) -> p b j", b=4), reads=[B_pA], writes=pxb)
        elif ti == 0:
            fw.op(DVE, lambda e: e.memset(pxT[:, :, 0:16], 0.0), writes=pxb)
        else:
            cp(DVE, pxT[:, :, 0:16], hist, reads=[B_hist], writes=pxb)

        slot = wnext()
        Wc = WS[slot]
        for e_ in range(4):
            ps, psb = next_ps()
            mm_acc(ps[:, 0:ntok], psb, [(Wc[:, d, e_ * 128:(e_ + 1) * 128], xT[:, d, 0:ntok]) for d in range(8)], [B_WS[slot], B_xT])
            cp(ACT, outview(pxT[:, e_, :]), as_out(ps[:, 0:ntok]), reads=[psb], writes=pxb)
        if sample or ti == NT - 1:
            s_last, n_last = subs[-1]
            ps, psb = next_ps()
            mm_acc(ps[0:n_last, :], psb, [(xT[:, d, s_last * 128:s_last * 128 + n_last], Wc[:, d, :]) for d in range(8)], [B_WS[slot], B_xT])
            cp(ACT, gt1[0:n_last, :], ps[0:n_last, :], reads=[psb], writes=[AB[9]])
            if sample:
                fw.dma(SP, [(lambda b: lambda e: e.dma_start(out=pool_s[l, 15 * b:15 * b + 15, :], in_=gt1[16 * b + 1:16 * b + 16, :]))(b)
                            for b in range(4)], AB[9], reads=[AB[9]], is_out=True)
            else:
                dma1(SP, pool_p[l], gt1[113:128, :], AB[9], reads=[AB[9]], is_out=True)
        wrel(slot)
        if not sample:
            cp(POOL, hist, pxT[:, :, 512:528], reads=pxb, writes=[B_hist])
        for g in range(4):
            u = pxT[:, g, :]
            w = 2 ** (g + 1)

            def add(out, a, b_, rd, wr):
                fw.op(POOL, lambda e: e.tensor_tensor(out=out, in0=a, in1=b_, op=ALU.add), reads=rd, writes=wr)

            if g == 0:
                add(tmpA[:, 16:Wd], u[:, 16:Wd], u[:, 15:Wd - 1], pxb, [B_tmpA])
                s_ap, s_b = tmpA, B_tmpA
            elif g == 1:
                add(tmpA[:, 14:Wd], u[:, 14:Wd], u[:, 13:Wd - 1], pxb, [B_tmpA])
                add(tmpB[:, 16:Wd], tmpA[:, 16:Wd], tmpA[:, 14:Wd - 2], [B_tmpA], [B_tmpB])
                s_ap, s_b = tmpB, B_tmpB
            elif g == 2:
                add(tmpA[:, 10:Wd], u[:, 10:Wd], u[:, 9:Wd - 1], pxb, [B_tmpA])
                add(tmpB[:, 12:Wd], tmpA[:, 12:Wd], tmpA[:, 10:Wd - 2], [B_tmpA], [B_tmpB])
                add(tmpA[:, 16:Wd], tmpB[:, 16:Wd], tmpB[:, 12:Wd - 4], [B_tmpB], [B_tmpA])
                s_ap, s_b = tmpA, B_tmpA
            else:
                add(tmpA[:, 2:Wd], u[:, 2:Wd], u[:, 1:Wd - 1], pxb, [B_tmpA])
                add(tmpB[:, 4:Wd], tmpA[:, 4:Wd], tmpA[:, 2:Wd - 2], [B_tmpA], [B_tmpB])
                add(tmpA[:, 8:Wd], tmpB[:, 8:Wd], tmpB[:, 4:Wd - 4], [B_tmpB], [B_tmpA])
                add(tmpB[:, 16:Wd], tmpA[:, 16:Wd], tmpA[:, 8:Wd - 8], [B_tmpA], [B_tmpB])
                s_ap, s_b = tmpB, B_tmpB
            fw.op(DVE, (lambda g, s_ap, u, w: lambda e: e.scalar_tensor_tensor(out=as_out(pooled[:, g, 0:ntok]), in0=outview(s_ap[:, 0:Wd]),
                                                                                scalar=1.0 / w, in1=outview(u), op0=ALU.mult,
                                                                                op1=ALU.subtract))(g, s_ap, u, w),
                  reads=[s_b] + pxb, writes=pooledb)
            if (not sample) and ti == 0:
                fw.op(DVE, (lambda g, s_ap: lambda e: e.tensor_tensor(out=gt0[:, 0:16], in0=s_ap[:, 16:32], in1=invc_sb[:, g, :],
                                                                       op=ALU.mult))(g, s_ap), reads=[s_b, B_const], writes=[AB[8]])
                fw.op(DVE, (lambda g, u: lambda e: e.tensor_tensor(out=pooled[:, g, 0:16], in0=gt0[:, 0:16], in1=u[:, 16:32],
                                                                    op=ALU.subtract))(g, u), reads=[AB[8]] + pxb, writes=pooledb)
        slot = wnext()
        Wc = WS[slot]
        for e_ in range(4):
            mm_acc(pA[:, 0:ntok], B_pA, [(Wc[:, d, e_ * 128:(e_ + 1) * 128], xT[:, d, 0:ntok]) for d in range(8)], [B_WS[slot], B_xT])
            mm_acc(pB[:, 0:ntok], B_pB, [(wpool_sb[:, e_, :], pooled[:, e_, 0:ntok])], [B_wpool] + pooledb)
            sig_gate(pA[:, 0:ntok], B_pA, ntok, gt0, AB[8], silu=True)
            fw.op(DVE, (lambda e_: lambda e: e.scalar_tensor_tensor(out=yaT[:, e_, 0:ntok], in0=pB[:, 0:ntok],
                                                                     scalar=pscale_all[:, l, e_:e_ + 1], in1=gt0[:, 0:ntok],
                                                                     op0=ALU.mult, op1=ALU.mult))(e_),
                  reads=[B_pB, AB[8], B_lay], writes=yab)
        wrel(slot)

        if _STOP == 2:
            wfree[:] = [0, 1]
            return
        for c in range(6):
            slot = wnext()
            Wc = WS[slot]
            kind = c // 2
            if str(kind) not in _os.environ.get('KP2', '012'):
                wrel(slot)
                continue
            h0 = (c % 2) * 4
            col0 = h0 * 128
            for s, n in subs:
                i2 = (c * nsub + s) % 2
                ps, psb = next_ps()
                mm_acc(ps[0:n, :], psb, [(xT[:, d, s * 128:s * 128 + n], Wc[:, d, :]) for d in range(8)], [B_WS[slot], B_xT])
                r0 = tok0 + s * 128
                if kind == 2:
                    vst = arf(2 + i2, 512)
                    cp(ACT, vst[0:n, :], ps[0:n, :], reads=[psb], writes=[AB[2 + i2]])
                    dma1(SP, (v_s if sample else v_p)[l, r0:r0 + n, col0:col0 + 512], vst[0:n, :], AB[2 + i2], reads=[AB[2 + i2]], is_out=True)
                    if sample:
                        cp(DVE, vbf_s[0:64, col0:col0 + 512], vst[0:64, :], reads=[AB[2 + i2]], writes=[B_tmpA])
                    else:
                        cp(DVE, VX[0:n, ti * 4 + s, h0:h0 + 4, 0:128], vst[0:n, :].rearrange("p (h v) -> p h v", h=4), reads=[AB[2 + i2]],
                           writes=[B_VX])
                    continue
                ra = arf(4 + i2, 512) if kind == 0 else arf(0 + i2, 512)
                rab = AB[4 + i2] if kind == 0 else AB[0 + i2]
                rb_ = arf(6 + i2, 512)
                rbb = AB[6 + i2]
                ps3 = ps[0:n, :].rearrange("p (g x) -> p g x", g=8)
                ra3 = ra[0:n, :].rearrange("p (g x) -> p g x", g=8)
                rb3 = rb_[0:n, :].rearrange("p (g x) -> p g x", g=8)
                cosb = cs_sb[0:n, s, 0:64].unsqueeze(1).to_broadcast([n, 8, 64])
                sin1 = cs_sb[0:n, s, 64:96].unsqueeze(1).to_broadcast([n, 8, 32])
                sin2 = cs_sb[0:n, s, 96:128].unsqueeze(1).to_broadcast([n, 8, 32])
                fw.op(DVE, (lambda ra3, ps3, cosb: lambda e: e.tensor_tensor(out=ra3, in0=ps3, in1=cosb, op=ALU.mult))(ra3, ps3, cosb),
                      reads=[psb, B_cs], writes=[rab])
                fw.op(DVE, (lambda rb3, ps3, sin1: lambda e: e.tensor_tensor(out=rb3[:, :, 0:32], in0=ps3[:, :, 32:64], in1=sin1,
                                                                              op=ALU.mult))(rb3, ps3, sin1),
                      reads=[psb, B_cs], writes=[rbb])
                fw.op(DVE, (lambda rb3, ps3, sin2: lambda e: e.tensor_tensor(out=rb3[:, :, 32:64], in0=ps3[:, :, 0:32], in1=sin2,
                                                                              op=ALU.mult))(rb3, ps3, sin2),
                      reads=[psb, B_cs], writes=[rbb])
                bfv = arb(8 + i2, 512, 512 * kind)
                bfb = AB[8 + i2]
                if kind == 0:
                    fw.op(POOL, (lambda bfv, ra, rb_, n: lambda e: e.tensor_tensor(out=bfv[0:n, :], in0=ra[0:n, :], in1=rb_[0:n, :],
                                                                                   op=ALU.add))(bfv, ra, rb_, n),
                          reads=[rab, rbb], writes=[bfb])
                else:
                    fw.op(POOL, (lambda ra, rb_, n: lambda e: e.tensor_tensor(out=ra[0:n, :], in0=ra[0:n, :], in1=rb_[0:n, :],
                                                                              op=ALU.add))(ra, rb_, n),
                          reads=[rab, rbb], writes=[rab])
                    dma1(SP, (k_s if sample else k_p)[l, r0:r0 + n, col0:col0 + 512], ra[0:n, :], rab, reads=[rab], is_out=True)
                    cp(ACT, bfv[0:n, :], ra[0:n, :], reads=[rab], writes=[bfb])
                for hh in range(4):
                    fw.op(PE, (lambda hh, bfv, n: lambda e: e.transpose(out=pT[:, hh * 128:hh * 128 + n], in_=bfv[0:n, hh * 128:(hh + 1) * 128],
                                                                       identity=ident[0:n, 0:n]))(hh, bfv, n),
                          reads=[bfb, B_const], writes=[B_pT], sig=(hh == 3))
                src = pT[:, 0:512].rearrange("p (h k) -> p h k", h=4)[:, :, 0:n]
                if kind == 0:
                    cp(ACT, QT[:, h0:h0 + 4, s * 128:s * 128 + n], src, reads=[B_pT], writes=[B_QT])
                elif sample:
                    cp(DVE, KTn[:, h0:h0 + 4, 0:64], src, reads=[B_pT], writes=[B_KTn])
                else:
                    cp(DVE, KT[:, h0:h0 + 4, r0:r0 + n], src, reads=[B_pT], writes=[B_KT])
            wrel(slot)
        if _STOP == 3:
            wfree[:] = [0, 1]
            return
        ptc = {"i": 0}

        def next_pt():
            ptc["i"] += 1
            k = ptc["i"] % 4
            return arb(k // 2, 512, 512 * (k % 2)), AB[k // 2]

        sublnc = subln_all[:, l:l + 1]
        if not sample:
            pending = None
            for h in range(HEADS):
                for qbl in range(2):
                    qb = 2 * ti + qbl
                    jmax = 2 * qb + 1
                    qc0 = qbl * 256
                    npairs = qb + 1
                    pts = {}
                    for step in range(npairs + 1):
                        if step < npairs:
                            jp = step
                            pS, pSb = (pSX, B_pSX) if jp % 2 == 0 else (pSY, B_pSY)
                            for jl in range(2):
                                j = 2 * jp + jl
                                for m in range(2):
                                    fw.op(PE, (lambda pS, m, jl, j, h, qc0: lambda e: e.matmul(
                                        pS[:, m, jl * 256:(jl + 1) * 256], lhsT=KT[m * 64:(m + 1) * 64, h, j * 128:(j + 1) * 128],
                                        rhs=QT[m * 64:(m + 1) * 64, h, qc0:qc0 + 256], start=True, stop=True))(pS, m, jl, j, h, qc0),
                                          reads=[B_KT, B_QT], writes=[pSb], sig=(jl == 1 and m == 1))
                            for jl in range(2):
                                j = 2 * jp + jl
                                pt, ptb = next_pt()
                                pts[j] = (pt, ptb)
                                pt3 = pt.rearrange("p (m q) -> p m q", m=2)
                                fw.op(ACT, (lambda pt3, pS, jl: lambda e: e.activation(out=pt3, in_=pS[:, :, jl * 256:(jl + 1) * 256],
                                                                                      func=AF.Exp, scale=0.125))(pt3, pS, jl),
                                      reads=[pSb], writes=[ptb])
                                if jp == qb:
                                    if jl == 0:
                                        fw.op(POOL, (lambda pt3: lambda e: e.memset(pt3[64:128, :, 0:64], 0.0))(pt3), writes=[ptb])
                                    else:
                                        fw.op(POOL, (lambda pt3: lambda e: e.memset(pt3[0:64, :, 0:128], 0.0))(pt3), writes=[ptb])
                                        fw.op(POOL, (lambda pt3: lambda e: e.memset(pt3[64:128, :, 0:192], 0.0))(pt3), writes=[ptb])
                        if step >= 1:
                            for jl in range(2):
                                j = 2 * (step - 1) + jl
                                pt, ptb = pts.pop(j)
                                fw.op(PE, (lambda pt, j, h, jmax: lambda e: e.matmul(pO, lhsT=VX[:, j, h, 0:128], rhs=pt, start=(j == 0),
                                                                                    stop=(j == jmax)))(pt, j, h, jmax),
                                      reads=[B_VX, ptb], writes=[B_pO], sig=False)
                                fw.op(PE, (lambda pt, j, jmax: lambda e: e.matmul(pB, lhsT=ones_bf, rhs=pt, start=(j == 0),
                                                                                 stop=(j == jmax)))(pt, j, jmax),
                                      reads=[B_const, ptb], writes=[B_pB], sig=True)
                        if step == 1 and pending is not None:
                            pending()
                            pending = None
                    attn_epilogue1(256)

                    def _p2(h=h, qc0=qc0):
                        o = attn_epilogue2(l, 256)
                        fw.op(ACT, lambda e: e.activation(out=ybT[:, h, qc0:qc0 + 256], in_=o, func=AF.Copy, scale=sublnc),
                              reads=[AB[4], B_lay], writes=[B_ybT])
                    pending = _p2
            if pending is not None:
                pending()
        else:
            for b in range(NSEQ_S):
                fw.op(PE, lambda e: e.matmul(pO[:, 0:256], lhsT=zeros_bf[:, 0:128], rhs=zeros_bf[:, 0:256], start=True, stop=True),
                      reads=[B_const], writes=[B_pO])
                fw.op(PE, lambda e: e.matmul(pB[:, 0:256], lhsT=zeros_bf[:, 0:128], rhs=zeros_bf[:, 0:256], start=True, stop=True),
                      reads=[B_const], writes=[B_pB])
                for grp in range(3):
                    pS, pSb = (pSX, B_pSX) if grp % 2 == 0 else (pSY, B_pSY)
                    njl = 4 if grp < 2 else 1
                    kp = 128 if grp < 2 else TS
                    for jl in range(njl):
                        jt = grp * 4 + jl
                        for h in range(HEADS):
                            for m in range(2):
                                if grp < 2:
                                    lt = KT[m * 64:(m + 1) * 64, h, b * PAST + jt * 128: b * PAST + (jt + 1) * 128]
                                else:
                                    lt = KTn[m * 64:(m + 1) * 64, h, b * TS:(b + 1) * TS]
                                fw.op(PE, (lambda pS, m, jl, h, lt, b, kp: lambda e: e.matmul(
                                    pS[0:kp, m, jl * 128 + h * TS: jl * 128 + (h + 1) * TS], lhsT=lt,
                                    rhs=QT[m * 64:(m + 1) * 64, h, b * TS:(b + 1) * TS], start=True, stop=True))(pS, m, jl, h, lt, b, kp),
                                      reads=[B_KT, B_KTn, B_QT], writes=[pSb], sig=(h == 7 and m == 1 and jl == njl - 1))
                    if grp < 2:
                        pt, ptb = arb(0, 1024), AB[0]
                        if grp == 1:
                            pt, ptb = arb(1, 1024), AB[1]
                        pt4 = pt.rearrange("p (m x) -> p m x", m=2)
                        fw.op(ACT, (lambda pt4, pS: lambda e: e.activation(out=pt4, in_=pS, func=AF.Exp, scale=0.125))(pt4, pS),
                              reads=[pSb], writes=[ptb])
                        for jl in range(4):
                            jt = grp * 4 + jl
                            for m in range(2):
                                fw.op(PE, (lambda pt4, m, jl: lambda e: e.matmul(pB[:, m * 128:(m + 1) * 128], lhsT=ones_bf,
                                                                                rhs=pt4[:, m, jl * 128:(jl + 1) * 128], start=False,
                                                                                stop=False, skip_group_check=True))(pt4, m, jl),
                                      reads=[ptb, B_const], writes=[B_pB], sig=False)
                                for h in range(HEADS):
                                    fw.op(PE, (lambda pt4, m, jl, h, jt, b: lambda e: e.matmul(
                                        pO[:, m * 128 + h * TS: m * 128 + (h + 1) * TS], lhsT=VX[:, b * 8 + jt, h, 0:128],
                                        rhs=pt4[:, m, jl * 128 + h * TS: jl * 128 + (h + 1) * TS], start=False, stop=False,
                                        skip_group_check=True))(pt4, m, jl, h, jt, b),
                                          reads=[ptb, B_VX], writes=[B_pO], sig=(m == 1 and h == 7))
                    else:
                        dma1(SP, VN[0:TS, :, :], vbf_s[TS * b:TS * b + TS, :].rearrange("p (h v) -> p h v", h=8), B_VN,
                             reads=[B_tmpA], writes=[B_VN])
                        fw.op(ACT, (lambda pS: lambda e: e.activation(out=PTn[0:TS].rearrange("p m h q -> p m (h q)"), in_=pS[0:TS, :, 0:128],
                                                                       func=AF.Exp, scale=0.125))(pS),
                              reads=[pSb], writes=[B_PTn])
                        for m in range(2):
                            fw.op(PE, (lambda m: lambda e: e.matmul(pB[:, m * 128:(m + 1) * 128], lhsT=ones_bf,
                                                                    rhs=PTn[:, m].rearrange("p h q -> p (h q)"), start=False, stop=False,
                                                                    skip_group_check=True))(m),
                                  reads=[B_PTn, B_const], writes=[B_pB], sig=False)
                            for h in range(HEADS):
                                fw.op(PE, (lambda m, h, b: lambda e: e.matmul(pO[:, m * 128 + h * TS: m * 128 + (h + 1) * TS], lhsT=VN[:, h, :],
                                                                              rhs=PTn[:, m, h, :], start=False, stop=True,
                                                                              skip_group_check=True))(m, h, b),
                                      reads=[B_PTn, B_VN], writes=[B_pO, B_pB], sig=(m == 1 and h == 7))
                attn_epilogue1(128)
                o = attn_epilogue2(l, 128)
                fw.op(ACT, (lambda b, o: lambda e: e.activation(out=ybT[:, :, b * TS:(b + 1) * TS], in_=o.rearrange("p (h q) -> p h q", h=8),
                                                                 func=AF.Copy, scale=sublnc))(b, o),
                      reads=[AB[4], B_lay], writes=[B_ybT])

        if _STOP == 4:
            wfree[:] = [0, 1]
            return
        gt0, gt1 = arf(8, 512), arf(9, 512)
        for c in range(2):
            slot = wnext()
            Wc = WS[slot]
            for e_ in range(4):
                h = 4 * c + e_
                ps, psb = next_ps()
                mm_acc(ps[:, 0:ntok], psb, [(Wc[:, d, e_ * 128:(e_ + 1) * 128], xT[:, d, 0:ntok]) for d in range(8)], [B_WS[slot], B_xT])
                sig_gate(ps[:, 0:ntok], psb, ntok, gt0, AB[8], silu=True)
                fw.op(DVE, (lambda h: lambda e: e.tensor_tensor(out=ybT[:, h, 0:ntok], in0=ybT[:, h, 0:ntok], in1=gt0[:, 0:ntok],
                                                                 op=ALU.mult))(h), reads=[B_ybT, AB[8]], writes=[B_ybT])
            wrel(slot)

        if _STOP == 5:
            wfree[:] = [0, 1]
            return
        m1 = arena[:, 0:8 * 512].rearrange("p (c t) -> p c t", c=8)
        m1b = AB[0:8]
        slot_a = wnext()
        WA = WS[slot_a].rearrange("p k c -> p (k c)").rearrange("p (g c) -> p g c", g=4)
        for half in range(2):
            slot = wnext()
            Wc = WS[slot]
            for dcl in range(4):
                dc = 4 * half + dcl
                mm_acc(pA[:, 0:ntok], B_pA, [(WA[:, g, dc * 128:(dc + 1) * 128], yaT[:, g, 0:ntok]) for g in range(4)], [B_WS[slot_a]] + yab)
                mm_acc(pB[:, 0:ntok], B_pB, [(Wc[:, d, dcl * 128:(dcl + 1) * 128], xT[:, d, 0:ntok]) for d in range(8)], [B_WS[slot], B_xT])
                sig_gate(pB[:, 0:ntok], B_pB, ntok, gt0, AB[8], silu=False)
                fw.op(DVE, (lambda dc: lambda e: e.tensor_tensor(out=m1[:, dc, 0:ntok], in0=pA[:, 0:ntok], in1=gt0[:, 0:ntok],
                                                                  op=ALU.mult))(dc), reads=[B_pA, AB[8]], writes=[m1b[dc]])
            if half == 0:
                wrel(slot)
        wrel(slot_a, slot)
        mergedT = QT
        for half in range(2):
            slot_b = wnext()
            WB = WS[slot_b]
            slot = wnext()
            Wc = WS[slot]
            for dcl in range(4):
                dc = 4 * half + dcl
                mm_acc(pA[:, 0:ntok], B_pA, [(WB[:, h, dcl * 128:(dcl + 1) * 128], ybT[:, h, 0:ntok]) for h in range(8)], [B_WS[slot_b], B_ybT])
                mm_acc(pB[:, 0:ntok], B_pB, [(Wc[:, d, dcl * 128:(dcl + 1) * 128], xT[:, d, 0:ntok]) for d in range(8)], [B_WS[slot], B_xT])
                sig_gate(pB[:, 0:ntok], B_pB, ntok, gt0, AB[8], silu=False)
                fw.op(DVE, lambda e: e.tensor_tensor(out=gt1[:, 0:ntok], in0=pA[:, 0:ntok], in1=gt0[:, 0:ntok], op=ALU.mult),
                      reads=[B_pA, AB[8]], writes=[AB[9]])
                fw.op(POOL, (lambda dc: lambda e: e.tensor_tensor(out=mergedT[:, dc, 0:ntok], in0=m1[:, dc, 0:ntok], in1=gt1[:, 0:ntok],
                                                                   op=ALU.add))(dc), reads=[m1b[dc], AB[9]], writes=[B_QT])
            wrel(slot_b, slot)

        if _STOP == 6:
            wfree[:] = [0, 1]
            return
        slot0 = wnext()
        slot1 = wnext()
        load_lnp(ln_g[l:l + 1, :], ln_b[l:l + 1, :])
        for s, n in subs:
            i2 = s % 2
            xres = arf(2 * i2, 1024)
            xrb = [AB[2 * i2], AB[2 * i2 + 1]]
            r_ap = arf(4 + 2 * i2, 1024)
            rbufs = [AB[4 + 2 * i2], AB[5 + 2 * i2]]
            r0 = tok0 + s * 128
            dma1(SP, xres[0:n, :], xsrc[r0:r0 + n, :], AB[2 * i2], reads=[xbuf(s)], writes=xrb)
            for hf, (pp_, ppb, sl_) in enumerate([(pA, B_pA, slot0), (pB, B_pB, slot1)]):
                mm_acc(pp_[0:n, :], ppb, [(mergedT[:, dc, s * 128:s * 128 + n], WS[sl_][:, dc, :]) for dc in range(8)], [B_QT, B_WS[sl_]])
                fw.op(DVE, (lambda hf, pp_, n, r_ap, xres: lambda e: e.scalar_tensor_tensor(
                    out=r_ap[0:n, hf * 512:(hf + 1) * 512], in0=xres[0:n, hf * 512:(hf + 1) * 512], scalar=ALPHA, in1=pp_[0:n, :],
                    op0=ALU.mult, op1=ALU.add))(hf, pp_, n, r_ap, xres), reads=xrb + [ppb], writes=rbufs)
            ln_rows(r_ap, rbufs, n, arf(8, 1024), arf(10, 1024), [AB[8], AB[9], AB[10], AB[11]])
            dma1(SP, ydst[r0:r0 + n, :], r_ap[0:n, :], AB[4 + 2 * i2], reads=rbufs, writes=[xbuf(s)], is_out=last_layer)
        assert nxt["k"] == NCH and not wq, (nxt, wq)
        wfree.extend([slot0, slot1])

    for l in range(L):
        fw.dma(POOL, [(lambda l: lambda e: e.dma_start(out=wpool_sb, in_=w_pool[l].rearrange("g c d -> c g d")))(l)], B_wpool, writes=[B_wpool])
        if l + 1 < L:
            cast_weights(l + 1)
        if with_sample:
            fw.new_epoch()
            tile_pass(l, 0, True)
        for ti in range(NT):
            if ti % 3 == 0:
                fw.new_epoch()
            tile_pass(l, ti, False)

    fw.emit()
    return nc, fw


def _rope_table(pos):
    half = 32
    inv = (np.float32(10000.0) ** (-np.arange(half, dtype=np.float32) / np.float32(half))).astype(np.float32)
    ang = pos.astype(np.float32)[:, None] * inv[None, :]
    c = np.cos(ang).astype(np.float32)
    s = np.sin(ang).astype(np.float32)
    return np.ascontiguousarray(np.concatenate([c, c, -s, s], axis=1).astype(np.float32))


def make_in_maps(inputs, L, SEQ, n_cores=8, with_sample=True):
    f = lambda a: np.ascontiguousarray(np.asarray(a, dtype=np.float32))
    cs_p = _rope_table(np.arange(SEQ))
    cs_s = np.ascontiguousarray(np.tile(_rope_table(PAST + np.arange(TS)), (NSEQ_S, 1)))
    invc = np.zeros((4, 16), np.float32)
    for g in range(4):
        w = 2 ** (g + 1)
        invc[g] = 1.0 / np.minimum(np.arange(16) + 1, w)
    common = dict(
        ln_in_g=f(inputs["ln_in_g"]).reshape(1, D), ln_in_b=f(inputs["ln_in_b"]).reshape(1, D),
        w_in=f(inputs["w_in"])[:L], w_pool=f(inputs["w_pool"])[:L], pool_scale=f(inputs["pool_scale"])[:L],
        lambda_qk=f(inputs["lambda_qk"])[:L].reshape(L, 256), subln_w=f(inputs["subln_w"])[:L],
        w_a=f(inputs["w_a"])[:L], w_b=f(inputs["w_b"])[:L], w_o=f(inputs["w_o"])[:L],
        ln_g=f(inputs["ln_g"])[:L], ln_b=f(inputs["ln_b"])[:L],
        cs_p=cs_p, cs_s=cs_s, idn_bf=np.eye(128).astype(ml_dtypes.bfloat16), idn_f=np.eye(128, dtype=np.float32),
        invc=invc.reshape(1, 64),
    )
    xp = f(inputs["x_prompt"])
    xs = f(inputs["x_sample"])
    ck = np.asarray(inputs["cache_k"], dtype=np.float32)
    cv = np.asarray(inputs["cache_v"], dtype=np.float32)
    sp = np.asarray(inputs["state_pool"], dtype=np.float32)
    maps = []
    for c in range(n_cores):
        m = dict(common)
        m["xp"] = np.ascontiguousarray(xp[c, :SEQ])
        m["xs"] = np.ascontiguousarray(xs[4 * c:4 * c + 4].reshape(64, D))
        m["ck"] = np.ascontiguousarray(ck[:L, 4 * c:4 * c + 4].reshape(L, 4, PAST, D))
        m["cv"] = np.ascontiguousarray(cv[:L, 4 * c:4 * c + 4].reshape(L, 4, PAST, D))
        m["spool"] = np.ascontiguousarray(sp[:L, 4 * c:4 * c + 4].reshape(L, 60, 512))
        maps.append(m)
    return maps


def gather(results, L, SEQ, n_cores=8):
    y_p = np.stack([r["y_p"] for r in results]).reshape(n_cores, SEQ, D)
    y_s = np.stack([r["y_s"] for r in results]).reshape(n_cores * 4, TS, D)
    k_p = np.stack([r["k_p"] for r in results], axis=1).reshape(L, n_cores, SEQ, HEADS, 128)
    v_p = np.stack([r["v_p"] for r in results], axis=1).reshape(L, n_cores, SEQ, HEADS, 128)
    pool_p = np.stack([r["pool_p"] for r in results], axis=1).reshape(L, n_cores, 15, 512)
    k_s = np.stack([r["k_s"] for r in results], axis=1).reshape(L, n_cores * 4, TS, HEADS, 128)
    v_s = np.stack([r["v_s"] for r in results], axis=1).reshape(L, n_cores * 4, TS, HEADS, 128)
    pool_s = np.stack([r["pool_s"] for r in results], axis=1).reshape(L, n_cores * 4, 15, 512)
    return tuple(np.ascontiguousarray(a.astype(np.float32)) for a in (y_p, y_s, k_p, v_p, pool_p, k_s, v_s, pool_s))


_CACHE = {}


def kernel(**inputs):
    L, SEQ = DEPTH_FULL, 4096
    if "nc" not in _CACHE:
        _CACHE["nc"] = build_program(L, SEQ)[0]
    nc = _CACHE["nc"]
    maps = make_in_maps(inputs, L, SEQ)
    res = run_bass_kernel_spmd(nc, maps, core_ids=list(range(8)))
    return gather(res.results, L, SEQ)
```

```python
import math
import numpy as np
import ml_dtypes
import concourse.bass as bass
import concourse.mybir as mybir
from concourse.bass_utils import run_bass_kernel_spmd

F32 = mybir.dt.float32
BF16 = mybir.dt.bfloat16
AF = mybir.ActivationFunctionType
ALU = mybir.AluOpType

D = 1024
TT = 512
HEADS = 8
DEPTH_FULL = 4
ALPHA = (2 * DEPTH_FULL) ** 0.25
LN_EPS = 1e-5
RMS_EPS = 1e-5
PAST = 1024
NSEQ_S = 4
TS = 16
NCH = 19


class Buf:
    __slots__ = ("name", "w", "r", "dsem", "dcnt")

    def __init__(self, name):
        self.name = name
        self.w = None
        self.r = {}
        self.dsem = None
        self.dcnt = 0


class Eng:
    def __init__(self, fw, name, is_pe=False):
        self.fw = fw
        self.name = name
        self.is_pe = is_pe
        self.prog = []
        self.seen = {}
        self.sem = None
        self.cnt = 0
        self.epoch = 0
        self.new_epoch()

    def new_epoch(self):
        self.sem = self.fw.nc.alloc_semaphore(f"s_{self.name}_{self.epoch}")
        self.fw.nsem += 1
        self.epoch += 1
        self.cnt = 0


class FW:
    def __init__(self, nc):
        self.nc = nc
        self.nsem = 0
        self.pe = Eng(self, "pe", is_pe=True)
        self.act = Eng(self, "act")
        self.dve = Eng(self, "dve")
        self.pool = Eng(self, "pool")
        self.sp = Eng(self, "sp")
        self.out_events = []
        self.ninst = 0

    def new_epoch(self):
        for e in (self.pe, self.act, self.dve, self.pool):
            if e.cnt > 0:
                e.new_epoch()

    def _collect(self, E, reads, writes):
        waits = []

        def need(ev, same_ok):
            if ev is None:
                return
            sem, val = ev
            if sem is E.sem and same_ok:
                return
            k = id(sem)
            if E.seen.get(k, 0) >= val:
                return
            E.seen[k] = val
            waits.append((sem, val))

        for b in reads:
            need(b.w, E.is_pe)
        for b in writes:
            need(b.w, E.is_pe)
            for ev in b.r.values():
                need(ev, True)
        return waits

    def op(self, E, fn, reads=(), writes=(), sig=True):
        waits = self._collect(E, reads, writes)
        self.ninst += 1
        if sig:
            E.cnt += 1
            ev = (E.sem, E.cnt)
            E.prog.append((waits, fn, (E.sem, 1)))
        else:
            ev = (E.sem, E.cnt + 1)
            E.prog.append((waits, fn, None))
        for b in reads:
            b.r[id(E.sem)] = ev
        for b in writes:
            b.w = ev
            b.r = {}
        return ev

    def dma(self, Q, fns, sembuf, reads=(), writes=(), is_out=False):
        waits = self._collect(Q, reads, writes)
        if sembuf.dsem is None:
            sembuf.dsem = self.nc.alloc_semaphore(f"d_{sembuf.name}")
            self.nsem += 1
        for i, fn in enumerate(fns):
            sembuf.dcnt += 16
            self.ninst += 1
            Q.prog.append((waits if i == 0 else [], fn, (sembuf.dsem, 16)))
        ev = (sembuf.dsem, sembuf.dcnt)
        for b in reads:
            b.r[id(sembuf.dsem)] = ev
        for b in writes:
            b.w = ev
            b.r = {}
        if is_out:
            self.out_events.append(ev)
        return ev

    def emit(self):
        nc = self.nc
        last = {}
        for sem, val in self.out_events:
            k = id(sem)
            if k not in last or last[k][1] < val:
                last[k] = (sem, val)
        self.sp.prog.append((list(last.values()), None, None))

        def replay(E, eng):
            for waits, fn, inc in E.prog:
                for sem, val in waits:
                    eng.wait_ge(sem, val)
                if fn is None:
                    continue
                ins = fn(eng)
                if inc is not None:
                    ins.then_inc(inc[0], inc[1])

        with nc.Block() as block:
            @block.tensor
            def _(eng):
                replay(self.pe, eng)

            @block.scalar
            def _(eng):
                replay(self.act, eng)

            @block.vector
            def _(eng):
                replay(self.dve, eng)

            @block.gpsimd
            def _(eng):
                replay(self.pool, eng)

            @block.sync
            def _(eng):
                replay(self.sp, eng)


def build_program(L, SEQ, with_sample=True):
    import os as _os
    _STOP = int(_os.environ.get('KSTOP', '99'))
    NT = SEQ // TT
    KTW = max(SEQ, NSEQ_S * PAST)
    NKT = KTW // 128
    nc = bass.Bass("TRN2", target_bir_lowering=False)
    fw = FW(nc)
    PE, ACT, DVE, POOL, SP = fw.pe, fw.act, fw.dve, fw.pool, fw.sp

    def din(name, shape, dt=F32):
        return nc.dram_tensor(name, shape, dt, kind="ExternalInput").ap()

    def dout(name, shape, dt=F32):
        return nc.dram_tensor(name, shape, dt, kind="ExternalOutput").ap()

    xp = din("xp", [SEQ, D])
    xs_in = din("xs", [64, D])
    ck = din("ck", [L, NSEQ_S, PAST, D])
    cv = din("cv", [L, NSEQ_S, PAST, D])
    spool = din("spool", [L, 60, 512])
    ln_in_g = din("ln_in_g", [1, D])
    ln_in_b = din("ln_in_b", [1, D])
    w_in = din("w_in", [L, D, 7168])
    w_pool = din("w_pool", [L, 4, 128, 128])
    pool_scale = din("pool_scale", [L, 512])
    lambda_qk = din("lambda_qk", [L, 256])
    subln_w = din("subln_w", [L, 128])
    w_a = din("w_a", [L, 512, D])
    w_b = din("w_b", [L, D, D])
    w_o = din("w_o", [L, D, D])
    ln_g = din("ln_g", [L, D])
    ln_b = din("ln_b", [L, D])
    cs_p = din("cs_p", [SEQ, 128])
    cs_s = din("cs_s", [64, 128])
    idn_bf = din("idn_bf", [128, 128], BF16)
    idn_f = din("idn_f", [128, 128])
    invc = din("invc", [1, 64])

    y_p = dout("y_p", [SEQ, D])
    y_s = dout("y_s", [64, D])
    k_p = dout("k_p", [L, SEQ, D])
    v_p = dout("v_p", [L, SEQ, D])
    pool_p = dout("pool_p", [L, 15, 512])
    k_s = dout("k_s", [L, 64, D])
    v_s = dout("v_s", [L, 64, D])
    pool_s = dout("pool_s", [L, 60, 512])

    xcur_p = nc.dram_tensor("xcur_p", [SEQ, D], F32, kind="Internal").ap()
    xcur_s = nc.dram_tensor("xcur_s", [64, D], F32, kind="Internal").ap()
    wsc = nc.dram_tensor("wsc", [L, NCH, 128, 4096], BF16, kind="Internal").ap()
    B_wsc = [Buf(f"wsc{l}") for l in range(L)]
    B_xcur_p = [Buf(f"xcp{i}") for i in range(SEQ // 128)]
    B_xcur_s = Buf("xcs")

    def sb(name, shape, dt=F32):
        return nc.alloc_sbuf_tensor(name, shape, dt).ap()

    KT = sb("KT", [128, HEADS, KTW], BF16)
    VX = sb("VX", [128, NKT, HEADS, 128], BF16)
    xT = sb("xT", [128, 8, TT], BF16)
    QT = sb("QT", [128, 8, TT], BF16)
    ybT = sb("ybT", [128, 8, TT], BF16)
    WS = [sb(f"WS{i}", [128, 8, 512], BF16) for i in range(2)]
    arena = sb("arena", [128, 12 * 512], F32)
    arena_bf = arena.bitcast(BF16)
    tmpA = sb("tmpA", [128, 528], F32)
    tmpB = sb("tmpB", [128, 528], F32)
    hist = sb("hist", [128, 4, 16], F32)
    cs_sb = sb("cs_sb", [128, 4, 128], F32)
    ident = sb("ident", [128, 128], BF16)
    identf = sb("identf", [128, 128], F32)
    ones_bf = sb("ones_bf", [128, 128], BF16)
    ones_f = sb("ones_f", [128, 128], F32)
    zeros_bf = sb("zeros_bf", [128, 256], BF16)
    neghalf = sb("neghalf", [128, 2], F32)
    epsc = sb("epsc", [128, 2], F32)
    invc_sb = sb("invc_sb", [128, 4, 16], F32)
    wpool_sb = sb("wpool_sb", [128, 4, 128], BF16)
    pscale_all = sb("pscale_all", [128, L, 4], F32)
    subln_all = sb("subln_all", [128, L], F32)
    lq_all = arena[:, 0:L * 256].rearrange("p (l c) -> p l c", l=L)
    lam_tmp = sb("lam_tmp", [128, 4 * L + 8], F32)
    neglam_all = sb("neglam_all", [128, L], F32)
    stats = sb("stats", [128, 2, 6], F32)
    mv = sb("mv", [128, 8], F32)
    KTn = sb("KTn", [128, HEADS, 64], BF16)
    VN = sb("VN", [128, HEADS, 128], BF16)
    PTn = sb("PTn", [128, 2, HEADS, TS], BF16)
    vbf_s = tmpA.bitcast(BF16)[0:64, 0:D]

    B_KT, B_VX, B_xT, B_QT, B_ybT = Buf("KT"), Buf("VX"), Buf("xT"), Buf("QT"), Buf("ybT")
    B_WS = [Buf(f"WS{i}") for i in range(2)]
    AB = [Buf(f"ar{i}") for i in range(12)]
    B_tmpA, B_tmpB, B_hist, B_cs = Buf("tmpA"), Buf("tmpB"), Buf("hist"), Buf("cs")
    B_const = Buf("const")
    B_lay = Buf("laycst")
    B_stats, B_mv = Buf("stats"), Buf("mv")
    B_KTn, B_VN, B_PTn = Buf("KTn"), Buf("VN"), Buf("PTn")
    B_wpool = Buf("wpool")

    pA = nc.alloc_psum_tensor("pA", [128, 512], F32).ap()
    pB = nc.alloc_psum_tensor("pB", [128, 512], F32).ap()
    pT = nc.alloc_psum_tensor("pT", [128, 1024], BF16).ap()
    pO = nc.alloc_psum_tensor("pO", [128, 512], F32).ap()
    pSX = nc.alloc_psum_tensor("pSX", [128, 2, 512], F32).ap()
    pSY = nc.alloc_psum_tensor("pSY", [128, 2, 512], F32).ap()
    B_pA, B_pB, B_pT, B_pO, B_pSX, B_pSY = (Buf(n) for n in ["pA", "pB", "pT", "pO", "pSX", "pSY"])

    def arf(slot, ncols, off=0):
        return arena[:, slot * 512 + off: slot * 512 + off + ncols]

    def arb(slot, ncols, off=0):
        return arena_bf[:, slot * 1024 + off: slot * 1024 + off + ncols]

    def cp(E, out, in_, reads, writes):
        if E is ACT:
            return fw.op(E, lambda e: e.activation(out=out, in_=in_, func=AF.Copy), reads=reads, writes=writes)
        return fw.op(E, lambda e: e.tensor_copy(out=out, in_=in_), reads=reads, writes=writes)

    def dma1(Q, out, in_, sembuf, reads=(), writes=(), is_out=False, slow=False):
        if slow:
            f = lambda e: e.dma_start(out=out, in_=in_, allow_slow_non_contiguous=True)
        else:
            f = lambda e: e.dma_start(out=out, in_=in_)
        return fw.dma(Q, [f], sembuf, reads=reads, writes=writes, is_out=is_out)

    dma1(SP, ident, idn_bf, B_const, writes=[B_const])
    dma1(SP, identf, idn_f, B_const, writes=[B_const])
    dma1(SP, invc_sb.rearrange("p g t -> p (g t)"), invc.partition_broadcast(128), B_const, writes=[B_const])
    dma1(SP, arena[:, 0:L * 256], lambda_qk.rearrange("(o l) c -> o (l c)", o=1).partition_broadcast(128),
         AB[0], writes=[AB[0], AB[1]])
    dma1(SP, pscale_all, pool_scale.rearrange("l (g p) -> p l g", p=128), B_lay, writes=[B_lay], slow=True)
    dma1(SP, subln_all, subln_w.rearrange("l p -> p l"), B_lay, writes=[B_lay], slow=True)
    fw.op(DVE, lambda e: e.memset(ones_bf, 1.0), writes=[B_const])
    fw.op(DVE, lambda e: e.memset(ones_f, 1.0), writes=[B_const])
    fw.op(DVE, lambda e: e.memset(zeros_bf, 0.0), writes=[B_const])
    fw.op(DVE, lambda e: e.memset(neghalf, -0.5), writes=[B_const])
    fw.op(DVE, lambda e: e.memset(epsc[:, 0:1], LN_EPS), writes=[B_const])
    fw.op(DVE, lambda e: e.memset(epsc[:, 1:2], RMS_EPS), writes=[B_const])
    fw.op(DVE, lambda e: e.memset(VN, 0.0), writes=[B_VN])
    fw.op(DVE, lambda e: e.memset(PTn, 0.0), writes=[B_PTn])
    for l in range(L):
        lam_init = 0.8 - 0.6 * math.exp(-0.3 * l)
        for i in range(2):
            fw.op(DVE, (lambda l, i: lambda e: e.tensor_tensor(out=tmpA[:, 0:64], in0=lq_all[:, l, 128 * i:128 * i + 64],
                                                                 in1=lq_all[:, l, 128 * i + 64:128 * i + 128], op=ALU.mult))(l, i),
                  reads=[AB[0], AB[1]], writes=[B_tmpA])
            fw.op(DVE, (lambda l, i: lambda e: e.reduce_sum(out=lam_tmp[:, 4 * l + i:4 * l + i + 1], in_=tmpA[:, 0:64],
                                                              axis=mybir.AxisListType.X))(l, i),
                  reads=[B_tmpA], writes=[B_lay])
        fw.op(ACT, (lambda l: lambda e: e.activation(out=lam_tmp[:, 4 * l + 2:4 * l + 4], in_=lam_tmp[:, 4 * l:4 * l + 2], func=AF.Exp))(l),
              reads=[B_lay], writes=[B_lay])
        fw.op(DVE, (lambda l, li: lambda e: e.scalar_tensor_tensor(out=neglam_all[:, l:l + 1], in0=lam_tmp[:, 4 * l + 3:4 * l + 4],
                                                                     scalar=-li, in1=lam_tmp[:, 4 * l + 2:4 * l + 3],
                                                                     op0=ALU.add, op1=ALU.subtract))(l, lam_init),
              reads=[B_lay], writes=[B_lay])
        fw.op(DVE, (lambda l, li: lambda e: e.tensor_scalar(out=subln_all[:, l:l + 1], in0=subln_all[:, l:l + 1], scalar1=1.0 - li,
                                                              scalar2=None, op0=ALU.mult))(l, lam_init),
              reads=[B_lay], writes=[B_lay])

    def wsc_view(l, ci, kc, ncol):
        return wsc[l, ci].rearrange("p (k c) -> p k c", k=kc)

    def cast_weights(l):
        fns = []
        for ci in range(14):
            src = w_in[l].rearrange("(k p) c -> p k c", p=128)[:, :, ci * 512:(ci + 1) * 512]
            dst = wsc_view(l, ci, 8, 512)
            fns.append((lambda s, d: lambda e: e.dma_start(out=d, in_=s))(src, dst))
        src = w_a[l].rearrange("(g p) c -> p g c", p=128)
        fns.append((lambda s, d: lambda e: e.dma_start(out=d, in_=s))(src, wsc_view(l, 14, 4, 1024)))
        for hb in range(2):
            src = w_b[l].rearrange("(k p) c -> p k c", p=128)[:, :, hb * 512:(hb + 1) * 512]
            fns.append((lambda s, d: lambda e: e.dma_start(out=d, in_=s))(src, wsc_view(l, 15 + hb, 8, 512)))
        for hb in range(2):
            src = w_o[l].rearrange("(k p) c -> p k c", p=128)[:, :, hb * 512:(hb + 1) * 512]
            fns.append((lambda s, d: lambda e: e.dma_start(out=d, in_=s))(src, wsc_view(l, 17 + hb, 8, 512)))
        fw.dma(POOL, fns, B_wsc[l], writes=[B_wsc[l]])

    cast_weights(0)

    CH_ORDER = [0, 1, 2, 3, 4, 5, 6, 7, 8, 9, 14, 10, 11, 15, 12, 16, 13, 17, 18]
    wfree = [0, 1]

    def wload(l, k, slot):
        ci = CH_ORDER[k]
        dst = WS[slot].rearrange("p k c -> p (k c)")
        dma1(SP, dst, wsc[l, ci], B_WS[slot], reads=[B_wsc[l]], writes=[B_WS[slot]])

    def ln_rows(r_ap, r_bufs, n, g_ap, b_ap, pbufs):
        for hlf in range(2):
            fw.op(DVE, (lambda hlf: lambda e: e.bn_stats(out=stats[0:n, hlf, :], in_=r_ap[0:n, hlf * 512:(hlf + 1) * 512]))(hlf),
                  reads=r_bufs, writes=[B_stats])
        fw.op(DVE, lambda e: e.bn_aggr(out=mv[0:n, 0:2], in_=stats[0:n].rearrange("p a b -> p (a b)")), reads=[B_stats], writes=[B_mv])
        fw.op(ACT, lambda e: e.activation(out=mv[0:n, 2:3], in_=mv[0:n, 1:2], func=AF.Ln, bias=epsc[0:n, 0:1], scale=1.0),
              reads=[B_mv, B_const], writes=[B_mv])
        fw.op(ACT, lambda e: e.activation(out=mv[0:n, 3:4], in_=mv[0:n, 2:3], func=AF.Exp, scale=-0.5),
              reads=[B_mv], writes=[B_mv])
        fw.op(DVE, lambda e: e.scalar_tensor_tensor(out=mv[0:n, 4:5], in0=mv[0:n, 0:1], scalar=-1.0, in1=mv[0:n, 3:4],
                                                    op0=ALU.mult, op1=ALU.mult), reads=[B_mv], writes=[B_mv])
        fw.op(ACT, lambda e: e.activation(out=r_ap[0:n, :], in_=r_ap[0:n, :], func=AF.Identity, bias=mv[0:n, 4:5], scale=mv[0:n, 3:4]),
              reads=r_bufs + [B_mv], writes=r_bufs)
        fw.op(DVE, lambda e: e.tensor_tensor(out=r_ap[0:n, :], in0=r_ap[0:n, :], in1=g_ap[0:n, :], op=ALU.mult),
              reads=r_bufs + pbufs, writes=r_bufs)
        fw.op(POOL, lambda e: e.tensor_tensor(out=r_ap[0:n, :], in0=r_ap[0:n, :], in1=b_ap[0:n, :], op=ALU.add),
              reads=r_bufs + pbufs, writes=r_bufs)

    def load_lnp(g_src, b_src):
        dma1(SP, arf(8, 1024), g_src.partition_broadcast(128), AB[8], writes=[AB[8], AB[9]])
        dma1(SP, arf(10, 1024), b_src.partition_broadcast(128), AB[10], writes=[AB[10], AB[11]])

    load_lnp(ln_in_g, ln_in_b)
    rows = [(xp[i * 128:(i + 1) * 128, :], xcur_p[i * 128:(i + 1) * 128, :], 128, B_xcur_p[i]) for i in range(SEQ // 128)]
    if with_sample:
        rows.append((xs_in, xcur_s, 64, B_xcur_s))
    for i, (src, dst, n, bdst) in enumerate(rows):
        sl = 2 * (i % 4)
        r_ap = arf(sl, 1024)
        rb = [AB[sl], AB[sl + 1]]
        dma1(SP, r_ap[0:n, :], src, AB[sl], writes=rb)
        ln_rows(r_ap, rb, n, arf(8, 1024), arf(10, 1024), [AB[8], AB[9], AB[10], AB[11]])
        dma1(SP, dst, r_ap[0:n, :], AB[sl], reads=rb, writes=[bdst])

    pp = {"i": 0}

    def next_ps():
        pp["i"] += 1
        return (pA, B_pA) if pp["i"] % 2 else (pB, B_pB)

    def mm_acc(out_ap, out_buf, pairs, reads):
        n = len(pairs)
        for i, (lt, rh) in enumerate(pairs):
            fw.op(PE, (lambda lt, rh, i: lambda e: e.matmul(out_ap, lhsT=lt, rhs=rh, start=(i == 0), stop=(i == n - 1)))(lt, rh, i),
                  reads=reads, writes=[out_buf], sig=(i == n - 1))

    def sig_gate(ps_ap, ps_buf, nt, gt, gt_buf, silu):
        fw.op(ACT, lambda e: e.activation(out=gt[:, 0:nt], in_=ps_ap, func=AF.Tanh, scale=0.5), reads=[ps_buf], writes=[gt_buf])
        fw.op(DVE, lambda e: e.tensor_scalar(out=gt[:, 0:nt], in0=gt[:, 0:nt], scalar1=0.5, scalar2=0.5, op0=ALU.mult, op1=ALU.add),
              reads=[gt_buf], writes=[gt_buf])
        if silu:
            fw.op(DVE, lambda e: e.tensor_tensor(out=gt[:, 0:nt], in0=ps_ap, in1=gt[:, 0:nt], op=ALU.mult),
                  reads=[ps_buf, gt_buf], writes=[gt_buf])

    def attn_epilogue1(W_):
        rinv, t1 = arf(2, 2 * W_), arf(3, 2 * W_)
        fw.op(ACT, lambda e: e.activation(out=rinv, in_=pB[:, 0:2 * W_], func=AF.Ln), reads=[B_pB], writes=[AB[2]])
        fw.op(ACT, lambda e: e.activation(out=rinv, in_=rinv, func=AF.Exp, scale=-1.0), reads=[AB[2]], writes=[AB[2]])
        fw.op(DVE, lambda e: e.tensor_tensor(out=t1, in0=pO[:, 0:2 * W_], in1=rinv, op=ALU.mult), reads=[B_pO, AB[2]], writes=[AB[3]])

    def attn_epilogue2(l, W_):
        t1 = arf(3, 2 * W_)
        o, sq = arf(4, W_), arf(4, W_, 256)
        rs, rstd = arf(5, W_), arf(5, W_, 256)
        fw.op(DVE, lambda e: e.scalar_tensor_tensor(out=o, in0=t1[:, W_:2 * W_], scalar=neglam_all[:, l:l + 1], in1=t1[:, 0:W_],
                                                    op0=ALU.mult, op1=ALU.add), reads=[AB[3], B_lay], writes=[AB[4]])
        fw.op(ACT, lambda e: e.activation(out=sq, in_=o, func=AF.Square), reads=[AB[4]], writes=[AB[4]])
        fw.op(PE, lambda e: e.matmul(pA[:, 0:W_], lhsT=ones_f, rhs=sq, start=True, stop=True), reads=[AB[4], B_const], writes=[B_pA])
        fw.op(ACT, lambda e: e.activation(out=rs, in_=pA[:, 0:W_], func=AF.Ln, bias=epsc[:, 1:2], scale=1.0 / 128),
              reads=[B_pA, B_const], writes=[AB[5]])
        fw.op(ACT, lambda e: e.activation(out=rstd, in_=rs, func=AF.Exp, scale=-0.5), reads=[AB[5]], writes=[AB[5]])
        fw.op(DVE, lambda e: e.tensor_tensor(out=o, in0=o, in1=rstd, op=ALU.mult), reads=[AB[4], AB[5]], writes=[AB[4]])
        return o

    def tile_pass(l, ti, sample):
        last_layer = (l == L - 1)
        if sample:
            ntok, subs = 64, [(0, 64)]
        else:
            ntok, subs = TT, [(s, 128) for s in range(4)]
        nsub = len(subs)
        tok0 = 0 if sample else ti * TT
        xsrc = xcur_s if sample else xcur_p
        ydst = (y_s if sample else y_p) if last_layer else xsrc

        def xbuf(s):
            return B_xcur_s if sample else B_xcur_p[ti * 4 + s]

        wq = []
        nxt = {"k": 0}

        def wtop():
            while wfree and nxt["k"] < NCH:
                sl_ = wfree.pop(0)
                wload(l, nxt["k"], sl_)
                wq.append(sl_)
                nxt["k"] += 1

        def wnext():
            wtop()
            return wq.pop(0)

        def wrel(*slots):
            for sl_ in slots:
                wfree.append(sl_)
            wtop()

        wtop()

        if sample:
            for b in range(NSEQ_S):
                for jt in range(8):
                    i = b * 8 + jt
                    sl = i % 4
                    kst = arb(sl, 1024)
                    fw.dma(POOL, [(lambda b, jt, kst: lambda e: e.dma_start(out=kst, in_=ck[l, b, jt * 128:(jt + 1) * 128, :]))(b, jt, kst)],
                           AB[sl], writes=[AB[sl]])
                    fw.dma(POOL, [(lambda b, jt: lambda e: e.dma_start(out=VX[:, b * 8 + jt, :, 0:128],
                                                                         in_=cv[l, b, jt * 128:(jt + 1) * 128, :].rearrange("p (h v) -> p h v", h=8)))(b, jt)],
                           B_VX, writes=[B_VX])
                    for h in range(8):
                        fw.op(PE, (lambda h, kst: lambda e: e.transpose(out=pT[:, h * 128:(h + 1) * 128], in_=kst[:, h * 128:(h + 1) * 128],
                                                                        identity=ident))(h, kst),
                              reads=[AB[sl], B_const], writes=[B_pT], sig=(h == 7))
                    E = ACT if i % 2 == 0 else DVE
                    cp(E, KT[:, :, b * PAST + jt * 128: b * PAST + (jt + 1) * 128], pT.rearrange("p (h k) -> p h k", h=8),
                       reads=[B_pT], writes=[B_KT])

        for s, n in subs:
            sl = 2 * (s % 2)
            xin = arf(sl, 1024)
            xbf = arb(4 + (s % 2), 1024)
            r0 = tok0 + s * 128
            dma1(SP, xin[0:n, :], xsrc[r0:r0 + n, :], AB[sl], reads=[xbuf(s)], writes=[AB[sl], AB[sl + 1]])
            cp(ACT, xbf[0:n, :], xin[0:n, :], reads=[AB[sl], AB[sl + 1]], writes=[AB[4 + s % 2]])
            for c in range(8):
                fw.op(PE, (lambda c, xbf, n: lambda e: e.transpose(out=pT[:, c * 128:c * 128 + n], in_=xbf[0:n, c * 128:(c + 1) * 128],
                                                                  identity=ident[0:n, 0:n]))(c, xbf, n),
                      reads=[AB[4 + s % 2], B_const], writes=[B_pT], sig=(c == 7))
            cp(DVE, xT[:, :, s * 128:s * 128 + n], pT.rearrange("p (c k) -> p c k", c=8)[:, :, 0:n], reads=[B_pT], writes=[B_xT])
        if sample:
            dma1(SP, cs_sb[0:64, 0, :], cs_s, B_cs, writes=[B_cs])
        else:
            dma1(SP, cs_sb, cs_p[tok0:tok0 + TT, :].rearrange("(s p) c -> p s c", p=128), B_cs, writes=[B_cs])

        if _STOP == 1:
            wfree[:] = [0, 1]
            return
        Wd = 128 if sample else 528
        pxT = arena[:, 0:4 * Wd].rearrange("p (g w) -> p g w", g=4)
        pxb = [AB[0], AB[1], AB[2], AB[3], AB[4]] if not sample else [AB[0]]
        pooled = arb(6, 4 * TT).rearrange("p (g t) -> p g t", g=4)
        pooledb = [AB[6], AB[7]]
        yaT = arb(10, 4 * TT).rearrange("p (g t) -> p g t", g=4)
        yab = [AB[10], AB[11]]
        gt0, gt1 = arf(8, 512), arf(9, 512)

        def outview(ap2d):
            if sample:
                return ap2d.rearrange("p (b w) -> p b w", b=4)[:, :, 16:32]
            return ap2d[:, 16:528]

        def as_out(ap2d):
            if sample:
                return ap2d.rearrange("p (b t) -> p b t", b=4)
            return ap2d

        if sample:
            fw.op(DVE, lambda e: e.memset(arena[:, 0:4 * Wd], 0.0), writes=pxb)
            sp_sb = arf(2, 512)
            dma1(SP, sp_sb[0:60, :], spool[l], AB[2], writes=[AB[2]])
            for g in range(4):
                fw.op(PE, (lambda g: lambda e: e.transpose(out=pA[:, g * 64:g * 64 + 60], in_=sp_sb[0:60, g * 128:(g + 1) * 128],
                                                           identity=identf[0:60, 0:60]))(g),
                      reads=[AB[2], B_const], writes=[B_pA], sig=(g == 3))
            for g in range(4):
                cp(ACT, pxT[:, g, :].rearrange("p (b w) -> p b w", b=4)[:, :, 1:16],
                   pA[:, g * 64:g * 64 + 60].rearrange("p (b j) -> p b j", b=4), reads=[B_pA], writes=pxb)
        elif ti == 0:
            fw.op(DVE, lambda e: e.memset(pxT[:, :, 0:16], 0.0), writes=pxb)
        else:
            cp(DVE, pxT[:, :, 0:16], hist, reads=[B_hist], writes=pxb)

        slot = wnext()
        Wc = WS[slot]
        for e_ in range(4):
            ps, psb = next_ps()
            mm_acc(ps[:, 0:ntok], psb, [(Wc[:, d, e_ * 128:(e_ + 1) * 128], xT[:, d, 0:ntok]) for d in range(8)], [B_WS[slot], B_xT])
            cp(ACT, outview(pxT[:, e_, :]), as_out(ps[:, 0:ntok]), reads=[psb], writes=pxb)
        if sample or ti == NT - 1:
            s_last, n_last = subs[-1]
            ps, psb = next_ps()
            mm_acc(ps[0:n_last, :], psb, [(xT[:, d, s_last * 128:s_last * 128 + n_last], Wc[:, d, :]) for d in range(8)], [B_WS[slot], B_xT])
            cp(ACT, gt1[0:n_last, :], ps[0:n_last, :], reads=[psb], writes=[AB[9]])
            if sample:
                fw.dma(SP, [(lambda b: lambda e: e.dma_start(out=pool_s[l, 15 * b:15 * b + 15, :], in_=gt1[16 * b + 1:16 * b + 16, :]))(b)
                            for b in range(4)], AB[9], reads=[AB[9]], is_out=True)
            else:
                dma1(SP, pool_p[l], gt1[113:128, :], AB[9], reads=[AB[9]], is_out=True)
        wrel(slot)
        if not sample:
            cp(POOL, hist, pxT[:, :, 512:528], reads=pxb, writes=[B_hist])
        for g in range(4):
            u = pxT[:, g, :]
            w = 2 ** (g + 1)

            def add(out, a, b_, rd, wr):
                fw.op(POOL, lambda e: e.tensor_tensor(out=out, in0=a, in1=b_, op=ALU.add), reads=rd, writes=wr)

            if g == 0:
                add(tmpA[:, 16:Wd], u[:, 16:Wd], u[:, 15:Wd - 1], pxb, [B_tmpA])
                s_ap, s_b = tmpA, B_tmpA
            elif g == 1:
                add(tmpA[:, 14:Wd], u[:, 14:Wd], u[:, 13:Wd - 1], pxb, [B_tmpA])
                add(tmpB[:, 16:Wd], tmpA[:, 16:Wd], tmpA[:, 14:Wd - 2], [B_tmpA], [B_tmpB])
                s_ap, s_b = tmpB, B_tmpB
            elif g == 2:
                add(tmpA[:, 10:Wd], u[:, 10:Wd], u[:, 9:Wd - 1], pxb, [B_tmpA])
                add(tmpB[:, 12:Wd], tmpA[:, 12:Wd], tmpA[:, 10:Wd - 2], [B_tmpA], [B_tmpB])
                add(tmpA[:, 16:Wd], tmpB[:, 16:Wd], tmpB[:, 12:Wd - 4], [B_tmpB], [B_tmpA])
                s_ap, s_b = tmpA, B_tmpA
            else:
                add(tmpA[:, 2:Wd], u[:, 2:Wd], u[:, 1:Wd - 1], pxb, [B_tmpA])
                add(tmpB[:, 4:Wd], tmpA[:, 4:Wd], tmpA[:, 2:Wd - 2], [B_tmpA], [B_tmpB])
                add(tmpA[:, 8:Wd], tmpB[:, 8:Wd], tmpB[:, 4:Wd - 4], [B_tmpB], [B_tmpA])
                add(tmpB[:, 16:Wd], tmpA[:, 16:Wd], tmpA[:, 8:Wd - 8], [B_tmpA], [B_tmpB])
                s_ap, s_b = tmpB, B_tmpB
            fw.op(DVE, (lambda g, s_ap, u, w: lambda e: e.scalar_tensor_tensor(out=as_out(pooled[:, g, 0:ntok]), in0=outview(s_ap[:, 0:Wd]),
                                                                                scalar=1.0 / w, in1=outview(u), op0=ALU.mult,
                                                                                op1=ALU.subtract))(g, s_ap, u, w),
                  reads=[s_b] + pxb, writes=pooledb)
            if (not sample) and ti == 0:
                fw.op(DVE, (lambda g, s_ap: lambda e: e.tensor_tensor(out=gt0[:, 0:16], in0=s_ap[:, 16:32], in1=invc_sb[:, g, :],
                                                                       op=ALU.mult))(g, s_ap), reads=[s_b, B_const], writes=[AB[8]])
                fw.op(DVE, (lambda g, u: lambda e: e.tensor_tensor(out=pooled[:, g, 0:16], in0=gt0[:, 0:16], in1=u[:, 16:32],
                                                                    op=ALU.subtract))(g, u), reads=[AB[8]] + pxb, writes=pooledb)
        slot = wnext()
        Wc = WS[slot]
        for e_ in range(4):
            mm_acc(pA[:, 0:ntok], B_pA, [(Wc[:, d, e_ * 128:(e_ + 1) * 128], xT[:, d, 0:ntok]) for d in range(8)], [B_WS[slot], B_xT])
            mm_acc(pB[:, 0:ntok], B_pB, [(wpool_sb[:, e_, :], pooled[:, e_, 0:ntok])], [B_wpool] + pooledb)
            sig_gate(pA[:, 0:ntok], B_pA, ntok, gt0, AB[8], silu=True)
            fw.op(DVE, (lambda e_: lambda e: e.scalar_tensor_tensor(out=yaT[:, e_, 0:ntok], in0=pB[:, 0:ntok],
                                                                     scalar=pscale_all[:, l, e_:e_ + 1], in1=gt0[:, 0:ntok],
                                                                     op0=ALU.mult, op1=ALU.mult))(e_),
                  reads=[B_pB, AB[8], B_lay], writes=yab)
        wrel(slot)

        if _STOP == 2:
            wfree[:] = [0, 1]
            return
        for c in range(6):
            slot = wnext()
            Wc = WS[slot]
            kind = c // 2
            if str(kind) not in _os.environ.get('KP2', '012'):
                wrel(slot)
                continue
            h0 = (c % 2) * 4
            col0 = h0 * 128
            for s, n in subs:
                i2 = (c * nsub + s) % 2
                ps, psb = next_ps()
                mm_acc(ps[0:n, :], psb, [(xT[:, d, s * 128:s * 128 + n], Wc[:, d, :]) for d in range(8)], [B_WS[slot], B_xT])
                r0 = tok0 + s * 128
                if kind == 2:
                    vst = arf(2 + i2, 512)
                    cp(ACT, vst[0:n, :], ps[0:n, :], reads=[psb], writes=[AB[2 + i2]])
                    dma1(SP, (v_s if sample else v_p)[l, r0:r0 + n, col0:col0 + 512], vst[0:n, :], AB[2 + i2], reads=[AB[2 + i2]], is_out=True)
                    if sample:
                        cp(DVE, vbf_s[0:64, col0:col0 + 512], vst[0:64, :], reads=[AB[2 + i2]], writes=[B_tmpA])
                    else:
                        cp(DVE, VX[0:n, ti * 4 + s, h0:h0 + 4, 0:128], vst[0:n, :].rearrange("p (h v) -> p h v", h=4), reads=[AB[2 + i2]],
                           writes=[B_VX])
                    continue
                ra = arf(4 + i2, 512) if kind == 0 else arf(0 + i2, 512)
                rab = AB[4 + i2] if kind == 0 else AB[0 + i2]
                rb_ = arf(6 + i2, 512)
                rbb = AB[6 + i2]
                ps3 = ps[0:n, :].rearrange("p (g x) -> p g x", g=8)
                ra3 = ra[0:n, :].rearrange("p (g x) -> p g x", g=8)
                rb3 = rb_[0:n, :].rearrange("p (g x) -> p g x", g=8)
                cosb = cs_sb[0:n, s, 0:64].unsqueeze(1).to_broadcast([n, 8, 64])
                sin1 = cs_sb[0:n, s, 64:96].unsqueeze(1).to_broadcast([n, 8, 32])
                sin2 = cs_sb[0:n, s, 96:128].unsqueeze(1).to_broadcast([n, 8, 32])
                fw.op(DVE, (lambda ra3, ps3, cosb: lambda e: e.tensor_tensor(out=ra3, in0=ps3, in1=cosb, op=ALU.mult))(ra3, ps3, cosb),
                      reads=[psb, B_cs], writes=[rab])
                fw.op(DVE, (lambda rb3, ps3, sin1: lambda e: e.tensor_tensor(out=rb3[:, :, 0:32], in0=ps3[:, :, 32:64], in1=sin1,
                                                                              op=ALU.mult))(rb3, ps3, sin1),
                      reads=[psb, B_cs], writes=[rbb])
                fw.op(DVE, (lambda rb3, ps3, sin2: lambda e: e.tensor_tensor(out=rb3[:, :, 32:64], in0=ps3[:, :, 0:32], in1=sin2,
                                                                              op=ALU.mult))(rb3, ps3, sin2),
                      reads=[psb, B_cs], writes=[rbb])
                bfv = arb(8 + i2, 512, 512 * kind)
                bfb = AB[8 + i2]
                if kind == 0:
                    fw.op(POOL, (lambda bfv, ra, rb_, n: lambda e: e.tensor_tensor(out=bfv[0:n, :], in0=ra[0:n, :], in1=rb_[0:n, :],
                                                                                   op=ALU.add))(bfv, ra, rb_, n),
                          reads=[rab, rbb], writes=[bfb])
                else:
                    fw.op(POOL, (lambda ra, rb_, n: lambda e: e.tensor_tensor(out=ra[0:n, :], in0=ra[0:n, :], in1=rb_[0:n, :],
                                                                              op=ALU.add))(ra, rb_, n),
                          reads=[rab, rbb], writes=[rab])
                    dma1(SP, (k_s if sample else k_p)[l, r0:r0 + n, col0:col0 + 512], ra[0:n, :], rab, reads=[rab], is_out=True)
                    cp(ACT, bfv[0:n, :], ra[0:n, :], reads=[rab], writes=[bfb])
                for hh in range(4):
                    fw.op(PE, (lambda hh, bfv, n: lambda e: e.transpose(out=pT[:, hh * 128:hh * 128 + n], in_=bfv[0:n, hh * 128:(hh + 1) * 128],
                                                                       identity=ident[0:n, 0:n]))(hh, bfv, n),
                          reads=[bfb, B_const], writes=[B_pT], sig=(hh == 3))
                src = pT[:, 0:512].rearrange("p (h k) -> p h k", h=4)[:, :, 0:n]
                if kind == 0:
                    cp(ACT, QT[:, h0:h0 + 4, s * 128:s * 128 + n], src, reads=[B_pT], writes=[B_QT])
                elif sample:
                    cp(DVE, KTn[:, h0:h0 + 4, 0:64], src, reads=[B_pT], writes=[B_KTn])
                else:
                    cp(DVE, KT[:, h0:h0 + 4, r0:r0 + n], src, reads=[B_pT], writes=[B_KT])
            wrel(slot)
        if _STOP == 3:
            wfree[:] = [0, 1]
            return
        ptc = {"i": 0}

        def next_pt():
            ptc["i"] += 1
            k = ptc["i"] % 4
            return arb(k // 2, 512, 512 * (k % 2)), AB[k // 2]

        sublnc = subln_all[:, l:l + 1]
        if not sample:
            pending = None
            for h in range(HEADS):
                for qbl in range(2):
                    qb = 2 * ti + qbl
                    jmax = 2 * qb + 1
                    qc0 = qbl * 256
                    npairs = qb + 1
                    pts = {}
                    for step in range(npairs + 1):
                        if step < npairs:
                            jp = step
                            pS, pSb = (pSX, B_pSX) if jp % 2 == 0 else (pSY, B_pSY)
                            for jl in range(2):
                                j = 2 * jp + jl
                                for m in range(2):
                                    fw.op(PE, (lambda pS, m, jl, j, h, qc0: lambda e: e.matmul(
                                        pS[:, m, jl * 256:(jl + 1) * 256], lhsT=KT[m * 64:(m + 1) * 64, h, j * 128:(j + 1) * 128],
                                        rhs=QT[m * 64:(m + 1) * 64, h, qc0:qc0 + 256], start=True, stop=True))(pS, m, jl, j, h, qc0),
                                          reads=[B_KT, B_QT], writes=[pSb], sig=(jl == 1 and m == 1))
                            for jl in range(2):
                                j = 2 * jp + jl
                                pt, ptb = next_pt()
                                pts[j] = (pt, ptb)
                                pt3 = pt.rearrange("p (m q) -> p m q", m=2)
                                fw.op(ACT, (lambda pt3, pS, jl: lambda e: e.activation(out=pt3, in_=pS[:, :, jl * 256:(jl + 1) * 256],
                                                                                      func=AF.Exp, scale=0.125))(pt3, pS, jl),
                                      reads=[pSb], writes=[ptb])
                                if jp == qb:
                                    if jl == 0:
                                        fw.op(POOL, (lambda pt3: lambda e: e.memset(pt3[64:128, :, 0:64], 0.0))(pt3), writes=[ptb])
                                    else:
                                        fw.op(POOL, (lambda pt3: lambda e: e.memset(pt3[0:64, :, 0:128], 0.0))(pt3), writes=[ptb])
                                        fw.op(POOL, (lambda pt3: lambda e: e.memset(pt3[64:128, :, 0:192], 0.0))(pt3), writes=[ptb])
                        if step >= 1:
                            for jl in range(2):
                                j = 2 * (step - 1) + jl
                                pt, ptb = pts.pop(j)
                                fw.op(PE, (lambda pt, j, h, jmax: lambda e: e.matmul(pO, lhsT=VX[:, j, h, 0:128], rhs=pt, start=(j == 0),
                                                                                    stop=(j == jmax)))(pt, j, h, jmax),
                                      reads=[B_VX, ptb], writes=[B_pO], sig=False)
                                fw.op(PE, (lambda pt, j, jmax: lambda e: e.matmul(pB, lhsT=ones_bf, rhs=pt, start=(j == 0),
                                                                                 stop=(j == jmax)))(pt, j, jmax),
                                      reads=[B_const, ptb], writes=[B_pB], sig=True)
                        if step == 1 and pending is not None:
                            pending()
                            pending = None
                    attn_epilogue1(256)

                    def _p2(h=h, qc0=qc0):
                        o = attn_epilogue2(l, 256)
                        fw.op(ACT, lambda e: e.activation(out=ybT[:, h, qc0:qc0 + 256], in_=o, func=AF.Copy, scale=sublnc),
                              reads=[AB[4], B_lay], writes=[B_ybT])
                    pending = _p2
            if pending is not None:
                pending()
        else:
            for b in range(NSEQ_S):
                fw.op(PE, lambda e: e.matmul(pO[:, 0:256], lhsT=zeros_bf[:, 0:128], rhs=zeros_bf[:, 0:256], start=True, stop=True),
                      reads=[B_const], writes=[B_pO])
                fw.op(PE, lambda e: e.matmul(pB[:, 0:256], lhsT=zeros_bf[:, 0:128], rhs=zeros_bf[:, 0:256], start=True, stop=True),
                      reads=[B_const], writes=[B_pB])
                for grp in range(3):
                    pS, pSb = (pSX, B_pSX) if grp % 2 == 0 else (pSY, B_pSY)
                    njl = 4 if grp < 2 else 1
                    kp = 128 if grp < 2 else TS
                    for jl in range(njl):
                        jt = grp * 4 + jl
                        for h in range(HEADS):
                            for m in range(2):
                                if grp < 2:
                                    lt = KT[m * 64:(m + 1) * 64, h, b * PAST + jt * 128: b * PAST + (jt + 1) * 128]
                                else:
                                    lt = KTn[m * 64:(m + 1) * 64, h, b * TS:(b + 1) * TS]
                                fw.op(PE, (lambda pS, m, jl, h, lt, b, kp: lambda e: e.matmul(
                                    pS[0:kp, m, jl * 128 + h * TS: jl * 128 + (h + 1) * TS], lhsT=lt,
                                    rhs=QT[m * 64:(m + 1) * 64, h, b * TS:(b + 1) * TS], start=True, stop=True))(pS, m, jl, h, lt, b, kp),
                                      reads=[B_KT, B_KTn, B_QT], writes=[pSb], sig=(h == 7 and m == 1 and jl == njl - 1))
                    if grp < 2:
                        pt, ptb = arb(0, 1024), AB[0]
                        if grp == 1:
                            pt, ptb = arb(1, 1024), AB[1]
                        pt4 = pt.rearrange("p (m x) -> p m x", m=2)
                        fw.op(ACT, (lambda pt4, pS: lambda e: e.activation(out=pt4, in_=pS, func=AF.Exp, scale=0.125))(pt4, pS),
                              reads=[pSb], writes=[ptb])
                        for jl in range(4):
                            jt = grp * 4 + jl
                            for m in range(2):
                                fw.op(PE, (lambda pt4, m, jl: lambda e: e.matmul(pB[:, m * 128:(m + 1) * 128], lhsT=ones_bf,
                                                                                rhs=pt4[:, m, jl * 128:(jl + 1) * 128], start=False,
                                                                                stop=False, skip_group_check=True))(pt4, m, jl),
                                      reads=[ptb, B_const], writes=[B_pB], sig=False)
                                for h in range(HEADS):
                                    fw.op(PE, (lambda pt4, m, jl, h, jt, b: lambda e: e.matmul(
                                        pO[:, m * 128 + h * TS: m * 128 + (h + 1) * TS], lhsT=VX[:, b * 8 + jt, h, 0:128],
                                        rhs=pt4[:, m, jl * 128 + h * TS: jl * 128 + (h + 1) * TS], start=False, stop=False,
                                        skip_group_check=True))(pt4, m, jl, h, jt, b),
                                          reads=[ptb, B_VX], writes=[B_pO], sig=(m == 1 and h == 7))
                    else:
                        dma1(SP, VN[0:TS, :, :], vbf_s[TS * b:TS * b + TS, :].rearrange("p (h v) -> p h v", h=8), B_VN,
                             reads=[B_tmpA], writes=[B_VN])
                        fw.op(ACT, (lambda pS: lambda e: e.activation(out=PTn[0:TS].rearrange("p m h q -> p m (h q)"), in_=pS[0:TS, :, 0:128],
                                                                       func=AF.Exp, scale=0.125))(pS),
                              reads=[pSb], writes=[B_PTn])
                        for m in range(2):
                            fw.op(PE, (lambda m: lambda e: e.matmul(pB[:, m * 128:(m + 1) * 128], lhsT=ones_bf,
                                                                    rhs=PTn[:, m].rearrange("p h q -> p (h q)"), start=False, stop=False,
                                                                    skip_group_check=True))(m),
                                  reads=[B_PTn, B_const], writes=[B_pB], sig=False)
                            for h in range(HEADS):
                                fw.op(PE, (lambda m, h, b: lambda e: e.matmul(pO[:, m * 128 + h * TS: m * 128 + (h + 1) * TS], lhsT=VN[:, h, :],
                                                                              rhs=PTn[:, m, h, :], start=False, stop=True,
                                                                              skip_group_check=True))(m, h, b),
                                      reads=[B_PTn, B_VN], writes=[B_pO, B_pB], sig=(m == 1 and h == 7))
                attn_epilogue1(128)
                o = attn_epilogue2(l, 128)
                fw.op(ACT, (lambda b, o: lambda e: e.activation(out=ybT[:, :, b * TS:(b + 1) * TS], in_=o.rearrange("p (h q) -> p h q", h=8),
                                                                 func=AF.Copy, scale=sublnc))(b, o),
                      reads=[AB[4], B_lay], writes=[B_ybT])

        if _STOP == 4:
            wfree[:] = [0, 1]
            return
        gt0, gt1 = arf(8, 512), arf(9, 512)
        for c in range(2):
            slot = wnext()
            Wc = WS[slot]
            for e_ in range(4):
                h = 4 * c + e_
                ps, psb = next_ps()
                mm_acc(ps[:, 0:ntok], psb, [(Wc[:, d, e_ * 128:(e_ + 1) * 128], xT[:, d, 0:ntok]) for d in range(8)], [B_WS[slot], B_xT])
                sig_gate(ps[:, 0:ntok], psb, ntok, gt0, AB[8], silu=True)
                fw.op(DVE, (lambda h: lambda e: e.tensor_tensor(out=ybT[:, h, 0:ntok], in0=ybT[:, h, 0:ntok], in1=gt0[:, 0:ntok],
                                                                 op=ALU.mult))(h), reads=[B_ybT, AB[8]], writes=[B_ybT])
            wrel(slot)

        if _STOP == 5:
            wfree[:] = [0, 1]
            return
        m1 = arena[:, 0:8 * 512].rearrange("p (c t) -> p c t", c=8)
        m1b = AB[0:8]
        slot_a = wnext()
        WA = WS[slot_a].rearrange("p k c -> p (k c)").rearrange("p (g c) -> p g c", g=4)
        for half in range(2):
            slot = wnext()
            Wc = WS[slot]
            for dcl in range(4):
                dc = 4 * half + dcl
                mm_acc(pA[:, 0:ntok], B_pA, [(WA[:, g, dc * 128:(dc + 1) * 128], yaT[:, g, 0:ntok]) for g in range(4)], [B_WS[slot_a]] + yab)
                mm_acc(pB[:, 0:ntok], B_pB, [(Wc[:, d, dcl * 128:(dcl + 1) * 128], xT[:, d, 0:ntok]) for d in range(8)], [B_WS[slot], B_xT])
                sig_gate(pB[:, 0:ntok], B_pB, ntok, gt0, AB[8], silu=False)
                fw.op(DVE, (lambda dc: lambda e: e.tensor_tensor(out=m1[:, dc, 0:ntok], in0=pA[:, 0:ntok], in1=gt0[:, 0:ntok],
                                                                  op=ALU.mult))(dc), reads=[B_pA, AB[8]], writes=[m1b[dc]])
            if half == 0:
                wrel(slot)
        wrel(slot_a, slot)
        mergedT = QT
        for half in range(2):
            slot_b = wnext()
            WB = WS[slot_b]
            slot = wnext()
            Wc = WS[slot]
            for dcl in range(4):
                dc = 4 * half + dcl
                mm_acc(pA[:, 0:ntok], B_pA, [(WB[:, h, dcl * 128:(dcl + 1) * 128], ybT[:, h, 0:ntok]) for h in range(8)], [B_WS[slot_b], B_ybT])
                mm_acc(pB[:, 0:ntok], B_pB, [(Wc[:, d, dcl * 128:(dcl + 1) * 128], xT[:, d, 0:ntok]) for d in range(8)], [B_WS[slot], B_xT])
                sig_gate(pB[:, 0:ntok], B_pB, ntok, gt0, AB[8], silu=False)
                fw.op(DVE, lambda e: e.tensor_tensor(out=gt1[:, 0:ntok], in0=pA[:, 0:ntok], in1=gt0[:, 0:ntok], op=ALU.mult),
                      reads=[B_pA, AB[8]], writes=[AB[9]])
                fw.op(POOL, (lambda dc: lambda e: e.tensor_tensor(out=mergedT[:, dc, 0:ntok], in0=m1[:, dc, 0:ntok], in1=gt1[:, 0:ntok],
                                                                   op=ALU.add))(dc), reads=[m1b[dc], AB[9]], writes=[B_QT])
            wrel(slot_b, slot)

        if _STOP == 6:
            wfree[:] = [0, 1]
            return
        slot0 = wnext()
        slot1 = wnext()
        load_lnp(ln_g[l:l + 1, :], ln_b[l:l + 1, :])
        for s, n in subs:
            i2 = s % 2
            xres = arf(2 * i2, 1024)
            xrb = [AB[2 * i2], AB[2 * i2 + 1]]
            r_ap = arf(4 + 2 * i2, 1024)
            rbufs = [AB[4 + 2 * i2], AB[5 + 2 * i2]]
            r0 = tok0 + s * 128
            dma1(SP, xres[0:n, :], xsrc[r0:r0 + n, :], AB[2 * i2], reads=[xbuf(s)], writes=xrb)
            for hf, (pp_, ppb, sl_) in enumerate([(pA, B_pA, slot0), (pB, B_pB, slot1)]):
                mm_acc(pp_[0:n, :], ppb, [(mergedT[:, dc, s * 128:s * 128 + n], WS[sl_][:, dc, :]) for dc in range(8)], [B_QT, B_WS[sl_]])
                fw.op(DVE, (lambda hf, pp_, n, r_ap, xres: lambda e: e.scalar_tensor_tensor(
                    out=r_ap[0:n, hf * 512:(hf + 1) * 512], in0=xres[0:n, hf * 512:(hf + 1) * 512], scalar=ALPHA, in1=pp_[0:n, :],
                    op0=ALU.mult, op1=ALU.add))(hf, pp_, n, r_ap, xres), reads=xrb + [ppb], writes=rbufs)
            ln_rows(r_ap, rbufs, n, arf(8, 1024), arf(10, 1024), [AB[8], AB[9], AB[10], AB[11]])
            dma1(SP, ydst[r0:r0 + n, :], r_ap[0:n, :], AB[4 + 2 * i2], reads=rbufs, writes=[xbuf(s)], is_out=last_layer)
        assert nxt["k"] == NCH and not wq, (nxt, wq)
        wfree.extend([slot0, slot1])

    for l in range(L):
        fw.dma(POOL, [(lambda l: lambda e: e.dma_start(out=wpool_sb, in_=w_pool[l].rearrange("g c d -> c g d")))(l)], B_wpool, writes=[B_wpool])
        if l + 1 < L:
            cast_weights(l + 1)
        if with_sample:
            fw.new_epoch()
            tile_pass(l, 0, True)
        for ti in range(NT):
            if ti % 3 == 0:
                fw.new_epoch()
            tile_pass(l, ti, False)

    fw.emit()
    return nc, fw


def _rope_table(pos):
    half = 32
    inv = (np.float32(10000.0) ** (-np.arange(half, dtype=np.float32) / np.float32(half))).astype(np.float32)
    ang = pos.astype(np.float32)[:, None] * inv[None, :]
    c = np.cos(ang).astype(np.float32)
    s = np.sin(ang).astype(np.float32)
    return np.ascontiguousarray(np.concatenate([c, c, -s, s], axis=1).astype(np.float32))


def make_in_maps(inputs, L, SEQ, n_cores=8, with_sample=True):
    f = lambda a: np.ascontiguousarray(np.asarray(a, dtype=np.float32))
    cs_p = _rope_table(np.arange(SEQ))
    cs_s = np.ascontiguousarray(np.tile(_rope_table(PAST + np.arange(TS)), (NSEQ_S, 1)))
    invc = np.zeros((4, 16), np.float32)
    for g in range(4):
        w = 2 ** (g + 1)
        invc[g] = 1.0 / np.minimum(np.arange(16) + 1, w)
    common = dict(
        ln_in_g=f(inputs["ln_in_g"]).reshape(1, D), ln_in_b=f(inputs["ln_in_b"]).reshape(1, D),
        w_in=f(inputs["w_in"])[:L], w_pool=f(inputs["w_pool"])[:L], pool_scale=f(inputs["pool_scale"])[:L],
        lambda_qk=f(inputs["lambda_qk"])[:L].reshape(L, 256), subln_w=f(inputs["subln_w"])[:L],
        w_a=f(inputs["w_a"])[:L], w_b=f(inputs["w_b"])[:L], w_o=f(inputs["w_o"])[:L],
        ln_g=f(inputs["ln_g"])[:L], ln_b=f(inputs["ln_b"])[:L],
        cs_p=cs_p, cs_s=cs_s, idn_bf=np.eye(128).astype(ml_dtypes.bfloat16), idn_f=np.eye(128, dtype=np.float32),
        invc=invc.reshape(1, 64),
    )
    xp = f(inputs["x_prompt"])
    xs = f(inputs["x_sample"])
    ck = np.asarray(inputs["cache_k"], dtype=np.float32)
    cv = np.asarray(inputs["cache_v"], dtype=np.float32)
    sp = np.asarray(inputs["state_pool"], dtype=np.float32)
    maps = []
    for c in range(n_cores):
        m = dict(common)
        m["xp"] = np.ascontiguousarray(xp[c, :SEQ])
        m["xs"] = np.ascontiguousarray(xs[4 * c:4 * c + 4].reshape(64, D))
        m["ck"] = np.ascontiguousarray(ck[:L, 4 * c:4 * c + 4].reshape(L, 4, PAST, D))
        m["cv"] = np.ascontiguousarray(cv[:L, 4 * c:4 * c + 4].reshape(L, 4, PAST, D))
        m["spool"] = np.ascontiguousarray(sp[:L, 4 * c:4 * c + 4].reshape(L, 60, 512))
        maps.append(m)
    return maps


def gather(results, L, SEQ, n_cores=8):
    y_p = np.stack([r["y_p"] for r in results]).reshape(n_cores, SEQ, D)
    y_s = np.stack([r["y_s"] for r in results]).reshape(n_cores * 4, TS, D)
    k_p = np.stack([r["k_p"] for r in results], axis=1).reshape(L, n_cores, SEQ, HEADS, 128)
    v_p = np.stack([r["v_p"] for r in results], axis=1).reshape(L, n_cores, SEQ, HEADS, 128)
    pool_p = np.stack([r["pool_p"] for r in results], axis=1).reshape(L, n_cores, 15, 512)
    k_s = np.stack([r["k_s"] for r in results], axis=1).reshape(L, n_cores * 4, TS, HEADS, 128)
    v_s = np.stack([r["v_s"] for r in results], axis=1).reshape(L, n_cores * 4, TS, HEADS, 128)
    pool_s = np.stack([r["pool_s"] for r in results], axis=1).reshape(L, n_cores * 4, 15, 512)
    return tuple(np.ascontiguousarray(a.astype(np.float32)) for a in (y_p, y_s, k_p, v_p, pool_p, k_s, v_s, pool_s))


_CACHE = {}


def kernel(**inputs):
    L, SEQ = DEPTH_FULL, 4096
    if "nc" not in _CACHE:
        _CACHE["nc"] = build_program(L, SEQ)[0]
    nc = _CACHE["nc"]
    maps = make_in_maps(inputs, L, SEQ)
    res = run_bass_kernel_spmd(nc, maps, core_ids=list(range(8)))
    return gather(res.results, L, SEQ)
```

```python
import math
import numpy as np
import ml_dtypes
import concourse.bass as bass
import concourse.mybir as mybir
from concourse.bass_utils import run_bass_kernel_spmd

F32 = mybir.dt.float32
BF16 = mybir.dt.bfloat16
AF = mybir.ActivationFunctionType
ALU = mybir.AluOpType

D = 1024
TT = 512
HEADS = 8
DEPTH_FULL = 4
ALPHA = (2 * DEPTH_FULL) ** 0.25
LN_EPS = 1e-5
RMS_EPS = 1e-5
PAST = 1024
NSEQ_S = 4
TS = 16
NCH = 19


class Buf:
    __slots__ = ("name", "w", "r", "dsem", "dcnt")

    def __init__(self, name):
        self.name = name
        self.w = None
        self.r = {}
        self.dsem = None
        self.dcnt = 0


class Eng:
    def __init__(self, fw, name, is_pe=False):
        self.fw = fw
        self.name = name
        self.is_pe = is_pe
        self.prog = []
        self.seen = {}
        self.sem = None
        self.cnt = 0
        self.epoch = 0
        self.new_epoch()

    def new_epoch(self):
        self.sem = self.fw.nc.alloc_semaphore(f"s_{self.name}_{self.epoch}")
        self.fw.nsem += 1
        self.epoch += 1
        self.cnt = 0


class FW:
    def __init__(self, nc):
        self.nc = nc
        self.nsem = 0
        self.pe = Eng(self, "pe", is_pe=True)
        self.act = Eng(self, "act")
        self.dve = Eng(self, "dve")
        self.pool = Eng(self, "pool")
        self.sp = Eng(self, "sp")
        self.out_events = []
        self.ninst = 0

    def new_epoch(self):
        for e in (self.pe, self.act, self.dve, self.pool):
            if e.cnt > 0:
                e.new_epoch()

    def _collect(self, E, reads, writes):
        waits = []

        def need(ev, same_ok):
            if ev is None:
                return
            sem, val = ev
            if sem is E.sem and same_ok:
                return
            k = id(sem)
            if E.seen.get(k, 0) >= val:
                return
            E.seen[k] = val
            waits.append((sem, val))

        for b in reads:
            need(b.w, E.is_pe)
        for b in writes:
            need(b.w, E.is_pe)
            for ev in b.r.values():
                need(ev, True)
        return waits

    def op(self, E, fn, reads=(), writes=(), sig=True):
        waits = self._collect(E, reads, writes)
        self.ninst += 1
        if sig:
            E.cnt += 1
            ev = (E.sem, E.cnt)
            E.prog.append((waits, fn, (E.sem, 1)))
        else:
            ev = (E.sem, E.cnt + 1)
            E.prog.append((waits, fn, None))
        for b in reads:
            b.r[id(E.sem)] = ev
        for b in writes:
            b.w = ev
            b.r = {}
        return ev

    def dma(self, Q, fns, sembuf, reads=(), writes=(), is_out=False):
        waits = self._collect(Q, reads, writes)
        if sembuf.dsem is None:
            sembuf.dsem = self.nc.alloc_semaphore(f"d_{sembuf.name}")
            self.nsem += 1
        for i, fn in enumerate(fns):
            sembuf.dcnt += 16
            self.ninst += 1
            Q.prog.append((waits if i == 0 else [], fn, (sembuf.dsem, 16)))
        ev = (sembuf.dsem, sembuf.dcnt)
        for b in reads:
            b.r[id(sembuf.dsem)] = ev
        for b in writes:
            b.w = ev
            b.r = {}
        if is_out:
            self.out_events.append(ev)
        return ev

    def emit(self):
        nc = self.nc
        last = {}
        for sem, val in self.out_events:
            k = id(sem)
            if k not in last or last[k][1] < val:
                last[k] = (sem, val)
        self.sp.prog.append((list(last.values()), None, None))

        def replay(E, eng):
            for waits, fn, inc in E.prog:
                for sem, val in waits:
                    eng.wait_ge(sem, val)
                if fn is None:
                    continue
                ins = fn(eng)
                if inc is not None:
                    ins.then_inc(inc[0], inc[1])

        with nc.Block() as block:
            @block.tensor
            def _(eng):
                replay(self.pe, eng)

            @block.scalar
            def _(eng):
                replay(self.act, eng)

            @block.vector
            def _(eng):
                replay(self.dve, eng)

            @block.gpsimd
            def _(eng):
                replay(self.pool, eng)

            @block.sync
            def _(eng):
                replay(self.sp, eng)


def build_program(L, SEQ, with_sample=True):
    import os as _os
    _STOP = int(_os.environ.get('KSTOP', '99'))
    NT = SEQ // TT
    KTW = max(SEQ, NSEQ_S * PAST)
    NKT = KTW // 128
    nc = bass.Bass("TRN2", target_bir_lowering=False)
    fw = FW(nc)
    PE, ACT, DVE, POOL, SP = fw.pe, fw.act, fw.dve, fw.pool, fw.sp

    def din(name, shape, dt=F32):
        return nc.dram_tensor(name, shape, dt, kind="ExternalInput").ap()

    def dout(name, shape, dt=F32):
        return nc.dram_tensor(name, shape, dt, kind="ExternalOutput").ap()

    xp = din("xp", [SEQ, D])
    xs_in = din("xs", [64, D])
    ck = din("ck", [L, NSEQ_S, PAST, D])
    cv = din("cv", [L, NSEQ_S, PAST, D])
    spool = din("spool", [L, 60, 512])
    ln_in_g = din("ln_in_g", [1, D])
    ln_in_b = din("ln_in_b", [1, D])
    w_in = din("w_in", [L, D, 7168])
    w_pool = din("w_pool", [L, 4, 128, 128])
    pool_scale = din("pool_scale", [L, 512])
    lambda_qk = din("lambda_qk", [L, 256])
    subln_w = din("subln_w", [L, 128])
    w_a = din("w_a", [L, 512, D])
    w_b = din("w_b", [L, D, D])
    w_o = din("w_o", [L, D, D])
    ln_g = din("ln_g", [L, D])
    ln_b = din("ln_b", [L, D])
    cs_p = din("cs_p", [SEQ, 128])
    cs_s = din("cs_s", [64, 128])
    idn_bf = din("idn_bf", [128, 128], BF16)
    idn_f = din("idn_f", [128, 128])
    invc = din("invc", [1, 64])

    y_p = dout("y_p", [SEQ, D])
    y_s = dout("y_s", [64, D])
    k_p = dout("k_p", [L, SEQ, D])
    v_p = dout("v_p", [L, SEQ, D])
    pool_p = dout("pool_p", [L, 15, 512])
    k_s = dout("k_s", [L, 64, D])
    v_s = dout("v_s", [L, 64, D])
    pool_s = dout("pool_s", [L, 60, 512])

    xcur_p = nc.dram_tensor("xcur_p", [SEQ, D], F32, kind="Internal").ap()
    xcur_s = nc.dram_tensor("xcur_s", [64, D], F32, kind="Internal").ap()
    wsc = nc.dram_tensor("wsc", [L, NCH, 128, 4096], BF16, kind="Internal").ap()
    B_wsc = [Buf(f"wsc{l}") for l in range(L)]
    B_xcur_p = [Buf(f"xcp{i}") for i in range(SEQ // 128)]
    B_xcur_s = Buf("xcs")

    def sb(name, shape, dt=F32):
        return nc.alloc_sbuf_tensor(name, shape, dt).ap()

    KT = sb("KT", [128, HEADS, KTW], BF16)
    VX = sb("VX", [128, NKT, HEADS, 128], BF16)
    xT = sb("xT", [128, 8, TT], BF16)
    QT = sb("QT", [128, 8, TT], BF16)
    ybT = sb("ybT", [128, 8, TT], BF16)
    WS = [sb(f"WS{i}", [128, 8, 512], BF16) for i in range(2)]
    arena = sb("arena", [128, 12 * 512], F32)
    arena_bf = arena.bitcast(BF16)
    tmpA = sb("tmpA", [128, 528], F32)
    tmpB = sb("tmpB", [128, 528], F32)
    hist = sb("hist", [128, 4, 16], F32)
    cs_sb = sb("cs_sb", [128, 4, 128], F32)
    ident = sb("ident", [128, 128], BF16)
    identf = sb("identf", [128, 128], F32)
    ones_bf = sb("ones_bf", [128, 128], BF16)
    ones_f = sb("ones_f", [128, 128], F32)
    zeros_bf = sb("zeros_bf", [128, 256], BF16)
    neghalf = sb("neghalf", [128, 2], F32)
    epsc = sb("epsc", [128, 2], F32)
    invc_sb = sb("invc_sb", [128, 4, 16], F32)
    wpool_sb = sb("wpool_sb", [128, 4, 128], BF16)
    pscale_all = sb("pscale_all", [128, L, 4], F32)
    subln_all = sb("subln_all", [128, L], F32)
    lq_all = arena[:, 0:L * 256].rearrange("p (l c) -> p l c", l=L)
    lam_tmp = sb("lam_tmp", [128, 4 * L + 8], F32)
    neglam_all = sb("neglam_all", [128, L], F32)
    stats = sb("stats", [128, 2, 6], F32)
    mv = sb("mv", [128, 8], F32)
    KTn = sb("KTn", [128, HEADS, 64], BF16)
    VN = sb("VN", [128, HEADS, 128], BF16)
    PTn = sb("PTn", [128, 2, HEADS, TS], BF16)
    vbf_s = tmpA.bitcast(BF16)[0:64, 0:D]

    B_KT, B_VX, B_xT, B_QT, B_ybT = Buf("KT"), Buf("VX"), Buf("xT"), Buf("QT"), Buf("ybT")
    B_WS = [Buf(f"WS{i}") for i in range(2)]
    AB = [Buf(f"ar{i}") for i in range(12)]
    B_tmpA, B_tmpB, B_hist, B_cs = Buf("tmpA"), Buf("tmpB"), Buf("hist"), Buf("cs")
    B_const = Buf("const")
    B_lay = Buf("laycst")
    B_stats, B_mv = Buf("stats"), Buf("mv")
    B_KTn, B_VN, B_PTn = Buf("KTn"), Buf("VN"), Buf("PTn")
    B_wpool = Buf("wpool")

    pA = nc.alloc_psum_tensor("pA", [128, 512], F32).ap()
    pB = nc.alloc_psum_tensor("pB", [128, 512], F32).ap()
    pT = nc.alloc_psum_tensor("pT", [128, 1024], BF16).ap()
    pO = nc.alloc_psum_tensor("pO", [128, 512], F32).ap()
    pSX = nc.alloc_psum_tensor("pSX", [128, 2, 512], F32).ap()
    pSY = nc.alloc_psum_tensor("pSY", [128, 2, 512], F32).ap()
    B_pA, B_pB, B_pT, B_pO, B_pSX, B_pSY = (Buf(n) for n in ["pA", "pB", "pT", "pO", "pSX", "pSY"])

    def arf(slot, ncols, off=0):
        return arena[:, slot * 512 + off: slot * 512 + off + ncols]

    def arb(slot, ncols, off=0):
        return arena_bf[:, slot * 1024 + off: slot * 1024 + off + ncols]

    def cp(E, out, in_, reads, writes):
        if E is ACT:
            return fw.op(E, lambda e: e.activation(out=out, in_=in_, func=AF.Copy), reads=reads, writes=writes)
        return fw.op(E, lambda e: e.tensor_copy(out=out, in_=in_), reads=reads, writes=writes)

    def dma1(Q, out, in_, sembuf, reads=(), writes=(), is_out=False, slow=False):
        if slow:
            f = lambda e: e.dma_start(out=out, in_=in_, allow_slow_non_contiguous=True)
        else:
            f = lambda e: e.dma_start(out=out, in_=in_)
        return fw.dma(Q, [f], sembuf, reads=reads, writes=writes, is_out=is_out)

    dma1(SP, ident, idn_bf, B_const, writes=[B_const])
    dma1(SP, identf, idn_f, B_const, writes=[B_const])
    dma1(SP, invc_sb.rearrange("p g t -> p (g t)"), invc.partition_broadcast(128), B_const, writes=[B_const])
    dma1(SP, arena[:, 0:L * 256], lambda_qk.rearrange("(o l) c -> o (l c)", o=1).partition_broadcast(128),
         AB[0], writes=[AB[0], AB[1]])
    dma1(SP, pscale_all, pool_scale.rearrange("l (g p) -> p l g", p=128), B_lay, writes=[B_lay], slow=True)
    dma1(SP, subln_all, subln_w.rearrange("l p -> p l"), B_lay, writes=[B_lay], slow=True)
    fw.op(DVE, lambda e: e.memset(ones_bf, 1.0), writes=[B_const])
    fw.op(DVE, lambda e: e.memset(ones_f, 1.0), writes=[B_const])
    fw.op(DVE, lambda e: e.memset(zeros_bf, 0.0), writes=[B_const])
    fw.op(DVE, lambda e: e.memset(neghalf, -0.5), writes=[B_const])
    fw.op(DVE, lambda e: e.memset(epsc[:, 0:1], LN_EPS), writes=[B_const])
    fw.op(DVE, lambda e: e.memset(epsc[:, 1:2], RMS_EPS), writes=[B_const])
    fw.op(DVE, lambda e: e.memset(VN, 0.0), writes=[B_VN])
    fw.op(DVE, lambda e: e.memset(PTn, 0.0), writes=[B_PTn])
    for l in range(L):
        lam_init = 0.8 - 0.6 * math.exp(-0.3 * l)
        for i in range(2):
            fw.op(DVE, (lambda l, i: lambda e: e.tensor_tensor(out=tmpA[:, 0:64], in0=lq_all[:, l, 128 * i:128 * i + 64],
                                                                 in1=lq_all[:, l, 128 * i + 64:128 * i + 128], op=ALU.mult))(l, i),
                  reads=[AB[0], AB[1]], writes=[B_tmpA])
            fw.op(DVE, (lambda l, i: lambda e: e.reduce_sum(out=lam_tmp[:, 4 * l + i:4 * l + i + 1], in_=tmpA[:, 0:64],
                                                              axis=mybir.AxisListType.X))(l, i),
                  reads=[B_tmpA], writes=[B_lay])
        fw.op(ACT, (lambda l: lambda e: e.activation(out=lam_tmp[:, 4 * l + 2:4 * l + 4], in_=lam_tmp[:, 4 * l:4 * l + 2], func=AF.Exp))(l),
              reads=[B_lay], writes=[B_lay])
        fw.op(DVE, (lambda l, li: lambda e: e.scalar_tensor_tensor(out=neglam_all[:, l:l + 1], in0=lam_tmp[:, 4 * l + 3:4 * l + 4],
                                                                     scalar=-li, in1=lam_tmp[:, 4 * l + 2:4 * l + 3],
                                                                     op0=ALU.add, op1=ALU.subtract))(l, lam_init),
              reads=[B_lay], writes=[B_lay])
        fw.op(DVE, (lambda l, li: lambda e: e.tensor_scalar(out=subln_all[:, l:l + 1], in0=subln_all[:, l:l + 1], scalar1=1.0 - li,
                                                              scalar2=None, op0=ALU.mult))(l, lam_init),
              reads=[B_lay], writes=[B_lay])

    def wsc_view(l, ci, kc, ncol):
        return wsc[l, ci].rearrange("p (k c) -> p k c", k=kc)

    def cast_weights(l):
        fns = []
        for ci in range(14):
            src = w_in[l].rearrange("(k p) c -> p k c", p=128)[:, :, ci * 512:(ci + 1) * 512]
            dst = wsc_view(l, ci, 8, 512)
            fns.append((lambda s, d: lambda e: e.dma_start(out=d, in_=s))(src, dst))
        src = w_a[l].rearrange("(g p) c -> p g c", p=128)
        fns.append((lambda s, d: lambda e: e.dma_start(out=d, in_=s))(src, wsc_view(l, 14, 4, 1024)))
        for hb in range(2):
            src = w_b[l].rearrange("(k p) c -> p k c", p=128)[:, :, hb * 512:(hb + 1) * 512]
            fns.append((lambda s, d: lambda e: e.dma_start(out=d, in_=s))(src, wsc_view(l, 15 + hb, 8, 512)))
        for hb in range(2):
            src = w_o[l].rearrange("(k p) c -> p k c", p=128)[:, :, hb * 512:(hb + 1) * 512]
            fns.append((lambda s, d: lambda e: e.dma_start(out=d, in_=s))(src, wsc_view(l, 17 + hb, 8, 512)))
        fw.dma(POOL, fns, B_wsc[l], writes=[B_wsc[l]])

    cast_weights(0)

    CH_ORDER = [0, 1, 2, 3, 4, 5, 6, 7, 8, 9, 14, 10, 11, 15, 12, 16, 13, 17, 18]
    wfree = [0, 1]

    def wload(l, k, slot):
        ci = CH_ORDER[k]
        dst = WS[slot].rearrange("p k c -> p (k c)")
        dma1(SP, dst, wsc[l, ci], B_WS[slot], reads=[B_wsc[l]], writes=[B_WS[slot]])

    def ln_rows(r_ap, r_bufs, n, g_ap, b_ap, pbufs):
        for hlf in range(2):
            fw.op(DVE, (lambda hlf: lambda e: e.bn_stats(out=stats[0:n, hlf, :], in_=r_ap[0:n, hlf * 512:(hlf + 1) * 512]))(hlf),
                  reads=r_bufs, writes=[B_stats])
        fw.op(DVE, lambda e: e.bn_aggr(out=mv[0:n, 0:2], in_=stats[0:n].rearrange("p a b -> p (a b)")), reads=[B_stats], writes=[B_mv])
        fw.op(ACT, lambda e: e.activation(out=mv[0:n, 2:3], in_=mv[0:n, 1:2], func=AF.Ln, bias=epsc[0:n, 0:1], scale=1.0),
              reads=[B_mv, B_const], writes=[B_mv])
        fw.op(ACT, lambda e: e.activation(out=mv[0:n, 3:4], in_=mv[0:n, 2:3], func=AF.Exp, scale=-0.5),
              reads=[B_mv], writes=[B_mv])
        fw.op(DVE, lambda e: e.scalar_tensor_tensor(out=mv[0:n, 4:5], in0=mv[0:n, 0:1], scalar=-1.0, in1=mv[0:n, 3:4],
                                                    op0=ALU.mult, op1=ALU.mult), reads=[B_mv], writes=[B_mv])
        fw.op(ACT, lambda e: e.activation(out=r_ap[0:n, :], in_=r_ap[0:n, :], func=AF.Identity, bias=mv[0:n, 4:5], scale=mv[0:n, 3:4]),
              reads=r_bufs + [B_mv], writes=r_bufs)
        fw.op(DVE, lambda e: e.tensor_tensor(out=r_ap[0:n, :], in0=r_ap[0:n, :], in1=g_ap[0:n, :], op=ALU.mult),
              reads=r_bufs + pbufs, writes=r_bufs)
        fw.op(DVE, lambda e: e.tensor_tensor(out=r_ap[0:n, :], in0=r_ap[0:n, :], in1=b_ap[0:n, :], op=ALU.add),
              reads=r_bufs + pbufs, writes=r_bufs)

    def load_lnp(g_src, b_src):
        dma1(SP, arf(8, 1024), g_src.partition_broadcast(128), AB[8], writes=[AB[8], AB[9]])
        dma1(SP, arf(10, 1024), b_src.partition_broadcast(128), AB[10], writes=[AB[10], AB[11]])

    load_lnp(ln_in_g, ln_in_b)
    rows = [(xp[i * 128:(i + 1) * 128, :], xcur_p[i * 128:(i + 1) * 128, :], 128, B_xcur_p[i]) for i in range(SEQ // 128)]
    if with_sample:
        rows.append((xs_in, xcur_s, 64, B_xcur_s))
    for i, (src, dst, n, bdst) in enumerate(rows):
        sl = 2 * (i % 4)
        r_ap = arf(sl, 1024)
        rb = [AB[sl], AB[sl + 1]]
        dma1(SP, r_ap[0:n, :], src, AB[sl], writes=rb)
        ln_rows(r_ap, rb, n, arf(8, 1024), arf(10, 1024), [AB[8], AB[9], AB[10], AB[11]])
        dma1(SP, dst, r_ap[0:n, :], AB[sl], reads=rb, writes=[bdst])

    pp = {"i": 0}

    def next_ps():
        pp["i"] += 1
        return (pA, B_pA) if pp["i"] % 2 else (pB, B_pB)

    def mm_acc(out_ap, out_buf, pairs, reads):
        n = len(pairs)
        for i, (lt, rh) in enumerate(pairs):
            fw.op(PE, (lambda lt, rh, i: lambda e: e.matmul(out_ap, lhsT=lt, rhs=rh, start=(i == 0), stop=(i == n - 1)))(lt, rh, i),
                  reads=reads, writes=[out_buf], sig=(i == n - 1))

    def sig_gate(ps_ap, ps_buf, nt, gt, gt_buf, silu):
        fw.op(ACT, lambda e: e.activation(out=gt[:, 0:nt], in_=ps_ap, func=AF.Tanh, scale=0.5), reads=[ps_buf], writes=[gt_buf])
        fw.op(DVE, lambda e: e.tensor_scalar(out=gt[:, 0:nt], in0=gt[:, 0:nt], scalar1=0.5, scalar2=0.5, op0=ALU.mult, op1=ALU.add),
              reads=[gt_buf], writes=[gt_buf])
        if silu:
            fw.op(DVE, lambda e: e.tensor_tensor(out=gt[:, 0:nt], in0=ps_ap, in1=gt[:, 0:nt], op=ALU.mult),
                  reads=[ps_buf, gt_buf], writes=[gt_buf])

    def attn_epilogue1(W_):
        rinv, t1 = arf(2, 2 * W_), arf(3, 2 * W_)
        fw.op(ACT, lambda e: e.activation(out=rinv, in_=pB[:, 0:2 * W_], func=AF.Ln), reads=[B_pB], writes=[AB[2]])
        fw.op(ACT, lambda e: e.activation(out=rinv, in_=rinv, func=AF.Exp, scale=-1.0), reads=[AB[2]], writes=[AB[2]])
        fw.op(DVE, lambda e: e.tensor_tensor(out=t1, in0=pO[:, 0:2 * W_], in1=rinv, op=ALU.mult), reads=[B_pO, AB[2]], writes=[AB[3]])

    def attn_epilogue2(l, W_):
        t1 = arf(3, 2 * W_)
        o, sq = arf(4, W_), arf(4, W_, 256)
        rs, rstd = arf(5, W_), arf(5, W_, 256)
        fw.op(DVE, lambda e: e.scalar_tensor_tensor(out=o, in0=t1[:, W_:2 * W_], scalar=neglam_all[:, l:l + 1], in1=t1[:, 0:W_],
                                                    op0=ALU.mult, op1=ALU.add), reads=[AB[3], B_lay], writes=[AB[4]])
        fw.op(ACT, lambda e: e.activation(out=sq, in_=o, func=AF.Square), reads=[AB[4]], writes=[AB[4]])
        fw.op(PE, lambda e: e.matmul(pA[:, 0:W_], lhsT=ones_f, rhs=sq, start=True, stop=True), reads=[AB[4], B_const], writes=[B_pA])
        fw.op(ACT, lambda e: e.activation(out=rs, in_=pA[:, 0:W_], func=AF.Ln, bias=epsc[:, 1:2], scale=1.0 / 128),
              reads=[B_pA, B_const], writes=[AB[5]])
        fw.op(ACT, lambda e: e.activation(out=rstd, in_=rs, func=AF.Exp, scale=-0.5), reads=[AB[5]], writes=[AB[5]])
        fw.op(DVE, lambda e: e.tensor_tensor(out=o, in0=o, in1=rstd, op=ALU.mult), reads=[AB[4], AB[5]], writes=[AB[4]])
        return o

    def tile_pass(l, ti, sample):
        last_layer = (l == L - 1)
        if sample:
            ntok, subs = 64, [(0, 64)]
        else:
            ntok, subs = TT, [(s, 128) for s in range(4)]
        nsub = len(subs)
        tok0 = 0 if sample else ti * TT
        xsrc = xcur_s if sample else xcur_p
        ydst = (y_s if sample else y_p) if last_layer else xsrc

        def xbuf(s):
            return B_xcur_s if sample else B_xcur_p[ti * 4 + s]

        wq = []
        nxt = {"k": 0}

        def wtop():
            while wfree and nxt["k"] < NCH:
                sl_ = wfree.pop(0)
                wload(l, nxt["k"], sl_)
                wq.append(sl_)
                nxt["k"] += 1

        def wnext():
            wtop()
            return wq.pop(0)

        def wrel(*slots):
            for sl_ in slots:
                wfree.append(sl_)
            wtop()

        wtop()

        if sample:
            for b in range(NSEQ_S):
                for jt in range(8):
                    i = b * 8 + jt
                    sl = i % 4
                    kst = arb(sl, 1024)
                    fw.dma(POOL, [(lambda b, jt, kst: lambda e: e.dma_start(out=kst, in_=ck[l, b, jt * 128:(jt + 1) * 128, :]))(b, jt, kst)],
                           AB[sl], writes=[AB[sl]])
                    fw.dma(POOL, [(lambda b, jt: lambda e: e.dma_start(out=VX[:, b * 8 + jt, :, 0:128],
                                                                         in_=cv[l, b, jt * 128:(jt + 1) * 128, :].rearrange("p (h v) -> p h v", h=8)))(b, jt)],
                           B_VX, writes=[B_VX])
                    for h in range(8):
                        fw.op(PE, (lambda h, kst: lambda e: e.transpose(out=pT[:, h * 128:(h + 1) * 128], in_=kst[:, h * 128:(h + 1) * 128],
                                                                        identity=ident))(h, kst),
                              reads=[AB[sl], B_const], writes=[B_pT], sig=(h == 7))
                    E = ACT if i % 2 == 0 else DVE
                    cp(E, KT[:, :, b * PAST + jt * 128: b * PAST + (jt + 1) * 128], pT.rearrange("p (h k) -> p h k", h=8),
                       reads=[B_pT], writes=[B_KT])

        for s, n in subs:
            sl = 2 * (s % 2)
            xin = arf(sl, 1024)
            xbf = arb(4 + (s % 2), 1024)
            r0 = tok0 + s * 128
            dma1(SP, xin[0:n, :], xsrc[r0:r0 + n, :], AB[sl], reads=[xbuf(s)], writes=[AB[sl], AB[sl + 1]])
            cp(ACT, xbf[0:n, :], xin[0:n, :], reads=[AB[sl], AB[sl + 1]], writes=[AB[4 + s % 2]])
            for c in range(8):
                fw.op(PE, (lambda c, xbf, n: lambda e: e.transpose(out=pT[:, c * 128:c * 128 + n], in_=xbf[0:n, c * 128:(c + 1) * 128],
                                                                  identity=ident[0:n, 0:n]))(c, xbf, n),
                      reads=[AB[4 + s % 2], B_const], writes=[B_pT], sig=(c == 7))
            cp(DVE, xT[:, :, s * 128:s * 128 + n], pT.rearrange("p (c k) -> p c k", c=8)[:, :, 0:n], reads=[B_pT], writes=[B_xT])
        if sample:
            dma1(SP, cs_sb[0:64, 0, :], cs_s, B_cs, writes=[B_cs])
        else:
            dma1(SP, cs_sb, cs_p[tok0:tok0 + TT, :].rearrange("(s p) c -> p s c", p=128), B_cs, writes=[B_cs])

        if _STOP == 1:
            wfree[:] = [0, 1]
            return
        Wd = 128 if sample else 528
        pxT = arena[:, 0:4 * Wd].rearrange("p (g w) -> p g w", g=4)
        pxb = [AB[0], AB[1], AB[2], AB[3], AB[4]] if not sample else [AB[0]]
        pooled = arb(6, 4 * TT).rearrange("p (g t) -> p g t", g=4)
        pooledb = [AB[6], AB[7]]
        yaT = arb(10, 4 * TT).rearrange("p (g t) -> p g t", g=4)
        yab = [AB[10], AB[11]]
        gt0, gt1 = arf(8, 512), arf(9, 512)

        def outview(ap2d):
            if sample:
                return ap2d.rearrange("p (b w) -> p b w", b=4)[:, :, 16:32]
            return ap2d[:, 16:528]

        def as_out(ap2d):
            if sample:
                return ap2d.rearrange("p (b t) -> p b t", b=4)
            return ap2d

        if sample:
            fw.op(DVE, lambda e: e.memset(arena[:, 0:4 * Wd], 0.0), writes=pxb)
            sp_sb = arf(2, 512)
            dma1(SP, sp_sb[0:60, :], spool[l], AB[2], writes=[AB[2]])
            for g in range(4):
                fw.op(PE, (lambda g: lambda e: e.transpose(out=pA[:, g * 64:g * 64 + 60], in_=sp_sb[0:60, g * 128:(g + 1) * 128],
                                                           identity=identf[0:60, 0:60]))(g),
                      reads=[AB[2], B_const], writes=[B_pA], sig=(g == 3))
            for g in range(4):
                cp(ACT, pxT[:, g, :].rearrange("p (b w) -> p b w", b=4)[:, :, 1:16],
                   pA[:, g * 64:g * 64 + 60].rearrange("p (b j) -> p b j", b=4), reads=[B_pA], writes=pxb)
        elif ti == 0:
            fw.op(DVE, lambda e: e.memset(pxT[:, :, 0:16], 0.0), writes=pxb)
        else:
            cp(DVE, pxT[:, :, 0:16], hist, reads=[B_hist], writes=pxb)

        slot = wnext()
        Wc = WS[slot]
        for e_ in range(4):
            ps, psb = next_ps()
            mm_acc(ps[:, 0:ntok], psb, [(Wc[:, d, e_ * 128:(e_ + 1) * 128], xT[:, d, 0:ntok]) for d in range(8)], [B_WS[slot], B_xT])
            cp(ACT, outview(pxT[:, e_, :]), as_out(ps[:, 0:ntok]), reads=[psb], writes=pxb)
        if sample or ti == NT - 1:
            s_last, n_last = subs[-1]
            ps, psb = next_ps()
            mm_acc(ps[0:n_last, :], psb, [(xT[:, d, s_last * 128:s_last * 128 + n_last], Wc[:, d, :]) for d in range(8)], [B_WS[slot], B_xT])
            cp(ACT, gt1[0:n_last, :], ps[0:n_last, :], reads=[psb], writes=[AB[9]])
            if sample:
                fw.dma(SP, [(lambda b: lambda e: e.dma_start(out=pool_s[l, 15 * b:15 * b + 15, :], in_=gt1[16 * b + 1:16 * b + 16, :]))(b)
                            for b in range(4)], AB[9], reads=[AB[9]], is_out=True)
            else:
                dma1(SP, pool_p[l], gt1[113:128, :], AB[9], reads=[AB[9]], is_out=True)
        wrel(slot)
        if not sample:
            cp(POOL, hist, pxT[:, :, 512:528], reads=pxb, writes=[B_hist])
        for g in range(4):
            u = pxT[:, g, :]
            w = 2 ** (g + 1)

            def add(out, a, b_, rd, wr):
                fw.op(DVE, lambda e: e.tensor_tensor(out=out, in0=a, in1=b_, op=ALU.add), reads=rd, writes=wr)

            if g == 0:
                add(tmpA[:, 16:Wd], u[:, 16:Wd], u[:, 15:Wd - 1], pxb, [B_tmpA])
                s_ap, s_b = tmpA, B_tmpA
            elif g == 1:
                add(tmpA[:, 14:Wd], u[:, 14:Wd], u[:, 13:Wd - 1], pxb, [B_tmpA])
                add(tmpB[:, 16:Wd], tmpA[:, 16:Wd], tmpA[:, 14:Wd - 2], [B_tmpA], [B_tmpB])
                s_ap, s_b = tmpB, B_tmpB
            elif g == 2:
                add(tmpA[:, 10:Wd], u[:, 10:Wd], u[:, 9:Wd - 1], pxb, [B_tmpA])
                add(tmpB[:, 12:Wd], tmpA[:, 12:Wd], tmpA[:, 10:Wd - 2], [B_tmpA], [B_tmpB])
                add(tmpA[:, 16:Wd], tmpB[:, 16:Wd], tmpB[:, 12:Wd - 4], [B_tmpB], [B_tmpA])
                s_ap, s_b = tmpA, B_tmpA
            else:
                add(tmpA[:, 2:Wd], u[:, 2:Wd], u[:, 1:Wd - 1], pxb, [B_tmpA])
                add(tmpB[:, 4:Wd], tmpA[:, 4:Wd], tmpA[:, 2:Wd - 2], [B_tmpA], [B_tmpB])
                add(tmpA[:, 8:Wd], tmpB[:, 8:Wd], tmpB[:, 4:Wd - 4], [B_tmpB], [B_tmpA])
                add(tmpB[:, 16:Wd], tmpA[:, 16:Wd], tmpA[:, 8:Wd - 8], [B_tmpA], [B_tmpB])
                s_ap, s_b = tmpB, B_tmpB
            fw.op(DVE, (lambda g, s_ap, u, w: lambda e: e.scalar_tensor_tensor(out=as_out(pooled[:, g, 0:ntok]), in0=outview(s_ap[:, 0:Wd]),
                                                                                scalar=1.0 / w, in1=outview(u), op0=ALU.mult,
                                                                                op1=ALU.subtract))(g, s_ap, u, w),
                  reads=[s_b] + pxb, writes=pooledb)
            if (not sample) and ti == 0:
                fw.op(DVE, (lambda g, s_ap: lambda e: e.tensor_tensor(out=gt0[:, 0:16], in0=s_ap[:, 16:32], in1=invc_sb[:, g, :],
                                                                       op=ALU.mult))(g, s_ap), reads=[s_b, B_const], writes=[AB[8]])
                fw.op(DVE, (lambda g, u: lambda e: e.tensor_tensor(out=pooled[:, g, 0:16], in0=gt0[:, 0:16], in1=u[:, 16:32],
                                                                    op=ALU.subtract))(g, u), reads=[AB[8]] + pxb, writes=pooledb)
        slot = wnext()
        Wc = WS[slot]
        for e_ in range(4):
            mm_acc(pA[:, 0:ntok], B_pA, [(Wc[:, d, e_ * 128:(e_ + 1) * 128], xT[:, d, 0:ntok]) for d in range(8)], [B_WS[slot], B_xT])
            mm_acc(pB[:, 0:ntok], B_pB, [(wpool_sb[:, e_, :], pooled[:, e_, 0:ntok])], [B_wpool] + pooledb)
            sig_gate(pA[:, 0:ntok], B_pA, ntok, gt0, AB[8], silu=True)
            fw.op(DVE, (lambda e_: lambda e: e.scalar_tensor_tensor(out=yaT[:, e_, 0:ntok], in0=pB[:, 0:ntok],
                                                                     scalar=pscale_all[:, l, e_:e_ + 1], in1=gt0[:, 0:ntok],
                                                                     op0=ALU.mult, op1=ALU.mult))(e_),
                  reads=[B_pB, AB[8], B_lay], writes=yab)
        wrel(slot)

        if _STOP == 2:
            wfree[:] = [0, 1]
            return
        for c in range(6):
            slot = wnext()
            Wc = WS[slot]
            kind = c // 2
            if str(kind) not in _os.environ.get('KP2', '012'):
                wrel(slot)
                continue
            h0 = (c % 2) * 4
            col0 = h0 * 128
            for s, n in subs:
                i2 = (c * nsub + s) % 2
                ps, psb = next_ps()
                mm_acc(ps[0:n, :], psb, [(xT[:, d, s * 128:s * 128 + n], Wc[:, d, :]) for d in range(8)], [B_WS[slot], B_xT])
                r0 = tok0 + s * 128
                if kind == 2:
                    vst = arf(2 + i2, 512)
                    cp(ACT, vst[0:n, :], ps[0:n, :], reads=[psb], writes=[AB[2 + i2]])
                    dma1(SP, (v_s if sample else v_p)[l, r0:r0 + n, col0:col0 + 512], vst[0:n, :], AB[2 + i2], reads=[AB[2 + i2]], is_out=True)
                    if sample:
                        cp(DVE, vbf_s[0:64, col0:col0 + 512], vst[0:64, :], reads=[AB[2 + i2]], writes=[B_tmpA])
                    else:
                        cp(DVE, VX[0:n, ti * 4 + s, h0:h0 + 4, 0:128], vst[0:n, :].rearrange("p (h v) -> p h v", h=4), reads=[AB[2 + i2]],
                           writes=[B_VX])
                    continue
                ra = arf(4 + i2, 512) if kind == 0 else arf(0 + i2, 512)
                rab = AB[4 + i2] if kind == 0 else AB[0 + i2]
                rb_ = arf(6 + i2, 512)
                rbb = AB[6 + i2]
                ps3 = ps[0:n, :].rearrange("p (g x) -> p g x", g=8)
                ra3 = ra[0:n, :].rearrange("p (g x) -> p g x", g=8)
                rb3 = rb_[0:n, :].rearrange("p (g x) -> p g x", g=8)
                cosb = cs_sb[0:n, s, 0:64].unsqueeze(1).to_broadcast([n, 8, 64])
                sin1 = cs_sb[0:n, s, 64:96].unsqueeze(1).to_broadcast([n, 8, 32])
                sin2 = cs_sb[0:n, s, 96:128].unsqueeze(1).to_broadcast([n, 8, 32])
                fw.op(DVE, (lambda ra3, ps3, cosb: lambda e: e.tensor_tensor(out=ra3, in0=ps3, in1=cosb, op=ALU.mult))(ra3, ps3, cosb),
                      reads=[psb, B_cs], writes=[rab])
                fw.op(DVE, (lambda rb3, ps3, sin1: lambda e: e.tensor_tensor(out=rb3[:, :, 0:32], in0=ps3[:, :, 32:64], in1=sin1,
                                                                              op=ALU.mult))(rb3, ps3, sin1),
                      reads=[psb, B_cs], writes=[rbb])
                fw.op(DVE, (lambda rb3, ps3, sin2: lambda e: e.tensor_tensor(out=rb3[:, :, 32:64], in0=ps3[:, :, 0:32], in1=sin2,
                                                                              op=ALU.mult))(rb3, ps3, sin2),
                      reads=[psb, B_cs], writes=[rbb])
                bfv = arb(8 + i2, 512, 512 * kind)
                bfb = AB[8 + i2]
                if kind == 0:
                    fw.op(DVE, (lambda bfv, ra, rb_, n: lambda e: e.tensor_tensor(out=bfv[0:n, :], in0=ra[0:n, :], in1=rb_[0:n, :],
                                                                                   op=ALU.add))(bfv, ra, rb_, n),
                          reads=[rab, rbb], writes=[bfb])
                else:
                    fw.op(DVE, (lambda ra, rb_, n: lambda e: e.tensor_tensor(out=ra[0:n, :], in0=ra[0:n, :], in1=rb_[0:n, :],
                                                                              op=ALU.add))(ra, rb_, n),
                          reads=[rab, rbb], writes=[rab])
                    dma1(SP, (k_s if sample else k_p)[l, r0:r0 + n, col0:col0 + 512], ra[0:n, :], rab, reads=[rab], is_out=True)
                    cp(ACT, bfv[0:n, :], ra[0:n, :], reads=[rab], writes=[bfb])
                for hh in range(4):
                    fw.op(PE, (lambda hh, bfv, n: lambda e: e.transpose(out=pT[:, hh * 128:hh * 128 + n], in_=bfv[0:n, hh * 128:(hh + 1) * 128],
                                                                       identity=ident[0:n, 0:n]))(hh, bfv, n),
                          reads=[bfb, B_const], writes=[B_pT], sig=(hh == 3))
                src = pT[:, 0:512].rearrange("p (h k) -> p h k", h=4)[:, :, 0:n]
                if kind == 0:
                    cp(ACT, QT[:, h0:h0 + 4, s * 128:s * 128 + n], src, reads=[B_pT], writes=[B_QT])
                elif sample:
                    cp(DVE, KTn[:, h0:h0 + 4, 0:64], src, reads=[B_pT], writes=[B_KTn])
                else:
                    cp(DVE, KT[:, h0:h0 + 4, r0:r0 + n], src, reads=[B_pT], writes=[B_KT])
            wrel(slot)
        if _STOP == 3:
            wfree[:] = [0, 1]
            return
        ptc = {"i": 0}

        def next_pt():
            ptc["i"] += 1
            k = ptc["i"] % 4
            return arb(k // 2, 512, 512 * (k % 2)), AB[k // 2]

        sublnc = subln_all[:, l:l + 1]
        if not sample:
            pending = None
            for h in range(HEADS):
                for qbl in range(2):
                    qb = 2 * ti + qbl
                    jmax = 2 * qb + 1
                    qc0 = qbl * 256
                    npairs = qb + 1
                    pts = {}
                    for step in range(npairs + 1):
                        if step < npairs:
                            jp = step
                            pS, pSb = (pSX, B_pSX) if jp % 2 == 0 else (pSY, B_pSY)
                            for jl in range(2):
                                j = 2 * jp + jl
                                for m in range(2):
                                    fw.op(PE, (lambda pS, m, jl, j, h, qc0: lambda e: e.matmul(
                                        pS[:, m, jl * 256:(jl + 1) * 256], lhsT=KT[m * 64:(m + 1) * 64, h, j * 128:(j + 1) * 128],
                                        rhs=QT[m * 64:(m + 1) * 64, h, qc0:qc0 + 256], start=True, stop=True))(pS, m, jl, j, h, qc0),
                                          reads=[B_KT, B_QT], writes=[pSb], sig=(jl == 1 and m == 1))
                            for jl in range(2):
                                j = 2 * jp + jl
                                pt, ptb = next_pt()
                                pts[j] = (pt, ptb)
                                pt3 = pt.rearrange("p (m q) -> p m q", m=2)
                                fw.op(ACT, (lambda pt3, pS, jl: lambda e: e.activation(out=pt3, in_=pS[:, :, jl * 256:(jl + 1) * 256],
                                                                                      func=AF.Exp, scale=0.125))(pt3, pS, jl),
                                      reads=[pSb], writes=[ptb])
                                if jp == qb:
                                    if jl == 0:
                                        fw.op(POOL, (lambda pt3: lambda e: e.memset(pt3[64:128, :, 0:64], 0.0))(pt3), writes=[ptb])
                                    else:
                                        fw.op(POOL, (lambda pt3: lambda e: e.memset(pt3[0:64, :, 0:128], 0.0))(pt3), writes=[ptb])
                                        fw.op(POOL, (lambda pt3: lambda e: e.memset(pt3[64:128, :, 0:192], 0.0))(pt3), writes=[ptb])
                        if step >= 1:
                            for jl in range(2):
                                j = 2 * (step - 1) + jl
                                pt, ptb = pts.pop(j)
                                fw.op(PE, (lambda pt, j, h, jmax: lambda e: e.matmul(pO, lhsT=VX[:, j, h, 0:128], rhs=pt, start=(j == 0),
                                                                                    stop=(j == jmax)))(pt, j, h, jmax),
                                      reads=[B_VX, ptb], writes=[B_pO], sig=False)
                                fw.op(PE, (lambda pt, j, jmax: lambda e: e.matmul(pB, lhsT=ones_bf, rhs=pt, start=(j == 0),
                                                                                 stop=(j == jmax)))(pt, j, jmax),
                                      reads=[B_const, ptb], writes=[B_pB], sig=True)
                        if step == 1 and pending is not None:
                            pending()
                            pending = None
                    attn_epilogue1(256)

                    def _p2(h=h, qc0=qc0):
                        o = attn_epilogue2(l, 256)
                        fw.op(ACT, lambda e: e.activation(out=ybT[:, h, qc0:qc0 + 256], in_=o, func=AF.Copy, scale=sublnc),
                              reads=[AB[4], B_lay], writes=[B_ybT])
                    pending = _p2
            if pending is not None:
                pending()
        else:
            for b in range(NSEQ_S):
                fw.op(PE, lambda e: e.matmul(pO[:, 0:256], lhsT=zeros_bf[:, 0:128], rhs=zeros_bf[:, 0:256], start=True, stop=True),
                      reads=[B_const], writes=[B_pO])
                fw.op(PE, lambda e: e.matmul(pB[:, 0:256], lhsT=zeros_bf[:, 0:128], rhs=zeros_bf[:, 0:256], start=True, stop=True),
                      reads=[B_const], writes=[B_pB])
                for grp in range(3):
                    pS, pSb = (pSX, B_pSX) if grp % 2 == 0 else (pSY, B_pSY)
                    njl = 4 if grp < 2 else 1
                    kp = 128 if grp < 2 else TS
                    for jl in range(njl):
                        jt = grp * 4 + jl
                        for h in range(HEADS):
                            for m in range(2):
                                if grp < 2:
                                    lt = KT[m * 64:(m + 1) * 64, h, b * PAST + jt * 128: b * PAST + (jt + 1) * 128]
                                else:
                                    lt = KTn[m * 64:(m + 1) * 64, h, b * TS:(b + 1) * TS]
                                fw.op(PE, (lambda pS, m, jl, h, lt, b, kp: lambda e: e.matmul(
                                    pS[0:kp, m, jl * 128 + h * TS: jl * 128 + (h + 1) * TS], lhsT=lt,
                                    rhs=QT[m * 64:(m + 1) * 64, h, b * TS:(b + 1) * TS], start=True, stop=True))(pS, m, jl, h, lt, b, kp),
                                      reads=[B_KT, B_KTn, B_QT], writes=[pSb], sig=(h == 7 and m == 1 and jl == njl - 1))
                    if grp < 2:
                        pt, ptb = arb(0, 1024), AB[0]
                        if grp == 1:
                            pt, ptb = arb(1, 1024), AB[1]
                        pt4 = pt.rearrange("p (m x) -> p m x", m=2)
                        fw.op(ACT, (lambda pt4, pS: lambda e: e.activation(out=pt4, in_=pS, func=AF.Exp, scale=0.125))(pt4, pS),
                              reads=[pSb], writes=[ptb])
                        for jl in range(4):
                            jt = grp * 4 + jl
                            for m in range(2):
                                fw.op(PE, (lambda pt4, m, jl: lambda e: e.matmul(pB[:, m * 128:(m + 1) * 128], lhsT=ones_bf,
                                                                                rhs=pt4[:, m, jl * 128:(jl + 1) * 128], start=False,
                                                                                stop=False, skip_group_check=True))(pt4, m, jl),
                                      reads=[ptb, B_const], writes=[B_pB], sig=False)
                                for h in range(HEADS):
                                    fw.op(PE, (lambda pt4, m, jl, h, jt, b: lambda e: e.matmul(
                                        pO[:, m * 128 + h * TS: m * 128 + (h + 1) * TS], lhsT=VX[:, b * 8 + jt, h, 0:128],
                                        rhs=pt4[:, m, jl * 128 + h * TS: jl * 128 + (h + 1) * TS], start=False, stop=False,
                                        skip_group_check=True))(pt4, m, jl, h, jt, b),
                                          reads=[ptb, B_VX], writes=[B_pO], sig=(m == 1 and h == 7))
                    else:
                        dma1(SP, VN[0:TS, :, :], vbf_s[TS * b:TS * b + TS, :].rearrange("p (h v) -> p h v", h=8), B_VN,
                             reads=[B_tmpA], writes=[B_VN])
                        fw.op(ACT, (lambda pS: lambda e: e.activation(out=PTn[0:TS].rearrange("p m h q -> p m (h q)"), in_=pS[0:TS, :, 0:128],
                                                                       func=AF.Exp, scale=0.125))(pS),
                              reads=[pSb], writes=[B_PTn])
                        for m in range(2):
                            fw.op(PE, (lambda m: lambda e: e.matmul(pB[:, m * 128:(m + 1) * 128], lhsT=ones_bf,
                                                                    rhs=PTn[:, m].rearrange("p h q -> p (h q)"), start=False, stop=False,
                                                                    skip_group_check=True))(m),
                                  reads=[B_PTn, B_const], writes=[B_pB], sig=False)
                            for h in range(HEADS):
                                fw.op(PE, (lambda m, h, b: lambda e: e.matmul(pO[:, m * 128 + h * TS: m * 128 + (h + 1) * TS], lhsT=VN[:, h, :],
                                                                              rhs=PTn[:, m, h, :], start=False, stop=True,
                                                                              skip_group_check=True))(m, h, b),
                                      reads=[B_PTn, B_VN], writes=[B_pO, B_pB], sig=(m == 1 and h == 7))
                attn_epilogue1(128)
                o = attn_epilogue2(l, 128)
                fw.op(ACT, (lambda b, o: lambda e: e.activation(out=ybT[:, :, b * TS:(b + 1) * TS], in_=o.rearrange("p (h q) -> p h q", h=8),
                                                                 func=AF.Copy, scale=sublnc))(b, o),
                      reads=[AB[4], B_lay], writes=[B_ybT])

        if _STOP == 4:
            wfree[:] = [0, 1]
            return
        gt0, gt1 = arf(8, 512), arf(9, 512)
        for c in range(2):
            slot = wnext()
            Wc = WS[slot]
            for e_ in range(4):
                h = 4 * c + e_
                ps, psb = next_ps()
                mm_acc(ps[:, 0:ntok], psb, [(Wc[:, d, e_ * 128:(e_ + 1) * 128], xT[:, d, 0:ntok]) for d in range(8)], [B_WS[slot], B_xT])
                sig_gate(ps[:, 0:ntok], psb, ntok, gt0, AB[8], silu=True)
                fw.op(DVE, (lambda h: lambda e: e.tensor_tensor(out=ybT[:, h, 0:ntok], in0=ybT[:, h, 0:ntok], in1=gt0[:, 0:ntok],
                                                                 op=ALU.mult))(h), reads=[B_ybT, AB[8]], writes=[B_ybT])
            wrel(slot)

        if _STOP == 5:
            wfree[:] = [0, 1]
            return
        m1 = arena[:, 0:8 * 512].rearrange("p (c t) -> p c t", c=8)
        m1b = AB[0:8]
        slot_a = wnext()
        WA = WS[slot_a].rearrange("p k c -> p (k c)").rearrange("p (g c) -> p g c", g=4)
        for half in range(2):
            slot = wnext()
            Wc = WS[slot]
            for dcl in range(4):
                dc = 4 * half + dcl
                mm_acc(pA[:, 0:ntok], B_pA, [(WA[:, g, dc * 128:(dc + 1) * 128], yaT[:, g, 0:ntok]) for g in range(4)], [B_WS[slot_a]] + yab)
                mm_acc(pB[:, 0:ntok], B_pB, [(Wc[:, d, dcl * 128:(dcl + 1) * 128], xT[:, d, 0:ntok]) for d in range(8)], [B_WS[slot], B_xT])
                sig_gate(pB[:, 0:ntok], B_pB, ntok, gt0, AB[8], silu=False)
                fw.op(DVE, (lambda dc: lambda e: e.tensor_tensor(out=m1[:, dc, 0:ntok], in0=pA[:, 0:ntok], in1=gt0[:, 0:ntok],
                                                                  op=ALU.mult))(dc), reads=[B_pA, AB[8]], writes=[m1b[dc]])
            if half == 0:
                wrel(slot)
        wrel(slot_a, slot)
        mergedT = QT
        for half in range(2):
            slot_b = wnext()
            WB = WS[slot_b]
            slot = wnext()
            Wc = WS[slot]
            for dcl in range(4):
                dc = 4 * half + dcl
                mm_acc(pA[:, 0:ntok], B_pA, [(WB[:, h, dcl * 128:(dcl + 1) * 128], ybT[:, h, 0:ntok]) for h in range(8)], [B_WS[slot_b], B_ybT])
                mm_acc(pB[:, 0:ntok], B_pB, [(Wc[:, d, dcl * 128:(dcl + 1) * 128], xT[:, d, 0:ntok]) for d in range(8)], [B_WS[slot], B_xT])
                sig_gate(pB[:, 0:ntok], B_pB, ntok, gt0, AB[8], silu=False)
                fw.op(DVE, lambda e: e.tensor_tensor(out=gt1[:, 0:ntok], in0=pA[:, 0:ntok], in1=gt0[:, 0:ntok], op=ALU.mult),
                      reads=[B_pA, AB[8]], writes=[AB[9]])
                fw.op(DVE, (lambda dc: lambda e: e.tensor_tensor(out=mergedT[:, dc, 0:ntok], in0=m1[:, dc, 0:ntok], in1=gt1[:, 0:ntok],
                                                                   op=ALU.add))(dc), reads=[m1b[dc], AB[9]], writes=[B_QT])
            wrel(slot_b, slot)

        if _STOP == 6:
            wfree[:] = [0, 1]
            return
        slot0 = wnext()
        slot1 = wnext()
        load_lnp(ln_g[l:l + 1, :], ln_b[l:l + 1, :])
        for s, n in subs:
            i2 = s % 2
            xres = arf(2 * i2, 1024)
            xrb = [AB[2 * i2], AB[2 * i2 + 1]]
            r_ap = arf(4 + 2 * i2, 1024)
            rbufs = [AB[4 + 2 * i2], AB[5 + 2 * i2]]
            r0 = tok0 + s * 128
            dma1(SP, xres[0:n, :], xsrc[r0:r0 + n, :], AB[2 * i2], reads=[xbuf(s)], writes=xrb)
            for hf, (pp_, ppb, sl_) in enumerate([(pA, B_pA, slot0), (pB, B_pB, slot1)]):
                mm_acc(pp_[0:n, :], ppb, [(mergedT[:, dc, s * 128:s * 128 + n], WS[sl_][:, dc, :]) for dc in range(8)], [B_QT, B_WS[sl_]])
                fw.op(DVE, (lambda hf, pp_, n, r_ap, xres: lambda e: e.scalar_tensor_tensor(
                    out=r_ap[0:n, hf * 512:(hf + 1) * 512], in0=xres[0:n, hf * 512:(hf + 1) * 512], scalar=ALPHA, in1=pp_[0:n, :],
                    op0=ALU.mult, op1=ALU.add))(hf, pp_, n, r_ap, xres), reads=xrb + [ppb], writes=rbufs)
            ln_rows(r_ap, rbufs, n, arf(8, 1024), arf(10, 1024), [AB[8], AB[9], AB[10], AB[11]])
            dma1(SP, ydst[r0:r0 + n, :], r_ap[0:n, :], AB[4 + 2 * i2], reads=rbufs, writes=[xbuf(s)], is_out=last_layer)
        assert nxt["k"] == NCH and not wq, (nxt, wq)
        wfree.extend([slot0, slot1])

    for l in range(L):
        fw.dma(POOL, [(lambda l: lambda e: e.dma_start(out=wpool_sb, in_=w_pool[l].rearrange("g c d -> c g d")))(l)], B_wpool, writes=[B_wpool])
        if l + 1 < L:
            cast_weights(l + 1)
        if with_sample:
            fw.new_epoch()
            tile_pass(l, 0, True)
        for ti in range(NT):
            if ti % 3 == 0:
                fw.new_epoch()
            tile_pass(l, ti, False)

    fw.emit()
    return nc, fw


def _rope_table(pos):
    half = 32
    inv = (np.float32(10000.0) ** (-np.arange(half, dtype=np.float32) / np.float32(half))).astype(np.float32)
    ang = pos.astype(np.float32)[:, None] * inv[None, :]
    c = np.cos(ang).astype(np.float32)
    s = np.sin(ang).astype(np.float32)
    return np.ascontiguousarray(np.concatenate([c, c, -s, s], axis=1).astype(np.float32))


def make_in_maps(inputs, L, SEQ, n_cores=8, with_sample=True):
    f = lambda a: np.ascontiguousarray(np.asarray(a, dtype=np.float32))
    cs_p = _rope_table(np.arange(SEQ))
    cs_s = np.ascontiguousarray(np.tile(_rope_table(PAST + np.arange(TS)), (NSEQ_S, 1)))
    invc = np.zeros((4, 16), np.float32)
    for g in range(4):
        w = 2 ** (g + 1)
        invc[g] = 1.0 / np.minimum(np.arange(16) + 1, w)
    common = dict(
        ln_in_g=f(inputs["ln_in_g"]).reshape(1, D), ln_in_b=f(inputs["ln_in_b"]).reshape(1, D),
        w_in=f(inputs["w_in"])[:L], w_pool=f(inputs["w_pool"])[:L], pool_scale=f(inputs["pool_scale"])[:L],
        lambda_qk=f(inputs["lambda_qk"])[:L].reshape(L, 256), subln_w=f(inputs["subln_w"])[:L],
        w_a=f(inputs["w_a"])[:L], w_b=f(inputs["w_b"])[:L], w_o=f(inputs["w_o"])[:L],
        ln_g=f(inputs["ln_g"])[:L], ln_b=f(inputs["ln_b"])[:L],
        cs_p=cs_p, cs_s=cs_s, idn_bf=np.eye(128).astype(ml_dtypes.bfloat16), idn_f=np.eye(128, dtype=np.float32),
        invc=invc.reshape(1, 64),
    )
    xp = f(inputs["x_prompt"])
    xs = f(inputs["x_sample"])
    ck = np.asarray(inputs["cache_k"], dtype=np.float32)
    cv = np.asarray(inputs["cache_v"], dtype=np.float32)
    sp = np.asarray(inputs["state_pool"], dtype=np.float32)
    maps = []
    for c in range(n_cores):
        m = dict(common)
        m["xp"] = np.ascontiguousarray(xp[c, :SEQ])
        m["xs"] = np.ascontiguousarray(xs[4 * c:4 * c + 4].reshape(64, D))
        m["ck"] = np.ascontiguousarray(ck[:L, 4 * c:4 * c + 4].reshape(L, 4, PAST, D))
        m["cv"] = np.ascontiguousarray(cv[:L, 4 * c:4 * c + 4].reshape(L, 4, PAST, D))
        m["spool"] = np.ascontiguousarray(sp[:L, 4 * c:4 * c + 4].reshape(L, 60, 512))
        maps.append(m)
    return maps


def gather(results, L, SEQ, n_cores=8):
    y_p = np.stack([r["y_p"] for r in results]).reshape(n_cores, SEQ, D)
    y_s = np.stack([r["y_s"] for r in results]).reshape(n_cores * 4, TS, D)
    k_p = np.stack([r["k_p"] for r in results], axis=1).reshape(L, n_cores, SEQ, HEADS, 128)
    v_p = np.stack([r["v_p"] for r in results], axis=1).reshape(L, n_cores, SEQ, HEADS, 128)
    pool_p = np.stack([r["pool_p"] for r in results], axis=1).reshape(L, n_cores, 15, 512)
    k_s = np.stack([r["k_s"] for r in results], axis=1).reshape(L, n_cores * 4, TS, HEADS, 128)
    v_s = np.stack([r["v_s"] for r in results], axis=1).reshape(L, n_cores * 4, TS, HEADS, 128)
    pool_s = np.stack([r["pool_s"] for r in results], axis=1).reshape(L, n_cores * 4, 15, 512)
    return tuple(np.ascontiguousarray(a.astype(np.float32)) for a in (y_p, y_s, k_p, v_p, pool_p, k_s, v_s, pool_s))


_CACHE = {}


def kernel(**inputs):
    L, SEQ = DEPTH_FULL, 4096
    if "nc" not in _CACHE:
        _CACHE["nc"] = build_program(L, SEQ)[0]
    nc = _CACHE["nc"]
    maps = make_in_maps(inputs, L, SEQ)
    res = run_bass_kernel_spmd(nc, maps, core_ids=list(range(8)))
    return gather(res.results, L, SEQ)
```

```python
import math
import numpy as np
import ml_dtypes
import concourse.bass as bass
import concourse.mybir as mybir
from concourse.bass_utils import run_bass_kernel_spmd

F32 = mybir.dt.float32
BF16 = mybir.dt.bfloat16
AF = mybir.ActivationFunctionType
ALU = mybir.AluOpType

D = 1024
TT = 512
HEADS = 8
DEPTH_FULL = 4
ALPHA = (2 * DEPTH_FULL) ** 0.25
LN_EPS = 1e-5
RMS_EPS = 1e-5
PAST = 1024
NSEQ_S = 4
TS = 16
NCH = 19


class Buf:
    __slots__ = ("name", "w", "r", "dsem", "dcnt")

    def __init__(self, name):
        self.name = name
        self.w = None
        self.r = {}
        self.dsem = None
        self.dcnt = 0


class Eng:
    def __init__(self, fw, name, is_pe=False):
        self.fw = fw
        self.name = name
        self.is_pe = is_pe
        self.prog = []
        self.seen = {}
        self.sem = None
        self.cnt = 0
        self.epoch = 0
        self.new_epoch()

    def new_epoch(self):
        self.sem = self.fw.nc.alloc_semaphore(f"s_{self.name}_{self.epoch}")
        self.fw.nsem += 1
        self.epoch += 1
        self.cnt = 0


class FW:
    def __init__(self, nc):
        self.nc = nc
        self.nsem = 0
        self.pe = Eng(self, "pe", is_pe=True)
        self.act = Eng(self, "act")
        self.dve = Eng(self, "dve")
        self.pool = Eng(self, "pool")
        self.sp = Eng(self, "sp")
        self.out_events = []
        self.ninst = 0

    def new_epoch(self):
        for e in (self.pe, self.act, self.dve, self.pool):
            if e.cnt > 0:
                e.new_epoch()

    def _collect(self, E, reads, writes):
        waits = []

        def need(ev, same_ok):
            if ev is None:
                return
            sem, val = ev
            if sem is E.sem and same_ok:
                return
            k = id(sem)
            if E.seen.get(k, 0) >= val:
                return
            E.seen[k] = val
            waits.append((sem, val))

        for b in reads:
            need(b.w, E.is_pe)
        for b in writes:
            need(b.w, E.is_pe)
            for ev in b.r.values():
                need(ev, True)
        return waits

    def op(self, E, fn, reads=(), writes=(), sig=True):
        waits = self._collect(E, reads, writes)
        self.ninst += 1
        if sig:
            E.cnt += 1
            ev = (E.sem, E.cnt)
            E.prog.append((waits, fn, (E.sem, 1)))
        else:
            ev = (E.sem, E.cnt + 1)
            E.prog.append((waits, fn, None))
        for b in reads:
            b.r[id(E.sem)] = ev
        for b in writes:
            b.w = ev
            b.r = {}
        return ev

    def dma(self, Q, fns, sembuf, reads=(), writes=(), is_out=False):
        waits = self._collect(Q, reads, writes)
        if sembuf.dsem is None:
            sembuf.dsem = self.nc.alloc_semaphore(f"d_{sembuf.name}")
            self.nsem += 1
        for i, fn in enumerate(fns):
            sembuf.dcnt += 16
            self.ninst += 1
            Q.prog.append((waits if i == 0 else [], fn, (sembuf.dsem, 16)))
        ev = (sembuf.dsem, sembuf.dcnt)
        for b in reads:
            b.r[id(sembuf.dsem)] = ev
        for b in writes:
            b.w = ev
            b.r = {}
        if is_out:
            self.out_events.append(ev)
        return ev

    def emit(self):
        nc = self.nc
        last = {}
        for sem, val in self.out_events:
            k = id(sem)
            if k not in last or last[k][1] < val:
                last[k] = (sem, val)
        self.sp.prog.append((list(last.values()), None, None))

        def replay(E, eng):
            for waits, fn, inc in E.prog:
                for sem, val in waits:
                    eng.wait_ge(sem, val)
                if fn is None:
                    continue
                ins = fn(eng)
                if inc is not None:
                    ins.then_inc(inc[0], inc[1])

        with nc.Block() as block:
            @block.tensor
            def _(eng):
                replay(self.pe, eng)

            @block.scalar
            def _(eng):
                replay(self.act, eng)

            @block.vector
            def _(eng):
                replay(self.dve, eng)

            @block.gpsimd
            def _(eng):
                replay(self.pool, eng)

            @block.sync
            def _(eng):
                replay(self.sp, eng)


def build_program(L, SEQ, with_sample=True):
    import os as _os
    _STOP = int(_os.environ.get('KSTOP', '99'))
    NT = SEQ // TT
    KTW = max(SEQ, NSEQ_S * PAST)
    NKT = KTW // 128
    nc = bass.Bass("TRN2", target_bir_lowering=False)
    fw = FW(nc)
    PE, ACT, DVE, POOL, SP = fw.pe, fw.act, fw.dve, fw.pool, fw.sp

    def din(name, shape, dt=F32):
        return nc.dram_tensor(name, shape, dt, kind="ExternalInput").ap()

    def dout(name, shape, dt=F32):
        return nc.dram_tensor(name, shape, dt, kind="ExternalOutput").ap()

    xp = din("xp", [SEQ, D])
    xs_in = din("xs", [64, D])
    ck = din("ck", [L, NSEQ_S, PAST, D])
    cv = din("cv", [L, NSEQ_S, PAST, D])
    spool = din("spool", [L, 60, 512])
    ln_in_g = din("ln_in_g", [1, D])
    ln_in_b = din("ln_in_b", [1, D])
    w_in = din("w_in", [L, D, 7168])
    w_pool = din("w_pool", [L, 4, 128, 128])
    pool_scale = din("pool_scale", [L, 512])
    lambda_qk = din("lambda_qk", [L, 256])
    subln_w = din("subln_w", [L, 128])
    w_a = din("w_a", [L, 512, D])
    w_b = din("w_b", [L, D, D])
    w_o = din("w_o", [L, D, D])
    ln_g = din("ln_g", [L, D])
    ln_b = din("ln_b", [L, D])
    cs_p = din("cs_p", [SEQ, 128])
    cs_s = din("cs_s", [64, 128])
    idn_bf = din("idn_bf", [128, 128], BF16)
    idn_f = din("idn_f", [128, 128])
    invc = din("invc", [1, 64])

    y_p = dout("y_p", [SEQ, D])
    y_s = dout("y_s", [64, D])
    k_p = dout("k_p", [L, SEQ, D])
    v_p = dout("v_p", [L, SEQ, D])
    pool_p = dout("pool_p", [L, 15, 512])
    k_s = dout("k_s", [L, 64, D])
    v_s = dout("v_s", [L, 64, D])
    pool_s = dout("pool_s", [L, 60, 512])

    xcur_p = nc.dram_tensor("xcur_p", [SEQ, D], F32, kind="Internal").ap()
    xcur_s = nc.dram_tensor("xcur_s", [64, D], F32, kind="Internal").ap()
    wsc = nc.dram_tensor("wsc", [L, NCH, 128, 4096], BF16, kind="Internal").ap()
    B_wsc = [Buf(f"wsc{l}") for l in range(L)]
    B_xcur_p = [Buf(f"xcp{i}") for i in range(SEQ // 128)]
    B_xcur_s = Buf("xcs")

    def sb(name, shape, dt=F32):
        return nc.alloc_sbuf_tensor(name, shape, dt).ap()

    KT = sb("KT", [128, HEADS, KTW], BF16)
    VX = sb("VX", [128, NKT, HEADS, 128], BF16)
    xT = sb("xT", [128, 8, TT], BF16)
    QT = sb("QT", [128, 8, TT], BF16)
    ybT = sb("ybT", [128, 8, TT], BF16)
    WS = [sb(f"WS{i}", [128, 8, 512], BF16) for i in range(2)]
    arena = sb("arena", [128, 12 * 512], F32)
    arena_bf = arena.bitcast(BF16)
    tmpA = sb("tmpA", [128, 528], F32)
    tmpB = sb("tmpB", [128, 528], F32)
    hist = sb("hist", [128, 4, 16], F32)
    cs_sb = sb("cs_sb", [128, 4, 128], F32)
    ident = sb("ident", [128, 128], BF16)
    identf = sb("identf", [128, 128], F32)
    ones_bf = sb("ones_bf", [128, 128], BF16)
    ones_f = sb("ones_f", [128, 128], F32)
    zeros_bf = sb("zeros_bf", [128, 256], BF16)
    neghalf = sb("neghalf", [128, 2], F32)
    epsc = sb("epsc", [128, 2], F32)
    invc_sb = sb("invc_sb", [128, 4, 16], F32)
    wpool_sb = sb("wpool_sb", [128, 4, 128], BF16)
    pscale_all = sb("pscale_all", [128, L, 4], F32)
    subln_all = sb("subln_all", [128, L], F32)
    lq_all = arena[:, 0:L * 256].rearrange("p (l c) -> p l c", l=L)
    lam_tmp = sb("lam_tmp", [128, 4 * L + 8], F32)
    neglam_all = sb("neglam_all", [128, L], F32)
    stats = sb("stats", [128, 2, 6], F32)
    mv = sb("mv", [128, 8], F32)
    KTn = sb("KTn", [128, HEADS, 64], BF16)
    VN = sb("VN", [128, HEADS, 128], BF16)
    PTn = sb("PTn", [128, 2, HEADS, TS], BF16)
    vbf_s = tmpA.bitcast(BF16)[0:64, 0:D]

    B_KT, B_VX, B_xT, B_QT, B_ybT = Buf("KT"), Buf("VX"), Buf("xT"), Buf("QT"), Buf("ybT")
    B_WS = [Buf(f"WS{i}") for i in range(2)]
    AB = [Buf(f"ar{i}") for i in range(12)]
    B_tmpA, B_tmpB, B_hist, B_cs = Buf("tmpA"), Buf("tmpB"), Buf("hist"), Buf("cs")
    B_const = Buf("const")
    B_lay = Buf("laycst")
    B_stats, B_mv = Buf("stats"), Buf("mv")
    B_KTn, B_VN, B_PTn = Buf("KTn"), Buf("VN"), Buf("PTn")
    B_wpool = Buf("wpool")
    PT = [Buf(f"pt{i}") for i in range(4)]

    def merge_ev(dst, ev):
        if ev is None:
            return
        k = id(ev[0])
        if k not in dst.r or dst.r[k][1] < ev[1]:
            dst.r[k] = ev

    pA = nc.alloc_psum_tensor("pA", [128, 512], F32).ap()
    pB = nc.alloc_psum_tensor("pB", [128, 512], F32).ap()
    pT = nc.alloc_psum_tensor("pT", [128, 1024], BF16).ap()
    pO = nc.alloc_psum_tensor("pO", [128, 512], F32).ap()
    pSX = nc.alloc_psum_tensor("pSX", [128, 2, 512], F32).ap()
    pSY = nc.alloc_psum_tensor("pSY", [128, 2, 512], F32).ap()
    B_pA, B_pB, B_pT, B_pO, B_pSX, B_pSY = (Buf(n) for n in ["pA", "pB", "pT", "pO", "pSX", "pSY"])

    def arf(slot, ncols, off=0):
        return arena[:, slot * 512 + off: slot * 512 + off + ncols]

    def arb(slot, ncols, off=0):
        return arena_bf[:, slot * 1024 + off: slot * 1024 + off + ncols]

    def cp(E, out, in_, reads, writes):
        if E is ACT:
            return fw.op(E, lambda e: e.activation(out=out, in_=in_, func=AF.Copy), reads=reads, writes=writes)
        return fw.op(E, lambda e: e.tensor_copy(out=out, in_=in_), reads=reads, writes=writes)

    def dma1(Q, out, in_, sembuf, reads=(), writes=(), is_out=False, slow=False):
        if slow:
            f = lambda e: e.dma_start(out=out, in_=in_, allow_slow_non_contiguous=True)
        else:
            f = lambda e: e.dma_start(out=out, in_=in_)
        return fw.dma(Q, [f], sembuf, reads=reads, writes=writes, is_out=is_out)

    dma1(SP, ident, idn_bf, B_const, writes=[B_const])
    dma1(SP, identf, idn_f, B_const, writes=[B_const])
    dma1(SP, invc_sb.rearrange("p g t -> p (g t)"), invc.partition_broadcast(128), B_const, writes=[B_const])
    dma1(SP, arena[:, 0:L * 256], lambda_qk.rearrange("(o l) c -> o (l c)", o=1).partition_broadcast(128),
         AB[0], writes=[AB[0], AB[1]])
    dma1(SP, pscale_all, pool_scale.rearrange("l (g p) -> p l g", p=128), B_lay, writes=[B_lay], slow=True)
    dma1(SP, subln_all, subln_w.rearrange("l p -> p l"), B_lay, writes=[B_lay], slow=True)
    fw.op(DVE, lambda e: e.memset(ones_bf, 1.0), writes=[B_const])
    fw.op(DVE, lambda e: e.memset(ones_f, 1.0), writes=[B_const])
    fw.op(DVE, lambda e: e.memset(zeros_bf, 0.0), writes=[B_const])
    fw.op(DVE, lambda e: e.memset(neghalf, -0.5), writes=[B_const])
    fw.op(DVE, lambda e: e.memset(epsc[:, 0:1], LN_EPS), writes=[B_const])
    fw.op(DVE, lambda e: e.memset(epsc[:, 1:2], RMS_EPS), writes=[B_const])
    fw.op(DVE, lambda e: e.memset(VN, 0.0), writes=[B_VN])
    fw.op(DVE, lambda e: e.memset(PTn, 0.0), writes=[B_PTn])
    for l in range(L):
        lam_init = 0.8 - 0.6 * math.exp(-0.3 * l)
        for i in range(2):
            fw.op(DVE, (lambda l, i: lambda e: e.tensor_tensor(out=tmpA[:, 0:64], in0=lq_all[:, l, 128 * i:128 * i + 64],
                                                                 in1=lq_all[:, l, 128 * i + 64:128 * i + 128], op=ALU.mult))(l, i),
                  reads=[AB[0], AB[1]], writes=[B_tmpA])
            fw.op(DVE, (lambda l, i: lambda e: e.reduce_sum(out=lam_tmp[:, 4 * l + i:4 * l + i + 1], in_=tmpA[:, 0:64],
                                                              axis=mybir.AxisListType.X))(l, i),
                  reads=[B_tmpA], writes=[B_lay])
        fw.op(ACT, (lambda l: lambda e: e.activation(out=lam_tmp[:, 4 * l + 2:4 * l + 4], in_=lam_tmp[:, 4 * l:4 * l + 2], func=AF.Exp))(l),
              reads=[B_lay], writes=[B_lay])
        fw.op(DVE, (lambda l, li: lambda e: e.scalar_tensor_tensor(out=neglam_all[:, l:l + 1], in0=lam_tmp[:, 4 * l + 3:4 * l + 4],
                                                                     scalar=-li, in1=lam_tmp[:, 4 * l + 2:4 * l + 3],
                                                                     op0=ALU.add, op1=ALU.subtract))(l, lam_init),
              reads=[B_lay], writes=[B_lay])
        fw.op(DVE, (lambda l, li: lambda e: e.tensor_scalar(out=subln_all[:, l:l + 1], in0=subln_all[:, l:l + 1], scalar1=1.0 - li,
                                                              scalar2=None, op0=ALU.mult))(l, lam_init),
              reads=[B_lay], writes=[B_lay])

    def wsc_view(l, ci, kc, ncol):
        return wsc[l, ci].rearrange("p (k c) -> p k c", k=kc)

    def cast_weights(l):
        fns = []
        for ci in range(14):
            src = w_in[l].rearrange("(k p) c -> p k c", p=128)[:, :, ci * 512:(ci + 1) * 512]
            dst = wsc_view(l, ci, 8, 512)
            fns.append((lambda s, d: lambda e: e.dma_start(out=d, in_=s))(src, dst))
        src = w_a[l].rearrange("(g p) c -> p g c", p=128)
        fns.append((lambda s, d: lambda e: e.dma_start(out=d, in_=s))(src, wsc_view(l, 14, 4, 1024)))
        for hb in range(2):
            src = w_b[l].rearrange("(k p) c -> p k c", p=128)[:, :, hb * 512:(hb + 1) * 512]
            fns.append((lambda s, d: lambda e: e.dma_start(out=d, in_=s))(src, wsc_view(l, 15 + hb, 8, 512)))
        for hb in range(2):
            src = w_o[l].rearrange("(k p) c -> p k c", p=128)[:, :, hb * 512:(hb + 1) * 512]
            fns.append((lambda s, d: lambda e: e.dma_start(out=d, in_=s))(src, wsc_view(l, 17 + hb, 8, 512)))
        fw.dma(POOL, fns, B_wsc[l], writes=[B_wsc[l]])

    cast_weights(0)

    CH_ORDER = [0, 1, 2, 3, 4, 5, 6, 7, 8, 9, 14, 10, 11, 15, 12, 16, 13, 17, 18]
    wfree = [0, 1]

    def wload(l, k, slot):
        ci = CH_ORDER[k]
        dst = WS[slot].rearrange("p k c -> p (k c)")
        dma1(SP, dst, wsc[l, ci], B_WS[slot], reads=[B_wsc[l]], writes=[B_WS[slot]])

    def ln_rows(r_ap, r_bufs, n, g_ap, b_ap, pbufs):
        for hlf in range(2):
            fw.op(DVE, (lambda hlf: lambda e: e.bn_stats(out=stats[0:n, hlf, :], in_=r_ap[0:n, hlf * 512:(hlf + 1) * 512]))(hlf),
                  reads=r_bufs, writes=[B_stats])
        fw.op(DVE, lambda e: e.bn_aggr(out=mv[0:n, 0:2], in_=stats[0:n].rearrange("p a b -> p (a b)")), reads=[B_stats], writes=[B_mv])
        fw.op(ACT, lambda e: e.activation(out=mv[0:n, 2:3], in_=mv[0:n, 1:2], func=AF.Ln, bias=epsc[0:n, 0:1], scale=1.0),
              reads=[B_mv, B_const], writes=[B_mv])
        fw.op(ACT, lambda e: e.activation(out=mv[0:n, 3:4], in_=mv[0:n, 2:3], func=AF.Exp, scale=-0.5),
              reads=[B_mv], writes=[B_mv])
        fw.op(DVE, lambda e: e.scalar_tensor_tensor(out=mv[0:n, 4:5], in0=mv[0:n, 0:1], scalar=-1.0, in1=mv[0:n, 3:4],
                                                    op0=ALU.mult, op1=ALU.mult), reads=[B_mv], writes=[B_mv])
        fw.op(ACT, lambda e: e.activation(out=r_ap[0:n, :], in_=r_ap[0:n, :], func=AF.Identity, bias=mv[0:n, 4:5], scale=mv[0:n, 3:4]),
              reads=r_bufs + [B_mv], writes=r_bufs)
        fw.op(DVE, lambda e: e.tensor_tensor(out=r_ap[0:n, :], in0=r_ap[0:n, :], in1=g_ap[0:n, :], op=ALU.mult),
              reads=r_bufs + pbufs, writes=r_bufs)
        fw.op(DVE, lambda e: e.tensor_tensor(out=r_ap[0:n, :], in0=r_ap[0:n, :], in1=b_ap[0:n, :], op=ALU.add),
              reads=r_bufs + pbufs, writes=r_bufs)

    def load_lnp(g_src, b_src):
        dma1(SP, arf(8, 1024), g_src.partition_broadcast(128), AB[8], writes=[AB[8], AB[9]])
        dma1(SP, arf(10, 1024), b_src.partition_broadcast(128), AB[10], writes=[AB[10], AB[11]])

    load_lnp(ln_in_g, ln_in_b)
    rows = [(xp[i * 128:(i + 1) * 128, :], xcur_p[i * 128:(i + 1) * 128, :], 128, B_xcur_p[i]) for i in range(SEQ // 128)]
    if with_sample:
        rows.append((xs_in, xcur_s, 64, B_xcur_s))
    for i, (src, dst, n, bdst) in enumerate(rows):
        sl = 2 * (i % 4)
        r_ap = arf(sl, 1024)
        rb = [AB[sl], AB[sl + 1]]
        dma1(SP, r_ap[0:n, :], src, AB[sl], writes=rb)
        ln_rows(r_ap, rb, n, arf(8, 1024), arf(10, 1024), [AB[8], AB[9], AB[10], AB[11]])
        dma1(SP, dst, r_ap[0:n, :], AB[sl], reads=rb, writes=[bdst])

    pp = {"i": 0}
    prefetched = set()

    def next_ps():
        pp["i"] += 1
        return (pA, B_pA) if pp["i"] % 2 else (pB, B_pB)

    PS_PAIRS = [(pA, B_pA, pB, B_pB), (pSX[:, 0, :], B_pSX, pSX[:, 1, :], B_pSX), (pSY[:, 0, :], B_pSY, pSY[:, 1, :], B_pSY)]

    def next_pair():
        pp["i"] += 1
        return PS_PAIRS[pp["i"] % 3]

    def mm_acc(out_ap, out_buf, pairs, reads):
        n = len(pairs)
        for i, (lt, rh) in enumerate(pairs):
            fw.op(PE, (lambda lt, rh, i: lambda e: e.matmul(out_ap, lhsT=lt, rhs=rh, start=(i == 0), stop=(i == n - 1)))(lt, rh, i),
                  reads=reads, writes=[out_buf], sig=(i == n - 1))

    def sig_gate(ps_ap, ps_buf, nt, gt, gt_buf, silu):
        fw.op(ACT, lambda e: e.activation(out=gt[:, 0:nt], in_=ps_ap, func=AF.Tanh, scale=0.5), reads=[ps_buf], writes=[gt_buf])
        fw.op(DVE, lambda e: e.tensor_scalar(out=gt[:, 0:nt], in0=gt[:, 0:nt], scalar1=0.5, scalar2=0.5, op0=ALU.mult, op1=ALU.add),
              reads=[gt_buf], writes=[gt_buf])
        if silu:
            fw.op(DVE, lambda e: e.tensor_tensor(out=gt[:, 0:nt], in0=ps_ap, in1=gt[:, 0:nt], op=ALU.mult),
                  reads=[ps_buf, gt_buf], writes=[gt_buf])

    def attn_epilogue1(W_):
        rinv, t1 = arf(2, 2 * W_), arf(3, 2 * W_)
        fw.op(ACT, lambda e: e.activation(out=rinv, in_=pB[:, 0:2 * W_], func=AF.Ln), reads=[B_pB], writes=[AB[2]])
        fw.op(ACT, lambda e: e.activation(out=rinv, in_=rinv, func=AF.Exp, scale=-1.0), reads=[AB[2]], writes=[AB[2]])
        fw.op(DVE, lambda e: e.tensor_tensor(out=t1, in0=pO[:, 0:2 * W_], in1=rinv, op=ALU.mult), reads=[B_pO, AB[2]], writes=[AB[3]])

    def attn_epilogue2(l, W_):
        t1 = arf(3, 2 * W_)
        o, sq = arf(4, W_), arf(4, W_, 256)
        rs, rstd = arf(5, W_), arf(5, W_, 256)
        fw.op(DVE, lambda e: e.scalar_tensor_tensor(out=o, in0=t1[:, W_:2 * W_], scalar=neglam_all[:, l:l + 1], in1=t1[:, 0:W_],
                                                    op0=ALU.mult, op1=ALU.add), reads=[AB[3], B_lay], writes=[AB[4]])
        fw.op(ACT, lambda e: e.activation(out=sq, in_=o, func=AF.Square), reads=[AB[4]], writes=[AB[4]])
        fw.op(PE, lambda e: e.matmul(pA[:, 0:W_], lhsT=ones_f, rhs=sq, start=True, stop=True), reads=[AB[4], B_const], writes=[B_pA])
        fw.op(ACT, lambda e: e.activation(out=rs, in_=pA[:, 0:W_], func=AF.Ln, bias=epsc[:, 1:2], scale=1.0 / 128),
              reads=[B_pA, B_const], writes=[AB[5]])
        fw.op(ACT, lambda e: e.activation(out=rstd, in_=rs, func=AF.Exp, scale=-0.5), reads=[AB[5]], writes=[AB[5]])
        fw.op(DVE, lambda e: e.tensor_tensor(out=o, in0=o, in1=rstd, op=ALU.mult), reads=[AB[4], AB[5]], writes=[AB[4]])
        return o

    def tile_pass(l, ti, sample):
        last_layer = (l == L - 1)
        if sample:
            ntok, subs = 64, [(0, 64)]
        else:
            ntok, subs = TT, [(s, 128) for s in range(4)]
        nsub = len(subs)
        tok0 = 0 if sample else ti * TT
        xsrc = xcur_s if sample else xcur_p
        ydst = (y_s if sample else y_p) if last_layer else xsrc

        def xbuf(s):
            return B_xcur_s if sample else B_xcur_p[ti * 4 + s]

        wq = []
        nxt = {"k": 0}

        def wtop():
            while wfree and nxt["k"] < NCH:
                sl_ = wfree.pop(0)
                wload(l, nxt["k"], sl_)
                wq.append(sl_)
                nxt["k"] += 1

        def wnext():
            wtop()
            return wq.pop(0)

        def wrel(*slots):
            for sl_ in slots:
                wfree.append(sl_)
            wtop()

        wtop()

        if sample:
            for b in range(NSEQ_S):
                for jt in range(8):
                    i = b * 8 + jt
                    sl = i % 4
                    kst = arb(sl, 1024)
                    fw.dma(POOL, [(lambda b, jt, kst: lambda e: e.dma_start(out=kst, in_=ck[l, b, jt * 128:(jt + 1) * 128, :]))(b, jt, kst)],
                           AB[sl], writes=[AB[sl]])
                    fw.dma(POOL, [(lambda b, jt: lambda e: e.dma_start(out=VX[:, b * 8 + jt, :, 0:128],
                                                                         in_=cv[l, b, jt * 128:(jt + 1) * 128, :].rearrange("p (h v) -> p h v", h=8)))(b, jt)],
                           B_VX, writes=[B_VX])
                    for h in range(8):
                        fw.op(PE, (lambda h, kst: lambda e: e.transpose(out=pT[:, h * 128:(h + 1) * 128], in_=kst[:, h * 128:(h + 1) * 128],
                                                                        identity=ident))(h, kst),
                              reads=[AB[sl], B_const], writes=[B_pT], sig=(h == 7))
                    E = ACT if i % 2 == 0 else DVE
                    cp(E, KT[:, :, b * PAST + jt * 128: b * PAST + (jt + 1) * 128], pT.rearrange("p (h k) -> p h k", h=8),
                       reads=[B_pT], writes=[B_KT])

        for s, n in subs:
            sl = 2 * (s % 2)
            xin = arf(sl, 1024)
            xbf = arb(4 + (s % 2), 1024)
            r0 = tok0 + s * 128
            if (l, ti, s) in prefetched and not sample:
                prefetched.discard((l, ti, s))
            else:
                dma1(SP, xin[0:n, :], xsrc[r0:r0 + n, :], AB[sl], reads=[xbuf(s)], writes=[AB[sl], AB[sl + 1]])
            cp(ACT, xbf[0:n, :], xin[0:n, :], reads=[AB[sl], AB[sl + 1]], writes=[AB[4 + s % 2]])
            for c in range(8):
                fw.op(PE, (lambda c, xbf, n: lambda e: e.transpose(out=pT[:, c * 128:c * 128 + n], in_=xbf[0:n, c * 128:(c + 1) * 128],
                                                                  identity=ident[0:n, 0:n]))(c, xbf, n),
                      reads=[AB[4 + s % 2], B_const], writes=[B_pT], sig=(c == 7))
            cp(DVE, xT[:, :, s * 128:s * 128 + n], pT.rearrange("p (c k) -> p c k", c=8)[:, :, 0:n], reads=[B_pT], writes=[B_xT])
        if sample:
            dma1(SP, cs_sb[0:64, 0, :], cs_s, B_cs, writes=[B_cs])
        else:
            dma1(SP, cs_sb, cs_p[tok0:tok0 + TT, :].rearrange("(s p) c -> p s c", p=128), B_cs, writes=[B_cs])

        if _STOP == 1:
            wfree[:] = [0, 1]
            return
        Wd = 128 if sample else 528
        pxT = arena[:, 0:4 * Wd].rearrange("p (g w) -> p g w", g=4)
        pxb = [AB[0], AB[1], AB[2], AB[3], AB[4]] if not sample else [AB[0]]
        pooled = arb(6, 4 * TT).rearrange("p (g t) -> p g t", g=4)
        pooledb = [AB[6], AB[7]]
        yaT = arb(10, 4 * TT).rearrange("p (g t) -> p g t", g=4)
        yab = [AB[10], AB[11]]
        gt0, gt1 = arf(8, 512), arf(9, 512)

        def outview(ap2d):
            if sample:
                return ap2d.rearrange("p (b w) -> p b w", b=4)[:, :, 16:32]
            return ap2d[:, 16:528]

        def as_out(ap2d):
            if sample:
                return ap2d.rearrange("p (b t) -> p b t", b=4)
            return ap2d

        if sample:
            fw.op(DVE, lambda e: e.memset(arena[:, 0:4 * Wd], 0.0), writes=pxb)
            sp_sb = arf(2, 512)
            dma1(SP, sp_sb[0:60, :], spool[l], AB[2], writes=[AB[2]])
            for g in range(4):
                fw.op(PE, (lambda g: lambda e: e.transpose(out=pA[:, g * 64:g * 64 + 60], in_=sp_sb[0:60, g * 128:(g + 1) * 128],
                                                           identity=identf[0:60, 0:60]))(g),
                      reads=[AB[2], B_const], writes=[B_pA], sig=(g == 3))
            for g in range(4):
                cp(ACT, pxT[:, g, :].rearrange("p (b w) -> p b w", b=4)[:, :, 1:16],
                   pA[:, g * 64:g * 64 + 60].rearrange("p (b j) -> p b j", b=4), reads=[B_pA], writes=pxb)
        elif ti == 0:
            fw.op(DVE, lambda e: e.memset(pxT[:, :, 0:16], 0.0), writes=pxb)
        else:
            cp(DVE, pxT[:, :, 0:16], hist, reads=[B_hist], writes=pxb)

        slot = wnext()
        Wc = WS[slot]
        for e_ in range(4):
            ps, psb = next_ps()
            mm_acc(ps[:, 0:ntok], psb, [(Wc[:, d, e_ * 128:(e_ + 1) * 128], xT[:, d, 0:ntok]) for d in range(8)], [B_WS[slot], B_xT])
            cp(ACT, outview(pxT[:, e_, :]), as_out(ps[:, 0:ntok]), reads=[psb], writes=pxb)
        if sample or ti == NT - 1:
            s_last, n_last = subs[-1]
            ps, psb = next_ps()
            mm_acc(ps[0:n_last, :], psb, [(xT[:, d, s_last * 128:s_last * 128 + n_last], Wc[:, d, :]) for d in range(8)], [B_WS[slot], B_xT])
            cp(ACT, gt1[0:n_last, :], ps[0:n_last, :], reads=[psb], writes=[AB[9]])
            if sample:
                fw.dma(SP, [(lambda b: lambda e: e.dma_start(out=pool_s[l, 15 * b:15 * b + 15, :], in_=gt1[16 * b + 1:16 * b + 16, :]))(b)
                            for b in range(4)], AB[9], reads=[AB[9]], is_out=True)
            else:
                dma1(SP, pool_p[l], gt1[113:128, :], AB[9], reads=[AB[9]], is_out=True)
        wrel(slot)
        if not sample:
            cp(POOL, hist, pxT[:, :, 512:528], reads=pxb, writes=[B_hist])
        for g in range(4):
            u = pxT[:, g, :]
            w = 2 ** (g + 1)

            def add(out, a, b_, rd, wr):
                fw.op(DVE, lambda e: e.tensor_tensor(out=out, in0=a, in1=b_, op=ALU.add), reads=rd, writes=wr)

            if g == 0:
                add(tmpA[:, 16:Wd], u[:, 16:Wd], u[:, 15:Wd - 1], pxb, [B_tmpA])
                s_ap, s_b = tmpA, B_tmpA
            elif g == 1:
                add(tmpA[:, 14:Wd], u[:, 14:Wd], u[:, 13:Wd - 1], pxb, [B_tmpA])
                add(tmpB[:, 16:Wd], tmpA[:, 16:Wd], tmpA[:, 14:Wd - 2], [B_tmpA], [B_tmpB])
                s_ap, s_b = tmpB, B_tmpB
            elif g == 2:
                add(tmpA[:, 10:Wd], u[:, 10:Wd], u[:, 9:Wd - 1], pxb, [B_tmpA])
                add(tmpB[:, 12:Wd], tmpA[:, 12:Wd], tmpA[:, 10:Wd - 2], [B_tmpA], [B_tmpB])
                add(tmpA[:, 16:Wd], tmpB[:, 16:Wd], tmpB[:, 12:Wd - 4], [B_tmpB], [B_tmpA])
                s_ap, s_b = tmpA, B_tmpA
            else:
                add(tmpA[:, 2:Wd], u[:, 2:Wd], u[:, 1:Wd - 1], pxb, [B_tmpA])
                add(tmpB[:, 4:Wd], tmpA[:, 4:Wd], tmpA[:, 2:Wd - 2], [B_tmpA], [B_tmpB])
                add(tmpA[:, 8:Wd], tmpB[:, 8:Wd], tmpB[:, 4:Wd - 4], [B_tmpB], [B_tmpA])
                add(tmpB[:, 16:Wd], tmpA[:, 16:Wd], tmpA[:, 8:Wd - 8], [B_tmpA], [B_tmpB])
                s_ap, s_b = tmpB, B_tmpB
            fw.op(DVE, (lambda g, s_ap, u, w: lambda e: e.scalar_tensor_tensor(out=as_out(pooled[:, g, 0:ntok]), in0=outview(s_ap[:, 0:Wd]),
                                                                                scalar=1.0 / w, in1=outview(u), op0=ALU.mult,
                                                                                op1=ALU.subtract))(g, s_ap, u, w),
                  reads=[s_b] + pxb, writes=pooledb)
            if (not sample) and ti == 0:
                fw.op(DVE, (lambda g, s_ap: lambda e: e.tensor_tensor(out=gt0[:, 0:16], in0=s_ap[:, 16:32], in1=invc_sb[:, g, :],
                                                                       op=ALU.mult))(g, s_ap), reads=[s_b, B_const], writes=[AB[8]])
                fw.op(DVE, (lambda g, u: lambda e: e.tensor_tensor(out=pooled[:, g, 0:16], in0=gt0[:, 0:16], in1=u[:, 16:32],
                                                                    op=ALU.subtract))(g, u), reads=[AB[8]] + pxb, writes=pooledb)
        slot = wnext()
        Wc = WS[slot]
        for e_ in range(4):
            qa, qab, qb_, qbb = next_pair()
            mm_acc(qa[:, 0:ntok], qab, [(Wc[:, d, e_ * 128:(e_ + 1) * 128], xT[:, d, 0:ntok]) for d in range(8)], [B_WS[slot], B_xT])
            mm_acc(qb_[:, 0:ntok], qbb, [(wpool_sb[:, e_, :], pooled[:, e_, 0:ntok])], [B_wpool] + pooledb)
            sig_gate(qa[:, 0:ntok], qab, ntok, gt0, AB[8], silu=True)
            fw.op(DVE, (lambda e_, qb_: lambda e: e.scalar_tensor_tensor(out=yaT[:, e_, 0:ntok], in0=qb_[:, 0:ntok],
                                                                          scalar=pscale_all[:, l, e_:e_ + 1], in1=gt0[:, 0:ntok],
                                                                          op0=ALU.mult, op1=ALU.mult))(e_, qb_),
                  reads=[qbb, AB[8], B_lay], writes=yab)
        wrel(slot)

        if _STOP == 2:
            wfree[:] = [0, 1]
            return
        deferred = {"fn": None}

        def flush_deferred():
            if deferred["fn"] is not None:
                deferred["fn"]()
                deferred["fn"] = None

        for c in range(6):
            slot = wnext()
            Wc = WS[slot]
            kind = c // 2
            if str(kind) not in _os.environ.get('KP2', '012'):
                wrel(slot)
                continue
            h0 = (c % 2) * 4
            col0 = h0 * 128
            for s, n in subs:
                i2 = (c * nsub + s) % 2
                ps, psb = next_ps()
                mm_acc(ps[0:n, :], psb, [(xT[:, d, s * 128:s * 128 + n], Wc[:, d, :]) for d in range(8)], [B_WS[slot], B_xT])
                flush_deferred()
                r0 = tok0 + s * 128
                if kind == 2:
                    vst = arf(2 + i2, 512)
                    cp(ACT, vst[0:n, :], ps[0:n, :], reads=[psb], writes=[AB[2 + i2]])
                    dma1(SP, (v_s if sample else v_p)[l, r0:r0 + n, col0:col0 + 512], vst[0:n, :], AB[2 + i2], reads=[AB[2 + i2]], is_out=True)
                    if sample:
                        cp(DVE, vbf_s[0:64, col0:col0 + 512], vst[0:64, :], reads=[AB[2 + i2]], writes=[B_tmpA])
                    else:
                        cp(DVE, VX[0:n, ti * 4 + s, h0:h0 + 4, 0:128], vst[0:n, :].rearrange("p (h v) -> p h v", h=4), reads=[AB[2 + i2]],
                           writes=[B_VX])
                    continue
                ra = arf(4 + i2, 512) if kind == 0 else arf(0 + i2, 512)
                rab = AB[4 + i2] if kind == 0 else AB[0 + i2]
                rb_ = arf(6 + i2, 512)
                rbb = AB[6 + i2]
                ps3 = ps[0:n, :].rearrange("p (g x) -> p g x", g=8)
                ra3 = ra[0:n, :].rearrange("p (g x) -> p g x", g=8)
                rb3 = rb_[0:n, :].rearrange("p (g x) -> p g x", g=8)
                cosb = cs_sb[0:n, s, 0:64].unsqueeze(1).to_broadcast([n, 8, 64])
                sin1 = cs_sb[0:n, s, 64:96].unsqueeze(1).to_broadcast([n, 8, 32])
                sin2 = cs_sb[0:n, s, 96:128].unsqueeze(1).to_broadcast([n, 8, 32])
                fw.op(DVE, (lambda ra3, ps3, cosb: lambda e: e.tensor_tensor(out=ra3, in0=ps3, in1=cosb, op=ALU.mult))(ra3, ps3, cosb),
                      reads=[psb, B_cs], writes=[rab])
                fw.op(DVE, (lambda rb3, ps3, sin1: lambda e: e.tensor_tensor(out=rb3[:, :, 0:32], in0=ps3[:, :, 32:64], in1=sin1,
                                                                              op=ALU.mult))(rb3, ps3, sin1),
                      reads=[psb, B_cs], writes=[rbb])
                fw.op(DVE, (lambda rb3, ps3, sin2: lambda e: e.tensor_tensor(out=rb3[:, :, 32:64], in0=ps3[:, :, 0:32], in1=sin2,
                                                                              op=ALU.mult))(rb3, ps3, sin2),
                      reads=[psb, B_cs], writes=[rbb])
                bfv = arb(8 + i2, 512, 512 * kind)
                bfb = AB[8 + i2]
                if kind == 0:
                    fw.op(DVE, (lambda bfv, ra, rb_, n: lambda e: e.tensor_tensor(out=bfv[0:n, :], in0=ra[0:n, :], in1=rb_[0:n, :],
                                                                                   op=ALU.add))(bfv, ra, rb_, n),
                          reads=[rab, rbb], writes=[bfb])
                else:
                    fw.op(DVE, (lambda ra, rb_, n: lambda e: e.tensor_tensor(out=ra[0:n, :], in0=ra[0:n, :], in1=rb_[0:n, :],
                                                                              op=ALU.add))(ra, rb_, n),
                          reads=[rab, rbb], writes=[rab])
                    dma1(SP, (k_s if sample else k_p)[l, r0:r0 + n, col0:col0 + 512], ra[0:n, :], rab, reads=[rab], is_out=True)
                    cp(ACT, bfv[0:n, :], ra[0:n, :], reads=[rab], writes=[bfb])
                def _tr(bfv=bfv, bfb=bfb, n=n, kind=kind, h0=h0, s=s, r0=r0):
                    for hh in range(4):
                        fw.op(PE, (lambda hh: lambda e: e.transpose(out=pT[:, hh * 128:hh * 128 + n], in_=bfv[0:n, hh * 128:(hh + 1) * 128],
                                                                    identity=ident[0:n, 0:n]))(hh),
                              reads=[bfb, B_const], writes=[B_pT], sig=(hh == 3))
                    src = pT[:, 0:512].rearrange("p (h k) -> p h k", h=4)[:, :, 0:n]
                    if kind == 0:
                        cp(ACT, QT[:, h0:h0 + 4, s * 128:s * 128 + n], src, reads=[B_pT], writes=[B_QT])
                    elif sample:
                        cp(DVE, KTn[:, h0:h0 + 4, 0:64], src, reads=[B_pT], writes=[B_KTn])
                    else:
                        cp(DVE, KT[:, h0:h0 + 4, r0:r0 + n], src, reads=[B_pT], writes=[B_KT])
                deferred["fn"] = _tr
            wrel(slot)
        flush_deferred()
        if _STOP == 3:
            wfree[:] = [0, 1]
            return
        ptc = {"i": 0}

        def next_pt():
            ptc["i"] += 1
            k = ptc["i"] % 4
            return arb(k // 2, 512, 512 * (k % 2)), PT[k]

        sublnc = subln_all[:, l:l + 1]
        if not sample:
            pending = None
            for k_ in range(4):
                PT[k_].w = AB[k_ // 2].w
                PT[k_].r = dict(AB[k_ // 2].r)
            for h in range(HEADS):
                for qbl in range(2):
                    qb = 2 * ti + qbl
                    jmax = 2 * qb + 1
                    qc0 = qbl * 256
                    npairs = qb + 1
                    pts = {}
                    for step in range(npairs + 1):
                        if step < npairs:
                            jp = step
                            pS, pSb = (pSX, B_pSX) if jp % 2 == 0 else (pSY, B_pSY)
                            for jl in range(2):
                                j = 2 * jp + jl
                                for m in range(2):
                                    fw.op(PE, (lambda pS, m, jl, j, h, qc0: lambda e: e.matmul(
                                        pS[:, m, jl * 256:(jl + 1) * 256], lhsT=KT[m * 64:(m + 1) * 64, h, j * 128:(j + 1) * 128],
                                        rhs=QT[m * 64:(m + 1) * 64, h, qc0:qc0 + 256], start=True, stop=True))(pS, m, jl, j, h, qc0),
                                          reads=[B_KT, B_QT], writes=[pSb], sig=(jl == 1 and m == 1))
                            for jl in range(2):
                                j = 2 * jp + jl
                                pt, ptb = next_pt()
                                pts[j] = (pt, ptb)
                                pt3 = pt.rearrange("p (m q) -> p m q", m=2)
                                fw.op(ACT, (lambda pt3, pS, jl: lambda e: e.activation(out=pt3, in_=pS[:, :, jl * 256:(jl + 1) * 256],
                                                                                      func=AF.Exp, scale=0.125))(pt3, pS, jl),
                                      reads=[pSb], writes=[ptb])
                                if jp == qb:
                                    if jl == 0:
                                        fw.op(POOL, (lambda pt3: lambda e: e.memset(pt3[64:128, :, 0:64], 0.0))(pt3), writes=[ptb])
                                    else:
                                        fw.op(POOL, (lambda pt3: lambda e: e.memset(pt3[0:64, :, 0:128], 0.0))(pt3), writes=[ptb])
                                        fw.op(POOL, (lambda pt3: lambda e: e.memset(pt3[64:128, :, 0:192], 0.0))(pt3), writes=[ptb])
                        if step >= 1:
                            for jl in range(2):
                                j = 2 * (step - 1) + jl
                                pt, ptb = pts.pop(j)
                                fw.op(PE, (lambda pt, j, h, jmax: lambda e: e.matmul(pO, lhsT=VX[:, j, h, 0:128], rhs=pt, start=(j == 0),
                                                                                    stop=(j == jmax)))(pt, j, h, jmax),
                                      reads=[B_VX, ptb], writes=[B_pO], sig=False)
                                fw.op(PE, (lambda pt, j, jmax: lambda e: e.matmul(pB, lhsT=ones_bf, rhs=pt, start=(j == 0),
                                                                                 stop=(j == jmax)))(pt, j, jmax),
                                      reads=[B_const, ptb], writes=[B_pB], sig=True)
                        if step == 1 and pending is not None:
                            pending()
                            pending = None
                    attn_epilogue1(256)

                    def _p2(h=h, qc0=qc0):
                        o = attn_epilogue2(l, 256)
                        fw.op(ACT, lambda e: e.activation(out=ybT[:, h, qc0:qc0 + 256], in_=o, func=AF.Copy, scale=sublnc),
                              reads=[AB[4], B_lay], writes=[B_ybT])
                    pending = _p2
            if pending is not None:
                pending()
            for k_ in range(4):
                merge_ev(AB[k_ // 2], PT[k_].w)
                for ev_ in PT[k_].r.values():
                    merge_ev(AB[k_ // 2], ev_)
        else:
            for b in range(NSEQ_S):
                fw.op(PE, lambda e: e.matmul(pO[:, 0:256], lhsT=zeros_bf[:, 0:128], rhs=zeros_bf[:, 0:256], start=True, stop=True),
                      reads=[B_const], writes=[B_pO])
                fw.op(PE, lambda e: e.matmul(pB[:, 0:256], lhsT=zeros_bf[:, 0:128], rhs=zeros_bf[:, 0:256], start=True, stop=True),
                      reads=[B_const], writes=[B_pB])
                for grp in range(3):
                    pS, pSb = (pSX, B_pSX) if grp % 2 == 0 else (pSY, B_pSY)
                    njl = 4 if grp < 2 else 1
                    kp = 128 if grp < 2 else TS
                    for jl in range(njl):
                        jt = grp * 4 + jl
                        for h in range(HEADS):
                            for m in range(2):
                                if grp < 2:
                                    lt = KT[m * 64:(m + 1) * 64, h, b * PAST + jt * 128: b * PAST + (jt + 1) * 128]
                                else:
                                    lt = KTn[m * 64:(m + 1) * 64, h, b * TS:(b + 1) * TS]
                                fw.op(PE, (lambda pS, m, jl, h, lt, b, kp: lambda e: e.matmul(
                                    pS[0:kp, m, jl * 128 + h * TS: jl * 128 + (h + 1) * TS], lhsT=lt,
                                    rhs=QT[m * 64:(m + 1) * 64, h, b * TS:(b + 1) * TS], start=True, stop=True))(pS, m, jl, h, lt, b, kp),
                                      reads=[B_KT, B_KTn, B_QT], writes=[pSb], sig=(h == 7 and m == 1 and jl == njl - 1))
                    if grp < 2:
                        pt, ptb = arb(0, 1024), AB[0]
                        if grp == 1:
                            pt, ptb = arb(1, 1024), AB[1]
                        pt4 = pt.rearrange("p (m x) -> p m x", m=2)
                        fw.op(ACT, (lambda pt4, pS: lambda e: e.activation(out=pt4, in_=pS, func=AF.Exp, scale=0.125))(pt4, pS),
                              reads=[pSb], writes=[ptb])
                        for jl in range(4):
                            jt = grp * 4 + jl
                            for m in range(2):
                                fw.op(PE, (lambda pt4, m, jl: lambda e: e.matmul(pB[:, m * 128:(m + 1) * 128], lhsT=ones_bf,
                                                                                rhs=pt4[:, m, jl * 128:(jl + 1) * 128], start=False,
                                                                                stop=False, skip_group_check=True))(pt4, m, jl),
                                      reads=[ptb, B_const], writes=[B_pB], sig=False)
                                for h in range(HEADS):
                                    fw.op(PE, (lambda pt4, m, jl, h, jt, b: lambda e: e.matmul(
                                        pO[:, m * 128 + h * TS: m * 128 + (h + 1) * TS], lhsT=VX[:, b * 8 + jt, h, 0:128],
                                        rhs=pt4[:, m, jl * 128 + h * TS: jl * 128 + (h + 1) * TS], start=False, stop=False,
                                        skip_group_check=True))(pt4, m, jl, h, jt, b),
                                          reads=[ptb, B_VX], writes=[B_pO], sig=(m == 1 and h == 7))
                    else:
                        dma1(SP, VN[0:TS, :, :], vbf_s[TS * b:TS * b + TS, :].rearrange("p (h v) -> p h v", h=8), B_VN,
                             reads=[B_tmpA], writes=[B_VN])
                        fw.op(ACT, (lambda pS: lambda e: e.activation(out=PTn[0:TS].rearrange("p m h q -> p m (h q)"), in_=pS[0:TS, :, 0:128],
                                                                       func=AF.Exp, scale=0.125))(pS),
                              reads=[pSb], writes=[B_PTn])
                        for m in range(2):
                            fw.op(PE, (lambda m: lambda e: e.matmul(pB[:, m * 128:(m + 1) * 128], lhsT=ones_bf,
                                                                    rhs=PTn[:, m].rearrange("p h q -> p (h q)"), start=False, stop=False,
                                                                    skip_group_check=True))(m),
                                  reads=[B_PTn, B_const], writes=[B_pB], sig=False)
                            for h in range(HEADS):
                                fw.op(PE, (lambda m, h, b: lambda e: e.matmul(pO[:, m * 128 + h * TS: m * 128 + (h + 1) * TS], lhsT=VN[:, h, :],
                                                                              rhs=PTn[:, m, h, :], start=False, stop=True,
                                                                              skip_group_check=True))(m, h, b),
                                      reads=[B_PTn, B_VN], writes=[B_pO, B_pB], sig=(m == 1 and h == 7))
                attn_epilogue1(128)
                o = attn_epilogue2(l, 128)
                fw.op(ACT, (lambda b, o: lambda e: e.activation(out=ybT[:, :, b * TS:(b + 1) * TS], in_=o.rearrange("p (h q) -> p h q", h=8),
                                                                 func=AF.Copy, scale=sublnc))(b, o),
                      reads=[AB[4], B_lay], writes=[B_ybT])

        if _STOP == 4:
            wfree[:] = [0, 1]
            return
        gt0, gt1 = arf(8, 512), arf(9, 512)
        for c in range(2):
            slot = wnext()
            Wc = WS[slot]
            for e_ in range(4):
                h = 4 * c + e_
                ps, psb = next_ps()
                mm_acc(ps[:, 0:ntok], psb, [(Wc[:, d, e_ * 128:(e_ + 1) * 128], xT[:, d, 0:ntok]) for d in range(8)], [B_WS[slot], B_xT])
                sig_gate(ps[:, 0:ntok], psb, ntok, gt0, AB[8], silu=True)
                fw.op(DVE, (lambda h: lambda e: e.tensor_tensor(out=ybT[:, h, 0:ntok], in0=ybT[:, h, 0:ntok], in1=gt0[:, 0:ntok],
                                                                 op=ALU.mult))(h), reads=[B_ybT, AB[8]], writes=[B_ybT])
            wrel(slot)

        if _STOP == 5:
            wfree[:] = [0, 1]
            return
        m1 = arena[:, 0:8 * 512].rearrange("p (c t) -> p c t", c=8)
        m1b = AB[0:8]
        slot_a = wnext()
        WA = WS[slot_a].rearrange("p k c -> p (k c)").rearrange("p (g c) -> p g c", g=4)
        for half in range(2):
            slot = wnext()
            Wc = WS[slot]
            for dcl in range(4):
                dc = 4 * half + dcl
                qa, qab, qb_, qbb = next_pair()
                mm_acc(qa[:, 0:ntok], qab, [(WA[:, g, dc * 128:(dc + 1) * 128], yaT[:, g, 0:ntok]) for g in range(4)], [B_WS[slot_a]] + yab)
                mm_acc(qb_[:, 0:ntok], qbb, [(Wc[:, d, dcl * 128:(dcl + 1) * 128], xT[:, d, 0:ntok]) for d in range(8)], [B_WS[slot], B_xT])
                sig_gate(qb_[:, 0:ntok], qbb, ntok, gt0, AB[8], silu=False)
                fw.op(DVE, (lambda dc, qa: lambda e: e.tensor_tensor(out=m1[:, dc, 0:ntok], in0=qa[:, 0:ntok], in1=gt0[:, 0:ntok],
                                                                      op=ALU.mult))(dc, qa), reads=[qab, AB[8]], writes=[m1b[dc]])
            if half == 0:
                wrel(slot)
        wrel(slot_a, slot)
        mergedT = QT
        for half in range(2):
            slot_b = wnext()
            WB = WS[slot_b]
            slot = wnext()
            Wc = WS[slot]
            for dcl in range(4):
                dc = 4 * half + dcl
                qa, qab, qb_, qbb = next_pair()
                mm_acc(qa[:, 0:ntok], qab, [(WB[:, h, dcl * 128:(dcl + 1) * 128], ybT[:, h, 0:ntok]) for h in range(8)], [B_WS[slot_b], B_ybT])
                mm_acc(qb_[:, 0:ntok], qbb, [(Wc[:, d, dcl * 128:(dcl + 1) * 128], xT[:, d, 0:ntok]) for d in range(8)], [B_WS[slot], B_xT])
                sig_gate(qb_[:, 0:ntok], qbb, ntok, gt0, AB[8], silu=False)
                fw.op(DVE, (lambda qa: lambda e: e.tensor_tensor(out=gt1[:, 0:ntok], in0=qa[:, 0:ntok], in1=gt0[:, 0:ntok], op=ALU.mult))(qa),
                      reads=[qab, AB[8]], writes=[AB[9]])
                fw.op(DVE, (lambda dc: lambda e: e.tensor_tensor(out=mergedT[:, dc, 0:ntok], in0=m1[:, dc, 0:ntok], in1=gt1[:, 0:ntok],
                                                                   op=ALU.add))(dc), reads=[m1b[dc], AB[9]], writes=[B_QT])
            wrel(slot_b, slot)

        if _STOP == 6:
            wfree[:] = [0, 1]
            return
        slot0 = wnext()
        slot1 = wnext()
        load_lnp(ln_g[l:l + 1, :], ln_b[l:l + 1, :])
        def ld_res(s, n):
            i2 = s % 2
            r0 = tok0 + s * 128
            dma1(SP, arf(2 * i2, 1024)[0:n, :], xsrc[r0:r0 + n, :], AB[2 * i2], reads=[xbuf(s)], writes=[AB[2 * i2], AB[2 * i2 + 1]])

        for s, n in subs[0:2]:
            ld_res(s, n)
        for s, n in subs:
            i2 = s % 2
            xres = arf(2 * i2, 1024)
            xrb = [AB[2 * i2], AB[2 * i2 + 1]]
            r_ap = arf(4 + 2 * i2, 1024)
            rbufs = [AB[4 + 2 * i2], AB[5 + 2 * i2]]
            r0 = tok0 + s * 128
            for hf, (pp_, ppb, sl_) in enumerate([(pA, B_pA, slot0), (pB, B_pB, slot1)]):
                mm_acc(pp_[0:n, :], ppb, [(mergedT[:, dc, s * 128:s * 128 + n], WS[sl_][:, dc, :]) for dc in range(8)], [B_QT, B_WS[sl_]])
                fw.op(DVE, (lambda hf, pp_, n, r_ap, xres: lambda e: e.scalar_tensor_tensor(
                    out=r_ap[0:n, hf * 512:(hf + 1) * 512], in0=xres[0:n, hf * 512:(hf + 1) * 512], scalar=ALPHA, in1=pp_[0:n, :],
                    op0=ALU.mult, op1=ALU.add))(hf, pp_, n, r_ap, xres), reads=xrb + [ppb], writes=rbufs)
            if s + 2 < nsub:
                ld_res(*subs[s + 2])
            elif (not sample) and ti + 1 < NT:
                s2 = s + 2 - nsub
                rn = (ti + 1) * TT + s2 * 128
                dma1(SP, arf(2 * s2, 1024), xcur_p[rn:rn + 128, :], AB[2 * s2], reads=[B_xcur_p[(ti + 1) * 4 + s2]],
                     writes=[AB[2 * s2], AB[2 * s2 + 1]])
                prefetched.add((l, ti + 1, s2))
            ln_rows(r_ap, rbufs, n, arf(8, 1024), arf(10, 1024), [AB[8], AB[9], AB[10], AB[11]])
            dma1(SP, ydst[r0:r0 + n, :], r_ap[0:n, :], AB[4 + 2 * i2], reads=rbufs, writes=[xbuf(s)], is_out=last_layer)
        assert nxt["k"] == NCH and not wq, (nxt, wq)
        wfree.extend([slot0, slot1])

    for l in range(L):
        fw.dma(POOL, [(lambda l: lambda e: e.dma_start(out=wpool_sb, in_=w_pool[l].rearrange("g c d -> c g d")))(l)], B_wpool, writes=[B_wpool])
        if l + 1 < L:
            cast_weights(l + 1)
        if with_sample:
            fw.new_epoch()
            tile_pass(l, 0, True)
        for ti in range(NT):
            if ti % 3 == 0:
                fw.new_epoch()
            tile_pass(l, ti, False)

    fw.emit()
    return nc, fw


def _rope_table(pos):
    half = 32
    inv = (np.float32(10000.0) ** (-np.arange(half, dtype=np.float32) / np.float32(half))).astype(np.float32)
    ang = pos.astype(np.float32)[:, None] * inv[None, :]
    c = np.cos(ang).astype(np.float32)
    s = np.sin(ang).astype(np.float32)
    return np.ascontiguousarray(np.concatenate([c, c, -s, s], axis=1).astype(np.float32))


def make_in_maps(inputs, L, SEQ, n_cores=8, with_sample=True):
    f = lambda a: np.ascontiguousarray(np.asarray(a, dtype=np.float32))
    cs_p = _rope_table(np.arange(SEQ))
    cs_s = np.ascontiguousarray(np.tile(_rope_table(PAST + np.arange(TS)), (NSEQ_S, 1)))
    invc = np.zeros((4, 16), np.float32)
    for g in range(4):
        w = 2 ** (g + 1)
        invc[g] = 1.0 / np.minimum(np.arange(16) + 1, w)
    common = dict(
        ln_in_g=f(inputs["ln_in_g"]).reshape(1, D), ln_in_b=f(inputs["ln_in_b"]).reshape(1, D),
        w_in=f(inputs["w_in"])[:L], w_pool=f(inputs["w_pool"])[:L], pool_scale=f(inputs["pool_scale"])[:L],
        lambda_qk=f(inputs["lambda_qk"])[:L].reshape(L, 256), subln_w=f(inputs["subln_w"])[:L],
        w_a=f(inputs["w_a"])[:L], w_b=f(inputs["w_b"])[:L], w_o=f(inputs["w_o"])[:L],
        ln_g=f(inputs["ln_g"])[:L], ln_b=f(inputs["ln_b"])[:L],
        cs_p=cs_p, cs_s=cs_s, idn_bf=np.eye(128).astype(ml_dtypes.bfloat16), idn_f=np.eye(128, dtype=np.float32),
        invc=invc.reshape(1, 64),
    )
    xp = f(inputs["x_prompt"])
    xs = f(inputs["x_sample"])
    ck = np.asarray(inputs["cache_k"], dtype=np.float32)
    cv = np.asarray(inputs["cache_v"], dtype=np.float32)
    sp = np.asarray(inputs["state_pool"], dtype=np.float32)
    maps = []
    for c in range(n_cores):
        m = dict(common)
        m["xp"] = np.ascontiguousarray(xp[c, :SEQ])
        m["xs"] = np.ascontiguousarray(xs[4 * c:4 * c + 4].reshape(64, D))
        m["ck"] = np.ascontiguousarray(ck[:L, 4 * c:4 * c + 4].reshape(L, 4, PAST, D))
        m["cv"] = np.ascontiguousarray(cv[:L, 4 * c:4 * c + 4].reshape(L, 4, PAST, D))
        m["spool"] = np.ascontiguousarray(sp[:L, 4 * c:4 * c + 4].reshape(L, 60, 512))
        maps.append(m)
    return maps


def gather(results, L, SEQ, n_cores=8):
    y_p = np.stack([r["y_p"] for r in results]).reshape(n_cores, SEQ, D)
    y_s = np.stack([r["y_s"] for r in results]).reshape(n_cores * 4, TS, D)
    k_p = np.stack([r["k_p"] for r in results], axis=1).reshape(L, n_cores, SEQ, HEADS, 128)
    v_p = np.stack([r["v_p"] for r in results], axis=1).reshape(L, n_cores, SEQ, HEADS, 128)
    pool_p = np.stack([r["pool_p"] for r in results], axis=1).reshape(L, n_cores, 15, 512)
    k_s = np.stack([r["k_s"] for r in results], axis=1).reshape(L, n_cores * 4, TS, HEADS, 128)
    v_s = np.stack([r["v_s"] for r in results], axis=1).reshape(L, n_cores * 4, TS, HEADS, 128)
    pool_s = np.stack([r["pool_s"] for r in results], axis=1).reshape(L, n_cores * 4, 15, 512)
    return tuple(np.ascontiguousarray(a.astype(np.float32)) for a in (y_p, y_s, k_p, v_p, pool_p, k_s, v_s, pool_s))


_CACHE = {}


def kernel(**inputs):
    L, SEQ = DEPTH_FULL, 4096
    if "nc" not in _CACHE:
        _CACHE["nc"] = build_program(L, SEQ)[0]
    nc = _CACHE["nc"]
    maps = make_in_maps(inputs, L, SEQ)
    res = run_bass_kernel_spmd(nc, maps, core_ids=list(range(8)))
    return gather(res.results, L, SEQ)
```

```python
import math
import numpy as np
import ml_dtypes
import concourse.bass as bass
import concourse.mybir as mybir
from concourse.bass_utils import run_bass_kernel_spmd

F32 = mybir.dt.float32
BF16 = mybir.dt.bfloat16
AF = mybir.ActivationFunctionType
ALU = mybir.AluOpType

D = 1024
TT = 512
HEADS = 8
DEPTH_FULL = 4
ALPHA = (2 * DEPTH_FULL) ** 0.25
LN_EPS = 1e-5
RMS_EPS = 1e-5
PAST = 1024
NSEQ_S = 4
TS = 16
NCH = 19


class Buf:
    __slots__ = ("name", "w", "r", "dsem", "dcnt")

    def __init__(self, name):
        self.name = name
        self.w = None
        self.r = {}
        self.dsem = None
        self.dcnt = 0


class Eng:
    def __init__(self, fw, name, is_pe=False):
        self.fw = fw
        self.name = name
        self.is_pe = is_pe
        self.prog = []
        self.seen = {}
        self.sem = None
        self.cnt = 0
        self.epoch = 0
        self.new_epoch()

    def new_epoch(self):
        self.sem = self.fw.nc.alloc_semaphore(f"s_{self.name}_{self.epoch}")
        self.fw.nsem += 1
        self.epoch += 1
        self.cnt = 0


class FW:
    def __init__(self, nc):
        self.nc = nc
        self.nsem = 0
        self.pe = Eng(self, "pe", is_pe=True)
        self.act = Eng(self, "act")
        self.dve = Eng(self, "dve")
        self.pool = Eng(self, "pool")
        self.sp = Eng(self, "sp")
        self.out_events = []
        self.ninst = 0

    def new_epoch(self):
        for e in (self.pe, self.act, self.dve, self.pool):
            if e.cnt > 0:
                e.new_epoch()

    def _collect(self, E, reads, writes):
        waits = []

        def need(ev, same_ok):
            if ev is None:
                return
            sem, val = ev
            if sem is E.sem and same_ok:
                return
            k = id(sem)
            if E.seen.get(k, 0) >= val:
                return
            E.seen[k] = val
            waits.append((sem, val))

        for b in reads:
            need(b.w, E.is_pe)
        for b in writes:
            need(b.w, E.is_pe)
            for ev in b.r.values():
                need(ev, True)
        return waits

    def op(self, E, fn, reads=(), writes=(), sig=True):
        waits = self._collect(E, reads, writes)
        self.ninst += 1
        if sig:
            E.cnt += 1
            ev = (E.sem, E.cnt)
            E.prog.append((waits, fn, (E.sem, 1)))
        else:
            ev = (E.sem, E.cnt + 1)
            E.prog.append((waits, fn, None))
        for b in reads:
            b.r[id(E.sem)] = ev
        for b in writes:
            b.w = ev
            b.r = {}
        return ev

    def dma(self, Q, fns, sembuf, reads=(), writes=(), is_out=False):
        waits = self._collect(Q, reads, writes)
        if sembuf.dsem is None:
            sembuf.dsem = self.nc.alloc_semaphore(f"d_{sembuf.name}")
            self.nsem += 1
        for i, fn in enumerate(fns):
            sembuf.dcnt += 16
            self.ninst += 1
            Q.prog.append((waits if i == 0 else [], fn, (sembuf.dsem, 16)))
        ev = (sembuf.dsem, sembuf.dcnt)
        for b in reads:
            b.r[id(sembuf.dsem)] = ev
        for b in writes:
            b.w = ev
            b.r = {}
        if is_out:
            self.out_events.append(ev)
        return ev

    def emit(self):
        nc = self.nc
        last = {}
        for sem, val in self.out_events:
            k = id(sem)
            if k not in last or last[k][1] < val:
                last[k] = (sem, val)
        self.sp.prog.append((list(last.values()), None, None))

        def replay(E, eng):
            for waits, fn, inc in E.prog:
                for sem, val in waits:
                    eng.wait_ge(sem, val)
                if fn is None:
                    continue
                ins = fn(eng)
                if inc is not None:
                    ins.then_inc(inc[0], inc[1])

        with nc.Block() as block:
            @block.tensor
            def _(eng):
                replay(self.pe, eng)

            @block.scalar
            def _(eng):
                replay(self.act, eng)

            @block.vector
            def _(eng):
                replay(self.dve, eng)

            @block.gpsimd
            def _(eng):
                replay(self.pool, eng)

            @block.sync
            def _(eng):
                replay(self.sp, eng)


def build_program(L, SEQ, with_sample=True):
    import os as _os
    _STOP = int(_os.environ.get('KSTOP', '99'))
    NT = SEQ // TT
    KTW = max(SEQ, NSEQ_S * PAST)
    NKT = KTW // 128
    nc = bass.Bass("TRN2", target_bir_lowering=False)
    fw = FW(nc)
    PE, ACT, DVE, POOL, SP = fw.pe, fw.act, fw.dve, fw.pool, fw.sp

    def din(name, shape, dt=F32):
        return nc.dram_tensor(name, shape, dt, kind="ExternalInput").ap()

    def dout(name, shape, dt=F32):
        return nc.dram_tensor(name, shape, dt, kind="ExternalOutput").ap()

    xp = din("xp", [SEQ, D])
    xs_in = din("xs", [64, D])
    ck = din("ck", [L, NSEQ_S, PAST, D])
    cv = din("cv", [L, NSEQ_S, PAST, D])
    spool = din("spool", [L, 60, 512])
    ln_in_g = din("ln_in_g", [1, D])
    ln_in_b = din("ln_in_b", [1, D])
    w_in = din("w_in", [L, D, 7168])
    w_pool = din("w_pool", [L, 4, 128, 128])
    pool_scale = din("pool_scale", [L, 512])
    lambda_qk = din("lambda_qk", [L, 256])
    subln_w = din("subln_w", [L, 128])
    w_a = din("w_a", [L, 512, D])
    w_b = din("w_b", [L, D, D])
    w_o = din("w_o", [L, D, D])
    ln_g = din("ln_g", [L, D])
    ln_b = din("ln_b", [L, D])
    cs_p = din("cs_p", [SEQ, 128])
    cs_s = din("cs_s", [64, 128])
    idn_bf = din("idn_bf", [128, 128], BF16)
    idn_f = din("idn_f", [128, 128])
    invc = din("invc", [1, 64])

    y_p = dout("y_p", [SEQ, D])
    y_s = dout("y_s", [64, D])
    k_p = dout("k_p", [L, SEQ, D])
    v_p = dout("v_p", [L, SEQ, D])
    pool_p = dout("pool_p", [L, 15, 512])
    k_s = dout("k_s", [L, 64, D])
    v_s = dout("v_s", [L, 64, D])
    pool_s = dout("pool_s", [L, 60, 512])

    xcur_p = nc.dram_tensor("xcur_p", [SEQ, D], F32, kind="Internal").ap()
    xcur_s = nc.dram_tensor("xcur_s", [64, D], F32, kind="Internal").ap()
    wsc = nc.dram_tensor("wsc", [L, NCH, 128, 4096], BF16, kind="Internal").ap()
    B_wsc = [Buf(f"wsc{l}") for l in range(L)]
    B_xcur_p = [Buf(f"xcp{i}") for i in range(SEQ // 128)]
    B_xcur_s = Buf("xcs")

    def sb(name, shape, dt=F32):
        return nc.alloc_sbuf_tensor(name, shape, dt).ap()

    KT = sb("KT", [128, HEADS, KTW], BF16)
    VX = sb("VX", [128, NKT, HEADS, 128], BF16)
    xT = sb("xT", [128, 8, TT], BF16)
    QT = sb("QT", [128, 8, TT], BF16)
    ybT = sb("ybT", [128, 8, TT], BF16)
    WS = [sb(f"WS{i}", [128, 8, 512], BF16) for i in range(2)]
    arena = sb("arena", [128, 12 * 512], F32)
    arena_bf = arena.bitcast(BF16)
    tmpA = sb("tmpA", [128, 528], F32)
    tmpB = sb("tmpB", [128, 528], F32)
    hist = sb("hist", [128, 4, 16], F32)
    cs_sb = sb("cs_sb", [128, 4, 128], F32)
    ident = sb("ident", [128, 128], BF16)
    identf = sb("identf", [128, 128], F32)
    ones_bf = sb("ones_bf", [128, 128], BF16)
    ones_f = sb("ones_f", [128, 128], F32)
    zeros_bf = sb("zeros_bf", [128, 256], BF16)
    neghalf = sb("neghalf", [128, 2], F32)
    epsc = sb("epsc", [128, 2], F32)
    invc_sb = sb("invc_sb", [128, 4, 16], F32)
    wpool_sb = sb("wpool_sb", [128, 4, 128], BF16)
    pscale_all = sb("pscale_all", [128, L, 4], F32)
    subln_all = sb("subln_all", [128, L], F32)
    lq_all = arena[:, 0:L * 256].rearrange("p (l c) -> p l c", l=L)
    lam_tmp = sb("lam_tmp", [128, 4 * L + 8], F32)
    neglam_all = sb("neglam_all", [128, L], F32)
    stats = sb("stats", [128, 2, 6], F32)
    mv = sb("mv", [128, 8], F32)
    KTn = sb("KTn", [128, HEADS, 64], BF16)
    VN = sb("VN", [128, HEADS, 128], BF16)
    PTn = sb("PTn", [128, 2, HEADS, TS], BF16)
    vbf_s = tmpA.bitcast(BF16)[0:64, 0:D]

    B_KT, B_VX, B_xT, B_QT, B_ybT = Buf("KT"), Buf("VX"), Buf("xT"), Buf("QT"), Buf("ybT")
    B_WS = [Buf(f"WS{i}") for i in range(2)]
    AB = [Buf(f"ar{i}") for i in range(12)]
    B_tmpA, B_tmpB, B_hist, B_cs = Buf("tmpA"), Buf("tmpB"), Buf("hist"), Buf("cs")
    B_const = Buf("const")
    B_lay = Buf("laycst")
    B_stats, B_mv = Buf("stats"), Buf("mv")
    B_KTn, B_VN, B_PTn = Buf("KTn"), Buf("VN"), Buf("PTn")
    B_wpool = Buf("wpool")
    PT = [Buf(f"pt{i}") for i in range(4)]

    def merge_ev(dst, ev):
        if ev is None:
            return
        k = id(ev[0])
        if k not in dst.r or dst.r[k][1] < ev[1]:
            dst.r[k] = ev

    pA = nc.alloc_psum_tensor("pA", [128, 512], F32).ap()
    pB = nc.alloc_psum_tensor("pB", [128, 512], F32).ap()
    pT = nc.alloc_psum_tensor("pT", [128, 1024], BF16).ap()
    pO = nc.alloc_psum_tensor("pO", [128, 512], F32).ap()
    pSX = nc.alloc_psum_tensor("pSX", [128, 2, 512], F32).ap()
    pSY = nc.alloc_psum_tensor("pSY", [128, 2, 512], F32).ap()
    B_pA, B_pB, B_pT, B_pO, B_pSX, B_pSY = (Buf(n) for n in ["pA", "pB", "pT", "pO", "pSX", "pSY"])

    def arf(slot, ncols, off=0):
        return arena[:, slot * 512 + off: slot * 512 + off + ncols]

    def arb(slot, ncols, off=0):
        return arena_bf[:, slot * 1024 + off: slot * 1024 + off + ncols]

    def cp(E, out, in_, reads, writes):
        if E is ACT:
            return fw.op(E, lambda e: e.activation(out=out, in_=in_, func=AF.Copy), reads=reads, writes=writes)
        return fw.op(E, lambda e: e.tensor_copy(out=out, in_=in_), reads=reads, writes=writes)

    def dma1(Q, out, in_, sembuf, reads=(), writes=(), is_out=False, slow=False):
        if slow:
            f = lambda e: e.dma_start(out=out, in_=in_, allow_slow_non_contiguous=True)
        else:
            f = lambda e: e.dma_start(out=out, in_=in_)
        return fw.dma(Q, [f], sembuf, reads=reads, writes=writes, is_out=is_out)

    dma1(SP, ident, idn_bf, B_const, writes=[B_const])
    dma1(SP, identf, idn_f, B_const, writes=[B_const])
    dma1(SP, invc_sb.rearrange("p g t -> p (g t)"), invc.partition_broadcast(128), B_const, writes=[B_const])
    dma1(SP, arena[:, 0:L * 256], lambda_qk.rearrange("(o l) c -> o (l c)", o=1).partition_broadcast(128),
         AB[0], writes=[AB[0], AB[1]])
    dma1(SP, pscale_all, pool_scale.rearrange("l (g p) -> p l g", p=128), B_lay, writes=[B_lay], slow=True)
    dma1(SP, subln_all, subln_w.rearrange("l p -> p l"), B_lay, writes=[B_lay], slow=True)
    fw.op(DVE, lambda e: e.memset(ones_bf, 1.0), writes=[B_const])
    fw.op(DVE, lambda e: e.memset(ones_f, 1.0), writes=[B_const])
    fw.op(DVE, lambda e: e.memset(zeros_bf, 0.0), writes=[B_const])
    fw.op(DVE, lambda e: e.memset(neghalf, -0.5), writes=[B_const])
    fw.op(DVE, lambda e: e.memset(epsc[:, 0:1], LN_EPS), writes=[B_const])
    fw.op(DVE, lambda e: e.memset(epsc[:, 1:2], RMS_EPS), writes=[B_const])
    fw.op(DVE, lambda e: e.memset(VN, 0.0), writes=[B_VN])
    fw.op(DVE, lambda e: e.memset(PTn, 0.0), writes=[B_PTn])
    for l in range(L):
        lam_init = 0.8 - 0.6 * math.exp(-0.3 * l)
        for i in range(2):
            fw.op(DVE, (lambda l, i: lambda e: e.tensor_tensor(out=tmpA[:, 0:64], in0=lq_all[:, l, 128 * i:128 * i + 64],
                                                                 in1=lq_all[:, l, 128 * i + 64:128 * i + 128], op=ALU.mult))(l, i),
                  reads=[AB[0], AB[1]], writes=[B_tmpA])
            fw.op(DVE, (lambda l, i: lambda e: e.reduce_sum(out=lam_tmp[:, 4 * l + i:4 * l + i + 1], in_=tmpA[:, 0:64],
                                                              axis=mybir.AxisListType.X))(l, i),
                  reads=[B_tmpA], writes=[B_lay])
        fw.op(ACT, (lambda l: lambda e: e.activation(out=lam_tmp[:, 4 * l + 2:4 * l + 4], in_=lam_tmp[:, 4 * l:4 * l + 2], func=AF.Exp))(l),
              reads=[B_lay], writes=[B_lay])
        fw.op(DVE, (lambda l, li: lambda e: e.scalar_tensor_tensor(out=neglam_all[:, l:l + 1], in0=lam_tmp[:, 4 * l + 3:4 * l + 4],
                                                                     scalar=-li, in1=lam_tmp[:, 4 * l + 2:4 * l + 3],
                                                                     op0=ALU.add, op1=ALU.subtract))(l, lam_init),
              reads=[B_lay], writes=[B_lay])
        fw.op(DVE, (lambda l, li: lambda e: e.tensor_scalar(out=subln_all[:, l:l + 1], in0=subln_all[:, l:l + 1], scalar1=1.0 - li,
                                                              scalar2=None, op0=ALU.mult))(l, lam_init),
              reads=[B_lay], writes=[B_lay])

    def wsc_view(l, ci, kc, ncol):
        return wsc[l, ci].rearrange("p (k c) -> p k c", k=kc)

    def cast_weights(l):
        fns = []
        for ci in range(14):
            src = w_in[l].rearrange("(k p) c -> p k c", p=128)[:, :, ci * 512:(ci + 1) * 512]
            dst = wsc_view(l, ci, 8, 512)
            fns.append((lambda s, d: lambda e: e.dma_start(out=d, in_=s))(src, dst))
        src = w_a[l].rearrange("(g p) c -> p g c", p=128)
        fns.append((lambda s, d: lambda e: e.dma_start(out=d, in_=s))(src, wsc_view(l, 14, 4, 1024)))
        for hb in range(2):
            src = w_b[l].rearrange("(k p) c -> p k c", p=128)[:, :, hb * 512:(hb + 1) * 512]
            fns.append((lambda s, d: lambda e: e.dma_start(out=d, in_=s))(src, wsc_view(l, 15 + hb, 8, 512)))
        for hb in range(2):
            src = w_o[l].rearrange("(k p) c -> p k c", p=128)[:, :, hb * 512:(hb + 1) * 512]
            fns.append((lambda s, d: lambda e: e.dma_start(out=d, in_=s))(src, wsc_view(l, 17 + hb, 8, 512)))
        fw.dma(POOL, fns, B_wsc[l], writes=[B_wsc[l]])

    cast_weights(0)

    CH_ORDER = [0, 1, 2, 3, 4, 5, 6, 7, 8, 9, 14, 10, 11, 15, 12, 16, 13, 17, 18]
    wfree = [0, 1]

    def wload(l, k, slot):
        ci = CH_ORDER[k]
        dst = WS[slot].rearrange("p k c -> p (k c)")
        dma1(SP, dst, wsc[l, ci], B_WS[slot], reads=[B_wsc[l]], writes=[B_WS[slot]])

    def ln_rows(r_ap, r_bufs, n, g_ap, b_ap, pbufs):
        for hlf in range(2):
            fw.op(DVE, (lambda hlf: lambda e: e.bn_stats(out=stats[0:n, hlf, :], in_=r_ap[0:n, hlf * 512:(hlf + 1) * 512]))(hlf),
                  reads=r_bufs, writes=[B_stats])
        fw.op(DVE, lambda e: e.bn_aggr(out=mv[0:n, 0:2], in_=stats[0:n].rearrange("p a b -> p (a b)")), reads=[B_stats], writes=[B_mv])
        fw.op(ACT, lambda e: e.activation(out=mv[0:n, 2:3], in_=mv[0:n, 1:2], func=AF.Ln, bias=epsc[0:n, 0:1], scale=1.0),
              reads=[B_mv, B_const], writes=[B_mv])
        fw.op(ACT, lambda e: e.activation(out=mv[0:n, 3:4], in_=mv[0:n, 2:3], func=AF.Exp, scale=-0.5),
              reads=[B_mv], writes=[B_mv])
        fw.op(DVE, lambda e: e.scalar_tensor_tensor(out=mv[0:n, 4:5], in0=mv[0:n, 0:1], scalar=-1.0, in1=mv[0:n, 3:4],
                                                    op0=ALU.mult, op1=ALU.mult), reads=[B_mv], writes=[B_mv])
        fw.op(ACT, lambda e: e.activation(out=r_ap[0:n, :], in_=r_ap[0:n, :], func=AF.Identity, bias=mv[0:n, 4:5], scale=mv[0:n, 3:4]),
              reads=r_bufs + [B_mv], writes=r_bufs)
        fw.op(DVE, lambda e: e.tensor_tensor(out=r_ap[0:n, :], in0=r_ap[0:n, :], in1=g_ap[0:n, :], op=ALU.mult),
              reads=r_bufs + pbufs, writes=r_bufs)
        fw.op(DVE, lambda e: e.tensor_tensor(out=r_ap[0:n, :], in0=r_ap[0:n, :], in1=b_ap[0:n, :], op=ALU.add),
              reads=r_bufs + pbufs, writes=r_bufs)

    def load_lnp(g_src, b_src):
        dma1(SP, arf(8, 1024), g_src.partition_broadcast(128), AB[8], writes=[AB[8], AB[9]])
        dma1(SP, arf(10, 1024), b_src.partition_broadcast(128), AB[10], writes=[AB[10], AB[11]])

    load_lnp(ln_in_g, ln_in_b)
    rows = [(xp[i * 128:(i + 1) * 128, :], xcur_p[i * 128:(i + 1) * 128, :], 128, B_xcur_p[i]) for i in range(SEQ // 128)]
    if with_sample:
        rows.append((xs_in, xcur_s, 64, B_xcur_s))
    for i, (src, dst, n, bdst) in enumerate(rows):
        sl = 2 * (i % 4)
        r_ap = arf(sl, 1024)
        rb = [AB[sl], AB[sl + 1]]
        dma1(SP, r_ap[0:n, :], src, AB[sl], writes=rb)
        ln_rows(r_ap, rb, n, arf(8, 1024), arf(10, 1024), [AB[8], AB[9], AB[10], AB[11]])
        dma1(SP, dst, r_ap[0:n, :], AB[sl], reads=rb, writes=[bdst])

    pp = {"i": 0}
    prefetched = set()

    def next_ps():
        pp["i"] += 1
        return (pA, B_pA) if pp["i"] % 2 else (pB, B_pB)

    PS_PAIRS = [(pA, B_pA, pB, B_pB), (pSX[:, 0, :], B_pSX, pSX[:, 1, :], B_pSX), (pSY[:, 0, :], B_pSY, pSY[:, 1, :], B_pSY)]

    def next_pair():
        pp["i"] += 1
        return PS_PAIRS[pp["i"] % 3]

    def mm_acc(out_ap, out_buf, pairs, reads):
        n = len(pairs)
        for i, (lt, rh) in enumerate(pairs):
            fw.op(PE, (lambda lt, rh, i: lambda e: e.matmul(out_ap, lhsT=lt, rhs=rh, start=(i == 0), stop=(i == n - 1)))(lt, rh, i),
                  reads=reads, writes=[out_buf], sig=(i == n - 1))

    def sig_gate(ps_ap, ps_buf, nt, gt, gt_buf, silu):
        fw.op(ACT, lambda e: e.activation(out=gt[:, 0:nt], in_=ps_ap, func=AF.Tanh, scale=0.5), reads=[ps_buf], writes=[gt_buf])
        fw.op(DVE, lambda e: e.tensor_scalar(out=gt[:, 0:nt], in0=gt[:, 0:nt], scalar1=0.5, scalar2=0.5, op0=ALU.mult, op1=ALU.add),
              reads=[gt_buf], writes=[gt_buf])
        if silu:
            fw.op(DVE, lambda e: e.tensor_tensor(out=gt[:, 0:nt], in0=ps_ap, in1=gt[:, 0:nt], op=ALU.mult),
                  reads=[ps_buf, gt_buf], writes=[gt_buf])

    def attn_epilogue1(W_):
        rinv, t1 = arf(2, 2 * W_), arf(3, 2 * W_)
        fw.op(ACT, lambda e: e.activation(out=rinv, in_=pB[:, 0:2 * W_], func=AF.Ln), reads=[B_pB], writes=[AB[2]])
        fw.op(ACT, lambda e: e.activation(out=rinv, in_=rinv, func=AF.Exp, scale=-1.0), reads=[AB[2]], writes=[AB[2]])
        fw.op(DVE, lambda e: e.tensor_tensor(out=t1, in0=pO[:, 0:2 * W_], in1=rinv, op=ALU.mult), reads=[B_pO, AB[2]], writes=[AB[3]])

    def attn_epilogue2(l, W_):
        t1 = arf(3, 2 * W_)
        o, sq = arf(4, W_), arf(4, W_, 256)
        rs, rstd = arf(5, W_), arf(5, W_, 256)
        fw.op(DVE, lambda e: e.scalar_tensor_tensor(out=o, in0=t1[:, W_:2 * W_], scalar=neglam_all[:, l:l + 1], in1=t1[:, 0:W_],
                                                    op0=ALU.mult, op1=ALU.add), reads=[AB[3], B_lay], writes=[AB[4]])
        fw.op(DVE, lambda e: e.tensor_tensor(out=sq, in0=o, in1=o, op=ALU.mult), reads=[AB[4]], writes=[AB[4]])
        fw.op(PE, lambda e: e.matmul(pA[:, 0:W_], lhsT=ones_f, rhs=sq, start=True, stop=True), reads=[AB[4], B_const], writes=[B_pA])
        fw.op(ACT, lambda e: e.activation(out=rs, in_=pA[:, 0:W_], func=AF.Ln, bias=epsc[:, 1:2], scale=1.0 / 128),
              reads=[B_pA, B_const], writes=[AB[5]])
        fw.op(ACT, lambda e: e.activation(out=rstd, in_=rs, func=AF.Exp, scale=-0.5), reads=[AB[5]], writes=[AB[5]])
        return o, rstd

    def tile_pass(l, ti, sample):
        last_layer = (l == L - 1)
        if sample:
            ntok, subs = 64, [(0, 64)]
        else:
            ntok, subs = TT, [(s, 128) for s in range(4)]
        nsub = len(subs)
        tok0 = 0 if sample else ti * TT
        xsrc = xcur_s if sample else xcur_p
        ydst = (y_s if sample else y_p) if last_layer else xsrc

        def xbuf(s):
            return B_xcur_s if sample else B_xcur_p[ti * 4 + s]

        wq = []
        nxt = {"k": 0}

        def wtop():
            while wfree and nxt["k"] < NCH:
                sl_ = wfree.pop(0)
                wload(l, nxt["k"], sl_)
                wq.append(sl_)
                nxt["k"] += 1

        def wnext():
            wtop()
            return wq.pop(0)

        def wrel(*slots):
            for sl_ in slots:
                wfree.append(sl_)
            wtop()

        wtop()

        if sample:
            for b in range(NSEQ_S):
                for jt in range(8):
                    i = b * 8 + jt
                    sl = i % 4
                    kst = arb(sl, 1024)
                    fw.dma(POOL, [(lambda b, jt, kst: lambda e: e.dma_start(out=kst, in_=ck[l, b, jt * 128:(jt + 1) * 128, :]))(b, jt, kst)],
                           AB[sl], writes=[AB[sl]])
                    fw.dma(POOL, [(lambda b, jt: lambda e: e.dma_start(out=VX[:, b * 8 + jt, :, 0:128],
                                                                         in_=cv[l, b, jt * 128:(jt + 1) * 128, :].rearrange("p (h v) -> p h v", h=8)))(b, jt)],
                           B_VX, writes=[B_VX])
                    for h in range(8):
                        fw.op(PE, (lambda h, kst: lambda e: e.transpose(out=pT[:, h * 128:(h + 1) * 128], in_=kst[:, h * 128:(h + 1) * 128],
                                                                        identity=ident))(h, kst),
                              reads=[AB[sl], B_const], writes=[B_pT], sig=(h == 7))
                    E = ACT if i % 2 == 0 else DVE
                    cp(E, KT[:, :, b * PAST + jt * 128: b * PAST + (jt + 1) * 128], pT.rearrange("p (h k) -> p h k", h=8),
                       reads=[B_pT], writes=[B_KT])

        for s, n in subs:
            sl = 2 * (s % 2)
            xin = arf(sl, 1024)
            xbf = arb(4 + (s % 2), 1024)
            r0 = tok0 + s * 128
            if (l, ti, s) in prefetched and not sample:
                prefetched.discard((l, ti, s))
            else:
                dma1(SP, xin[0:n, :], xsrc[r0:r0 + n, :], AB[sl], reads=[xbuf(s)], writes=[AB[sl], AB[sl + 1]])
            cp(ACT, xbf[0:n, :], xin[0:n, :], reads=[AB[sl], AB[sl + 1]], writes=[AB[4 + s % 2]])
            for c in range(8):
                fw.op(PE, (lambda c, xbf, n: lambda e: e.transpose(out=pT[:, c * 128:c * 128 + n], in_=xbf[0:n, c * 128:(c + 1) * 128],
                                                                  identity=ident[0:n, 0:n]))(c, xbf, n),
                      reads=[AB[4 + s % 2], B_const], writes=[B_pT], sig=(c == 7))
            cp(DVE, xT[:, :, s * 128:s * 128 + n], pT.rearrange("p (c k) -> p c k", c=8)[:, :, 0:n], reads=[B_pT], writes=[B_xT])
        if sample:
            dma1(SP, cs_sb[0:64, 0, :], cs_s, B_cs, writes=[B_cs])
        else:
            dma1(SP, cs_sb, cs_p[tok0:tok0 + TT, :].rearrange("(s p) c -> p s c", p=128), B_cs, writes=[B_cs])

        if _STOP == 1:
            wfree[:] = [0, 1]
            return
        Wd = 128 if sample else 528
        pxT = arena[:, 0:4 * Wd].rearrange("p (g w) -> p g w", g=4)
        pxb = [AB[0], AB[1], AB[2], AB[3], AB[4]] if not sample else [AB[0]]
        pooled = arb(6, 4 * TT).rearrange("p (g t) -> p g t", g=4)
        pooledb = [AB[6], AB[7]]
        yaT = arb(10, 4 * TT).rearrange("p (g t) -> p g t", g=4)
        yab = [AB[10], AB[11]]
        gt0, gt1 = arf(8, 512), arf(9, 512)

        def outview(ap2d):
            if sample:
                return ap2d.rearrange("p (b w) -> p b w", b=4)[:, :, 16:32]
            return ap2d[:, 16:528]

        def as_out(ap2d):
            if sample:
                return ap2d.rearrange("p (b t) -> p b t", b=4)
            return ap2d

        if sample:
            fw.op(DVE, lambda e: e.memset(arena[:, 0:4 * Wd], 0.0), writes=pxb)
            sp_sb = arf(2, 512)
            dma1(SP, sp_sb[0:60, :], spool[l], AB[2], writes=[AB[2]])
            for g in range(4):
                fw.op(PE, (lambda g: lambda e: e.transpose(out=pA[:, g * 64:g * 64 + 60], in_=sp_sb[0:60, g * 128:(g + 1) * 128],
                                                           identity=identf[0:60, 0:60]))(g),
                      reads=[AB[2], B_const], writes=[B_pA], sig=(g == 3))
            for g in range(4):
                cp(ACT, pxT[:, g, :].rearrange("p (b w) -> p b w", b=4)[:, :, 1:16],
                   pA[:, g * 64:g * 64 + 60].rearrange("p (b j) -> p b j", b=4), reads=[B_pA], writes=pxb)
        elif ti == 0:
            fw.op(DVE, lambda e: e.memset(pxT[:, :, 0:16], 0.0), writes=pxb)
        else:
            cp(DVE, pxT[:, :, 0:16], hist, reads=[B_hist], writes=pxb)

        slot = wnext()
        Wc = WS[slot]
        for e_ in range(4):
            ps, psb = next_ps()
            mm_acc(ps[:, 0:ntok], psb, [(Wc[:, d, e_ * 128:(e_ + 1) * 128], xT[:, d, 0:ntok]) for d in range(8)], [B_WS[slot], B_xT])
            cp(ACT, outview(pxT[:, e_, :]), as_out(ps[:, 0:ntok]), reads=[psb], writes=pxb)
        if sample or ti == NT - 1:
            s_last, n_last = subs[-1]
            ps, psb = next_ps()
            mm_acc(ps[0:n_last, :], psb, [(xT[:, d, s_last * 128:s_last * 128 + n_last], Wc[:, d, :]) for d in range(8)], [B_WS[slot], B_xT])
            cp(ACT, gt1[0:n_last, :], ps[0:n_last, :], reads=[psb], writes=[AB[9]])
            if sample:
                fw.dma(SP, [(lambda b: lambda e: e.dma_start(out=pool_s[l, 15 * b:15 * b + 15, :], in_=gt1[16 * b + 1:16 * b + 16, :]))(b)
                            for b in range(4)], AB[9], reads=[AB[9]], is_out=True)
            else:
                dma1(SP, pool_p[l], gt1[113:128, :], AB[9], reads=[AB[9]], is_out=True)
        wrel(slot)
        if not sample:
            cp(POOL, hist, pxT[:, :, 512:528], reads=pxb, writes=[B_hist])
        for g in range(4):
            u = pxT[:, g, :]
            w = 2 ** (g + 1)

            def add(out, a, b_, rd, wr):
                fw.op(DVE, lambda e: e.tensor_tensor(out=out, in0=a, in1=b_, op=ALU.add), reads=rd, writes=wr)

            if g == 0:
                add(tmpA[:, 16:Wd], u[:, 16:Wd], u[:, 15:Wd - 1], pxb, [B_tmpA])
                s_ap, s_b = tmpA, B_tmpA
            elif g == 1:
                add(tmpA[:, 14:Wd], u[:, 14:Wd], u[:, 13:Wd - 1], pxb, [B_tmpA])
                add(tmpB[:, 16:Wd], tmpA[:, 16:Wd], tmpA[:, 14:Wd - 2], [B_tmpA], [B_tmpB])
                s_ap, s_b = tmpB, B_tmpB
            elif g == 2:
                add(tmpA[:, 10:Wd], u[:, 10:Wd], u[:, 9:Wd - 1], pxb, [B_tmpA])
                add(tmpB[:, 12:Wd], tmpA[:, 12:Wd], tmpA[:, 10:Wd - 2], [B_tmpA], [B_tmpB])
                add(tmpA[:, 16:Wd], tmpB[:, 16:Wd], tmpB[:, 12:Wd - 4], [B_tmpB], [B_tmpA])
                s_ap, s_b = tmpA, B_tmpA
            else:
                add(tmpA[:, 2:Wd], u[:, 2:Wd], u[:, 1:Wd - 1], pxb, [B_tmpA])
                add(tmpB[:, 4:Wd], tmpA[:, 4:Wd], tmpA[:, 2:Wd - 2], [B_tmpA], [B_tmpB])
                add(tmpA[:, 8:Wd], tmpB[:, 8:Wd], tmpB[:, 4:Wd - 4], [B_tmpB], [B_tmpA])
                add(tmpB[:, 16:Wd], tmpA[:, 16:Wd], tmpA[:, 8:Wd - 8], [B_tmpA], [B_tmpB])
                s_ap, s_b = tmpB, B_tmpB
            fw.op(DVE, (lambda g, s_ap, u, w: lambda e: e.scalar_tensor_tensor(out=as_out(pooled[:, g, 0:ntok]), in0=outview(s_ap[:, 0:Wd]),
                                                                                scalar=1.0 / w, in1=outview(u), op0=ALU.mult,
                                                                                op1=ALU.subtract))(g, s_ap, u, w),
                  reads=[s_b] + pxb, writes=pooledb)
            if (not sample) and ti == 0:
                fw.op(DVE, (lambda g, s_ap: lambda e: e.tensor_tensor(out=gt0[:, 0:16], in0=s_ap[:, 16:32], in1=invc_sb[:, g, :],
                                                                       op=ALU.mult))(g, s_ap), reads=[s_b, B_const], writes=[AB[8]])
                fw.op(DVE, (lambda g, u: lambda e: e.tensor_tensor(out=pooled[:, g, 0:16], in0=gt0[:, 0:16], in1=u[:, 16:32],
                                                                    op=ALU.subtract))(g, u), reads=[AB[8]] + pxb, writes=pooledb)
        slot = wnext()
        Wc = WS[slot]
        for e_ in range(4):
            qa, qab, qb_, qbb = next_pair()
            mm_acc(qa[:, 0:ntok], qab, [(Wc[:, d, e_ * 128:(e_ + 1) * 128], xT[:, d, 0:ntok]) for d in range(8)], [B_WS[slot], B_xT])
            mm_acc(qb_[:, 0:ntok], qbb, [(wpool_sb[:, e_, :], pooled[:, e_, 0:ntok])], [B_wpool] + pooledb)
            sig_gate(qa[:, 0:ntok], qab, ntok, gt0, AB[8], silu=True)
            fw.op(DVE, (lambda e_, qb_: lambda e: e.scalar_tensor_tensor(out=yaT[:, e_, 0:ntok], in0=qb_[:, 0:ntok],
                                                                          scalar=pscale_all[:, l, e_:e_ + 1], in1=gt0[:, 0:ntok],
                                                                          op0=ALU.mult, op1=ALU.mult))(e_, qb_),
                  reads=[qbb, AB[8], B_lay], writes=yab)
        wrel(slot)

        if _STOP == 2:
            wfree[:] = [0, 1]
            return
        deferred = {"fn": None}

        def flush_deferred():
            if deferred["fn"] is not None:
                deferred["fn"]()
                deferred["fn"] = None

        for c in range(6):
            slot = wnext()
            Wc = WS[slot]
            kind = c // 2
            if str(kind) not in _os.environ.get('KP2', '012'):
                wrel(slot)
                continue
            h0 = (c % 2) * 4
            col0 = h0 * 128
            for s, n in subs:
                i2 = (c * nsub + s) % 2
                ps, psb = next_ps()
                mm_acc(ps[0:n, :], psb, [(xT[:, d, s * 128:s * 128 + n], Wc[:, d, :]) for d in range(8)], [B_WS[slot], B_xT])
                flush_deferred()
                r0 = tok0 + s * 128
                if kind == 2:
                    vst = arf(2 + i2, 512)
                    cp(ACT, vst[0:n, :], ps[0:n, :], reads=[psb], writes=[AB[2 + i2]])
                    dma1(SP, (v_s if sample else v_p)[l, r0:r0 + n, col0:col0 + 512], vst[0:n, :], AB[2 + i2], reads=[AB[2 + i2]], is_out=True)
                    if sample:
                        cp(DVE, vbf_s[0:64, col0:col0 + 512], vst[0:64, :], reads=[AB[2 + i2]], writes=[B_tmpA])
                    else:
                        cp(DVE, VX[0:n, ti * 4 + s, h0:h0 + 4, 0:128], vst[0:n, :].rearrange("p (h v) -> p h v", h=4), reads=[AB[2 + i2]],
                           writes=[B_VX])
                    continue
                ra = arf(4 + i2, 512) if kind == 0 else arf(0 + i2, 512)
                rab = AB[4 + i2] if kind == 0 else AB[0 + i2]
                rb_ = arf(6 + i2, 512)
                rbb = AB[6 + i2]
                ps3 = ps[0:n, :].rearrange("p (g x) -> p g x", g=8)
                ra3 = ra[0:n, :].rearrange("p (g x) -> p g x", g=8)
                rb3 = rb_[0:n, :].rearrange("p (g x) -> p g x", g=8)
                cosb = cs_sb[0:n, s, 0:64].unsqueeze(1).to_broadcast([n, 8, 64])
                sin1 = cs_sb[0:n, s, 64:96].unsqueeze(1).to_broadcast([n, 8, 32])
                sin2 = cs_sb[0:n, s, 96:128].unsqueeze(1).to_broadcast([n, 8, 32])
                fw.op(DVE, (lambda ra3, ps3, cosb: lambda e: e.tensor_tensor(out=ra3, in0=ps3, in1=cosb, op=ALU.mult))(ra3, ps3, cosb),
                      reads=[psb, B_cs], writes=[rab])
                fw.op(DVE, (lambda rb3, ps3, sin1: lambda e: e.tensor_tensor(out=rb3[:, :, 0:32], in0=ps3[:, :, 32:64], in1=sin1,
                                                                              op=ALU.mult))(rb3, ps3, sin1),
                      reads=[psb, B_cs], writes=[rbb])
                fw.op(DVE, (lambda rb3, ps3, sin2: lambda e: e.tensor_tensor(out=rb3[:, :, 32:64], in0=ps3[:, :, 0:32], in1=sin2,
                                                                              op=ALU.mult))(rb3, ps3, sin2),
                      reads=[psb, B_cs], writes=[rbb])
                bfv = arb(8 + i2, 512, 512 * kind)
                bfb = AB[8 + i2]
                if kind == 0:
                    fw.op(DVE, (lambda bfv, ra, rb_, n: lambda e: e.tensor_tensor(out=bfv[0:n, :], in0=ra[0:n, :], in1=rb_[0:n, :],
                                                                                   op=ALU.add))(bfv, ra, rb_, n),
                          reads=[rab, rbb], writes=[bfb])
                else:
                    fw.op(DVE, (lambda ra, rb_, n: lambda e: e.tensor_tensor(out=ra[0:n, :], in0=ra[0:n, :], in1=rb_[0:n, :],
                                                                              op=ALU.add))(ra, rb_, n),
                          reads=[rab, rbb], writes=[rab])
                    dma1(SP, (k_s if sample else k_p)[l, r0:r0 + n, col0:col0 + 512], ra[0:n, :], rab, reads=[rab], is_out=True)
                    cp(ACT, bfv[0:n, :], ra[0:n, :], reads=[rab], writes=[bfb])
                def _tr(bfv=bfv, bfb=bfb, n=n, kind=kind, h0=h0, s=s, r0=r0):
                    for hh in range(4):
                        fw.op(PE, (lambda hh: lambda e: e.transpose(out=pT[:, hh * 128:hh * 128 + n], in_=bfv[0:n, hh * 128:(hh + 1) * 128],
                                                                    identity=ident[0:n, 0:n]))(hh),
                              reads=[bfb, B_const], writes=[B_pT], sig=(hh == 3))
                    src = pT[:, 0:512].rearrange("p (h k) -> p h k", h=4)[:, :, 0:n]
                    if kind == 0:
                        cp(ACT, QT[:, h0:h0 + 4, s * 128:s * 128 + n], src, reads=[B_pT], writes=[B_QT])
                    elif sample:
                        cp(DVE, KTn[:, h0:h0 + 4, 0:64], src, reads=[B_pT], writes=[B_KTn])
                    else:
                        cp(ACT, KT[:, h0:h0 + 4, r0:r0 + n], src, reads=[B_pT], writes=[B_KT])
                deferred["fn"] = _tr
            wrel(slot)
        flush_deferred()
        if _STOP == 3:
            wfree[:] = [0, 1]
            return
        ptc = {"i": 0}

        def next_pt():
            ptc["i"] += 1
            k = ptc["i"] % 4
            return arb(k // 2, 512, 512 * (k % 2)), PT[k]

        sublnc = subln_all[:, l:l + 1]
        if not sample:
            pending = None
            for k_ in range(4):
                PT[k_].w = AB[k_ // 2].w
                PT[k_].r = dict(AB[k_ // 2].r)
            for h in range(HEADS):
                for qbl in range(2):
                    qb = 2 * ti + qbl
                    jmax = 2 * qb + 1
                    qc0 = qbl * 256
                    npairs = qb + 1
                    pts = {}
                    for step in range(npairs + 1):
                        if step < npairs:
                            jp = step
                            pS, pSb = (pSX, B_pSX) if jp % 2 == 0 else (pSY, B_pSY)
                            for jl in range(2):
                                j = 2 * jp + jl
                                for m in range(2):
                                    fw.op(PE, (lambda pS, m, jl, j, h, qc0: lambda e: e.matmul(
                                        pS[:, m, jl * 256:(jl + 1) * 256], lhsT=KT[m * 64:(m + 1) * 64, h, j * 128:(j + 1) * 128],
                                        rhs=QT[m * 64:(m + 1) * 64, h, qc0:qc0 + 256], start=True, stop=True))(pS, m, jl, j, h, qc0),
                                          reads=[B_KT, B_QT], writes=[pSb], sig=(jl == 1 and m == 1))
                            for jl in range(2):
                                j = 2 * jp + jl
                                pt, ptb = next_pt()
                                pts[j] = (pt, ptb)
                                pt3 = pt.rearrange("p (m q) -> p m q", m=2)
                                fw.op(ACT, (lambda pt3, pS, jl: lambda e: e.activation(out=pt3, in_=pS[:, :, jl * 256:(jl + 1) * 256],
                                                                                      func=AF.Exp, scale=0.125))(pt3, pS, jl),
                                      reads=[pSb], writes=[ptb])
                                if jp == qb:
                                    if jl == 0:
                                        fw.op(POOL, (lambda pt3: lambda e: e.memset(pt3[64:128, :, 0:64], 0.0))(pt3), writes=[ptb])
                                    else:
                                        fw.op(POOL, (lambda pt3: lambda e: e.memset(pt3[0:64, :, 0:128], 0.0))(pt3), writes=[ptb])
                                        fw.op(POOL, (lambda pt3: lambda e: e.memset(pt3[64:128, :, 0:192], 0.0))(pt3), writes=[ptb])
                        if step >= 1:
                            for jl in range(2):
                                j = 2 * (step - 1) + jl
                                pt, ptb = pts.pop(j)
                                fw.op(PE, (lambda pt, j, h, jmax: lambda e: e.matmul(pO, lhsT=VX[:, j, h, 0:128], rhs=pt, start=(j == 0),
                                                                                    stop=(j == jmax)))(pt, j, h, jmax),
                                      reads=[B_VX, ptb], writes=[B_pO], sig=False)
                                fw.op(PE, (lambda pt, j, jmax: lambda e: e.matmul(pB, lhsT=ones_bf, rhs=pt, start=(j == 0),
                                                                                 stop=(j == jmax)))(pt, j, jmax),
                                      reads=[B_const, ptb], writes=[B_pB], sig=True)
                        if step == 1 and pending is not None:
                            pending()
                            pending = None
                    attn_epilogue1(256)

                    def _p2(h=h, qc0=qc0):
                        o, rstd = attn_epilogue2(l, 256)
                        fw.op(DVE, lambda e: e.scalar_tensor_tensor(out=ybT[:, h, qc0:qc0 + 256], in0=o, scalar=sublnc, in1=rstd,
                                                                    op0=ALU.mult, op1=ALU.mult),
                              reads=[AB[4], AB[5], B_lay], writes=[B_ybT])
                    pending = _p2
            if pending is not None:
                pending()
            for k_ in range(4):
                merge_ev(AB[k_ // 2], PT[k_].w)
                for ev_ in PT[k_].r.values():
                    merge_ev(AB[k_ // 2], ev_)
        else:
            for b in range(NSEQ_S):
                fw.op(PE, lambda e: e.matmul(pO[:, 0:256], lhsT=zeros_bf[:, 0:128], rhs=zeros_bf[:, 0:256], start=True, stop=True),
                      reads=[B_const], writes=[B_pO])
                fw.op(PE, lambda e: e.matmul(pB[:, 0:256], lhsT=zeros_bf[:, 0:128], rhs=zeros_bf[:, 0:256], start=True, stop=True),
                      reads=[B_const], writes=[B_pB])
                for grp in range(3):
                    pS, pSb = (pSX, B_pSX) if grp % 2 == 0 else (pSY, B_pSY)
                    njl = 4 if grp < 2 else 1
                    kp = 128 if grp < 2 else TS
                    for jl in range(njl):
                        jt = grp * 4 + jl
                        for h in range(HEADS):
                            for m in range(2):
                                if grp < 2:
                                    lt = KT[m * 64:(m + 1) * 64, h, b * PAST + jt * 128: b * PAST + (jt + 1) * 128]
                                else:
                                    lt = KTn[m * 64:(m + 1) * 64, h, b * TS:(b + 1) * TS]
                                fw.op(PE, (lambda pS, m, jl, h, lt, b, kp: lambda e: e.matmul(
                                    pS[0:kp, m, jl * 128 + h * TS: jl * 128 + (h + 1) * TS], lhsT=lt,
                                    rhs=QT[m * 64:(m + 1) * 64, h, b * TS:(b + 1) * TS], start=True, stop=True))(pS, m, jl, h, lt, b, kp),
                                      reads=[B_KT, B_KTn, B_QT], writes=[pSb], sig=(h == 7 and m == 1 and jl == njl - 1))
                    if grp < 2:
                        pt, ptb = arb(0, 1024), AB[0]
                        if grp == 1:
                            pt, ptb = arb(1, 1024), AB[1]
                        pt4 = pt.rearrange("p (m x) -> p m x", m=2)
                        fw.op(ACT, (lambda pt4, pS: lambda e: e.activation(out=pt4, in_=pS, func=AF.Exp, scale=0.125))(pt4, pS),
                              reads=[pSb], writes=[ptb])
                        for jl in range(4):
                            jt = grp * 4 + jl
                            for m in range(2):
                                fw.op(PE, (lambda pt4, m, jl: lambda e: e.matmul(pB[:, m * 128:(m + 1) * 128], lhsT=ones_bf,
                                                                                rhs=pt4[:, m, jl * 128:(jl + 1) * 128], start=False,
                                                                                stop=False, skip_group_check=True))(pt4, m, jl),
                                      reads=[ptb, B_const], writes=[B_pB], sig=False)
                                for h in range(HEADS):
                                    fw.op(PE, (lambda pt4, m, jl, h, jt, b: lambda e: e.matmul(
                                        pO[:, m * 128 + h * TS: m * 128 + (h + 1) * TS], lhsT=VX[:, b * 8 + jt, h, 0:128],
                                        rhs=pt4[:, m, jl * 128 + h * TS: jl * 128 + (h + 1) * TS], start=False, stop=False,
                                        skip_group_check=True))(pt4, m, jl, h, jt, b),
                                          reads=[ptb, B_VX], writes=[B_pO], sig=(m == 1 and h == 7))
                    else:
                        dma1(SP, VN[0:TS, :, :], vbf_s[TS * b:TS * b + TS, :].rearrange("p (h v) -> p h v", h=8), B_VN,
                             reads=[B_tmpA], writes=[B_VN])
                        fw.op(ACT, (lambda pS: lambda e: e.activation(out=PTn[0:TS].rearrange("p m h q -> p m (h q)"), in_=pS[0:TS, :, 0:128],
                                                                       func=AF.Exp, scale=0.125))(pS),
                              reads=[pSb], writes=[B_PTn])
                        for m in range(2):
                            fw.op(PE, (lambda m: lambda e: e.matmul(pB[:, m * 128:(m + 1) * 128], lhsT=ones_bf,
                                                                    rhs=PTn[:, m].rearrange("p h q -> p (h q)"), start=False, stop=False,
                                                                    skip_group_check=True))(m),
                                  reads=[B_PTn, B_const], writes=[B_pB], sig=False)
                            for h in range(HEADS):
                                fw.op(PE, (lambda m, h, b: lambda e: e.matmul(pO[:, m * 128 + h * TS: m * 128 + (h + 1) * TS], lhsT=VN[:, h, :],
                                                                              rhs=PTn[:, m, h, :], start=False, stop=True,
                                                                              skip_group_check=True))(m, h, b),
                                      reads=[B_PTn, B_VN], writes=[B_pO, B_pB], sig=(m == 1 and h == 7))
                attn_epilogue1(128)
                o, rstd = attn_epilogue2(l, 128)
                fw.op(DVE, (lambda b, o, rstd: lambda e: e.scalar_tensor_tensor(
                    out=ybT[:, :, b * TS:(b + 1) * TS], in0=o.rearrange("p (h q) -> p h q", h=8), scalar=sublnc,
                    in1=rstd.rearrange("p (h q) -> p h q", h=8), op0=ALU.mult, op1=ALU.mult))(b, o, rstd),
                      reads=[AB[4], AB[5], B_lay], writes=[B_ybT])

        if _STOP == 4:
            wfree[:] = [0, 1]
            return
        gt0, gt1 = arf(8, 512), arf(9, 512)
        for c in range(2):
            slot = wnext()
            Wc = WS[slot]
            for e_ in range(4):
                h = 4 * c + e_
                ps, psb = next_ps()
                mm_acc(ps[:, 0:ntok], psb, [(Wc[:, d, e_ * 128:(e_ + 1) * 128], xT[:, d, 0:ntok]) for d in range(8)], [B_WS[slot], B_xT])
                sig_gate(ps[:, 0:ntok], psb, ntok, gt0, AB[8], silu=True)
                fw.op(DVE, (lambda h: lambda e: e.tensor_tensor(out=ybT[:, h, 0:ntok], in0=ybT[:, h, 0:ntok], in1=gt0[:, 0:ntok],
                                                                 op=ALU.mult))(h), reads=[B_ybT, AB[8]], writes=[B_ybT])
            wrel(slot)

        if _STOP == 5:
            wfree[:] = [0, 1]
            return
        m1 = arena[:, 0:8 * 512].rearrange("p (c t) -> p c t", c=8)
        m1b = AB[0:8]
        slot_a = wnext()
        WA = WS[slot_a].rearrange("p k c -> p (k c)").rearrange("p (g c) -> p g c", g=4)
        for half in range(2):
            slot = wnext()
            Wc = WS[slot]
            for dcl in range(4):
                dc = 4 * half + dcl
                qa, qab, qb_, qbb = next_pair()
                mm_acc(qa[:, 0:ntok], qab, [(WA[:, g, dc * 128:(dc + 1) * 128], yaT[:, g, 0:ntok]) for g in range(4)], [B_WS[slot_a]] + yab)
                mm_acc(qb_[:, 0:ntok], qbb, [(Wc[:, d, dcl * 128:(dcl + 1) * 128], xT[:, d, 0:ntok]) for d in range(8)], [B_WS[slot], B_xT])
                sig_gate(qb_[:, 0:ntok], qbb, ntok, gt0, AB[8], silu=False)
                fw.op(DVE, (lambda dc, qa: lambda e: e.tensor_tensor(out=m1[:, dc, 0:ntok], in0=qa[:, 0:ntok], in1=gt0[:, 0:ntok],
                                                                      op=ALU.mult))(dc, qa), reads=[qab, AB[8]], writes=[m1b[dc]])
            if half == 0:
                wrel(slot)
        wrel(slot_a, slot)
        mergedT = QT
        for half in range(2):
            slot_b = wnext()
            WB = WS[slot_b]
            slot = wnext()
            Wc = WS[slot]
            for dcl in range(4):
                dc = 4 * half + dcl
                qa, qab, qb_, qbb = next_pair()
                mm_acc(qa[:, 0:ntok], qab, [(WB[:, h, dcl * 128:(dcl + 1) * 128], ybT[:, h, 0:ntok]) for h in range(8)], [B_WS[slot_b], B_ybT])
                mm_acc(qb_[:, 0:ntok], qbb, [(Wc[:, d, dcl * 128:(dcl + 1) * 128], xT[:, d, 0:ntok]) for d in range(8)], [B_WS[slot], B_xT])
                sig_gate(qb_[:, 0:ntok], qbb, ntok, gt0, AB[8], silu=False)
                fw.op(DVE, (lambda qa: lambda e: e.tensor_tensor(out=gt1[:, 0:ntok], in0=qa[:, 0:ntok], in1=gt0[:, 0:ntok], op=ALU.mult))(qa),
                      reads=[qab, AB[8]], writes=[AB[9]])
                fw.op(DVE, (lambda dc: lambda e: e.tensor_tensor(out=mergedT[:, dc, 0:ntok], in0=m1[:, dc, 0:ntok], in1=gt1[:, 0:ntok],
                                                                   op=ALU.add))(dc), reads=[m1b[dc], AB[9]], writes=[B_QT])
            wrel(slot_b, slot)

        if _STOP == 6:
            wfree[:] = [0, 1]
            return
        slot0 = wnext()
        slot1 = wnext()
        load_lnp(ln_g[l:l + 1, :], ln_b[l:l + 1, :])
        def ld_res(s, n):
            i2 = s % 2
            r0 = tok0 + s * 128
            dma1(SP, arf(2 * i2, 1024)[0:n, :], xsrc[r0:r0 + n, :], AB[2 * i2], reads=[xbuf(s)], writes=[AB[2 * i2], AB[2 * i2 + 1]])

        for s, n in subs[0:2]:
            ld_res(s, n)
        for s, n in subs:
            i2 = s % 2
            xres = arf(2 * i2, 1024)
            xrb = [AB[2 * i2], AB[2 * i2 + 1]]
            r_ap = arf(4 + 2 * i2, 1024)
            rbufs = [AB[4 + 2 * i2], AB[5 + 2 * i2]]
            r0 = tok0 + s * 128
            for hf, (pp_, ppb, sl_) in enumerate([(pA, B_pA, slot0), (pB, B_pB, slot1)]):
                mm_acc(pp_[0:n, :], ppb, [(mergedT[:, dc, s * 128:s * 128 + n], WS[sl_][:, dc, :]) for dc in range(8)], [B_QT, B_WS[sl_]])
                fw.op(DVE, (lambda hf, pp_, n, r_ap, xres: lambda e: e.scalar_tensor_tensor(
                    out=r_ap[0:n, hf * 512:(hf + 1) * 512], in0=xres[0:n, hf * 512:(hf + 1) * 512], scalar=ALPHA, in1=pp_[0:n, :],
                    op0=ALU.mult, op1=ALU.add))(hf, pp_, n, r_ap, xres), reads=xrb + [ppb], writes=rbufs)
            if s + 2 < nsub:
                ld_res(*subs[s + 2])
            elif (not sample) and ti + 1 < NT:
                s2 = s + 2 - nsub
                rn = (ti + 1) * TT + s2 * 128
                dma1(SP, arf(2 * s2, 1024), xcur_p[rn:rn + 128, :], AB[2 * s2], reads=[B_xcur_p[(ti + 1) * 4 + s2]],
                     writes=[AB[2 * s2], AB[2 * s2 + 1]])
                prefetched.add((l, ti + 1, s2))
            ln_rows(r_ap, rbufs, n, arf(8, 1024), arf(10, 1024), [AB[8], AB[9], AB[10], AB[11]])
            dma1(SP, ydst[r0:r0 + n, :], r_ap[0:n, :], AB[4 + 2 * i2], reads=rbufs, writes=[xbuf(s)], is_out=last_layer)
        assert nxt["k"] == NCH and not wq, (nxt, wq)
        wfree.extend([slot0, slot1])

    for l in range(L):
        fw.dma(POOL, [(lambda l: lambda e: e.dma_start(out=wpool_sb, in_=w_pool[l].rearrange("g c d -> c g d")))(l)], B_wpool, writes=[B_wpool])
        if l + 1 < L:
            cast_weights(l + 1)
        if with_sample:
            fw.new_epoch()
            tile_pass(l, 0, True)
        for ti in range(NT):
            if ti % 3 == 0:
                fw.new_epoch()
            tile_pass(l, ti, False)

    fw.emit()
    return nc, fw


def _rope_table(pos):
    half = 32
    inv = (np.float32(10000.0) ** (-np.arange(half, dtype=np.float32) / np.float32(half))).astype(np.float32)
    ang = pos.astype(np.float32)[:, None] * inv[None, :]
    c = np.cos(ang).astype(np.float32)
    s = np.sin(ang).astype(np.float32)
    return np.ascontiguousarray(np.concatenate([c, c, -s, s], axis=1).astype(np.float32))


def make_in_maps(inputs, L, SEQ, n_cores=8, with_sample=True):
    f = lambda a: np.ascontiguousarray(np.asarray(a, dtype=np.float32))
    cs_p = _rope_table(np.arange(SEQ))
    cs_s = np.ascontiguousarray(np.tile(_rope_table(PAST + np.arange(TS)), (NSEQ_S, 1)))
    invc = np.zeros((4, 16), np.float32)
    for g in range(4):
        w = 2 ** (g + 1)
        invc[g] = 1.0 / np.minimum(np.arange(16) + 1, w)
    common = dict(
        ln_in_g=f(inputs["ln_in_g"]).reshape(1, D), ln_in_b=f(inputs["ln_in_b"]).reshape(1, D),
        w_in=f(inputs["w_in"])[:L], w_pool=f(inputs["w_pool"])[:L], pool_scale=f(inputs["pool_scale"])[:L],
        lambda_qk=f(inputs["lambda_qk"])[:L].reshape(L, 256), subln_w=f(inputs["subln_w"])[:L],
        w_a=f(inputs["w_a"])[:L], w_b=f(inputs["w_b"])[:L], w_o=f(inputs["w_o"])[:L],
        ln_g=f(inputs["ln_g"])[:L], ln_b=f(inputs["ln_b"])[:L],
        cs_p=cs_p, cs_s=cs_s, idn_bf=np.eye(128).astype(ml_dtypes.bfloat16), idn_f=np.eye(128, dtype=np.float32),
        invc=invc.reshape(1, 64),
    )
    xp = f(inputs["x_prompt"])
    xs = f(inputs["x_sample"])
    ck = np.asarray(inputs["cache_k"], dtype=np.float32)
    cv = np.asarray(inputs["cache_v"], dtype=np.float32)
    sp = np.asarray(inputs["state_pool"], dtype=np.float32)
    maps = []
    for c in range(n_cores):
        m = dict(common)
        m["xp"] = np.ascontiguousarray(xp[c, :SEQ])
        m["xs"] = np.ascontiguousarray(xs[4 * c:4 * c + 4].reshape(64, D))
        m["ck"] = np.ascontiguousarray(ck[:L, 4 * c:4 * c + 4].reshape(L, 4, PAST, D))
        m["cv"] = np.ascontiguousarray(cv[:L, 4 * c:4 * c + 4].reshape(L, 4, PAST, D))
        m["spool"] = np.ascontiguousarray(sp[:L, 4 * c:4 * c + 4].reshape(L, 60, 512))
        maps.append(m)
    return maps


def gather(results, L, SEQ, n_cores=8):
    y_p = np.stack([r["y_p"] for r in results]).reshape(n_cores, SEQ, D)
    y_s = np.stack([r["y_s"] for r in results]).reshape(n_cores * 4, TS, D)
    k_p = np.stack([r["k_p"] for r in results], axis=1).reshape(L, n_cores, SEQ, HEADS, 128)
    v_p = np.stack([r["v_p"] for r in results], axis=1).reshape(L, n_cores, SEQ, HEADS, 128)
    pool_p = np.stack([r["pool_p"] for r in results], axis=1).reshape(L, n_cores, 15, 512)
    k_s = np.stack([r["k_s"] for r in results], axis=1).reshape(L, n_cores * 4, TS, HEADS, 128)
    v_s = np.stack([r["v_s"] for r in results], axis=1).reshape(L, n_cores * 4, TS, HEADS, 128)
    pool_s = np.stack([r["pool_s"] for r in results], axis=1).reshape(L, n_cores * 4, 15, 512)
    return tuple(np.ascontiguousarray(a.astype(np.float32)) for a in (y_p, y_s, k_p, v_p, pool_p, k_s, v_s, pool_s))


_CACHE = {}


def kernel(**inputs):
    L, SEQ = DEPTH_FULL, 4096
    if "nc" not in _CACHE:
        _CACHE["nc"] = build_program(L, SEQ)[0]
    nc = _CACHE["nc"]
    maps = make_in_maps(inputs, L, SEQ)
    res = run_bass_kernel_spmd(nc, maps, core_ids=list(range(8)))
    return gather(res.results, L, SEQ)
```
